# Optimizing a Trainium2 kernel written in Bass

```python
import math
import jax, jax.numpy as jnp
from jax import lax
import numpy as np

D_MODEL = 1024
BATCH = 4
SEQ = 8192
DEPTH = 2

MEM_LEN = 256
BLK = 128
EPS = 1e-6
A_HEADS = 8
A_HEAD_DIM = 64
A_PATTERNS = ((128, 1), (512, 4), (2048, 16))
B_CHANNELS = 512
B_CONV_WIDTH = 31
C_HEADS = 4
C_HEAD_DIM = 64
D_HEADS = 8
D_NOPE_DIM = 64
D_ROPE_DIM = 32
D_V_DIM = 64
D_Q_RANK = 384
D_KV_RANK = 256
ROPE_THETA = 10000.0
X_HEADS = 4
X_HEAD_DIM = 128
D_FF = 4 * D_MODEL

A_WIDTH = A_HEADS * A_HEAD_DIM
AB_IN = 3 * A_WIDTH + 2 * B_CHANNELS
AB_OUT = A_WIDTH + B_CHANNELS
C_WIDTH = C_HEADS * 2 * C_HEAD_DIM
CD_IN = 3 * C_WIDTH + D_Q_RANK + D_KV_RANK + D_ROPE_DIM
CD_OUT = C_WIDTH + D_HEADS * D_V_DIM
N_EVEN = (DEPTH + 1) // 2
N_ODD = DEPTH // 2

kernel_name = 'hybrid_dilated_conformer_diff_mla'


def rms_norm(x, g):
    x32 = x.astype(jnp.float32)
    y = x32 * lax.rsqrt(jnp.mean(x32 * x32, axis=-1, keepdims=True) + EPS)
    return (y * g.astype(jnp.float32)).astype(x.dtype)


def layer_norm(x, g, b):
    x32 = x.astype(jnp.float32)
    mu = jnp.mean(x32, axis=-1, keepdims=True)
    var = jnp.mean(jnp.square(x32 - mu), axis=-1, keepdims=True)
    y = (x32 - mu) * lax.rsqrt(var + EPS)
    return (y * g.astype(jnp.float32) + b.astype(jnp.float32)).astype(x.dtype)


def rope(x, positions):
    half = x.shape[-1] // 2
    inv_freq = ROPE_THETA ** (-jnp.arange(half, dtype=jnp.float32) / half)
    ang = positions.astype(jnp.float32)[..., None] * inv_freq
    cos, sin = jnp.cos(ang)[:, :, None, :], jnp.sin(ang)[:, :, None, :]
    x1, x2 = x[..., :half].astype(jnp.float32), x[..., half:].astype(jnp.float32)
    return jnp.concatenate([x1 * cos - x2 * sin, x2 * cos + x1 * sin], axis=-1).astype(x.dtype)


def dilated_window_attention(q, k, v, dilation, steps):
    B, S, H, Dh = q.shape
    assert steps <= BLK
    span = dilation * BLK
    s_pad = -(-S // span) * span
    n_blk = s_pad // span

    def to_sub(t):
        t = jnp.pad(t, ((0, 0), (0, s_pad - S), (0, 0), (0, 0)))
        return t.reshape(B, n_blk, BLK, dilation, H, Dh).transpose(0, 4, 3, 1, 2, 5)

    def with_prev(t):
        prev = jnp.pad(t, ((0, 0), (0, 0), (0, 0), (1, 0), (0, 0), (0, 0)))[:, :, :, :-1]
        return jnp.concatenate([prev, t], axis=4)

    qs = to_sub(q)
    kb, vb = with_prev(to_sub(k)), with_prev(to_sub(v))
    s = jnp.einsum('bhrnqd,bhrnkd->bhrnqk', qs, kb).astype(jnp.float32) * (Dh ** -0.5)
    qi = jnp.arange(BLK)[:, None]
    kj = jnp.arange(2 * BLK)[None, :]
    dist = qi + BLK - kj
    blk = jnp.arange(n_blk)[:, None, None]
    valid = (dist >= 0) & (dist <= steps) & (blk * BLK + kj >= BLK)
    s = jnp.where(valid, s, -jnp.inf)
    lse = jax.nn.logsumexp(s, axis=-1)
    p = jnp.exp(s - lse[..., None]).astype(v.dtype)
    o = jnp.einsum('bhrnqk,bhrnkd->bhrnqd', p, vb)
    o = o.transpose(0, 3, 4, 2, 1, 5).reshape(B, s_pad, H, Dh)[:, :S]
    lse = lse.transpose(0, 3, 4, 2, 1).reshape(B, s_pad, H)[:, :S]
    return o, lse


def dilated_mixture_attention(q, k, v):
    outs, lses = [], []
    for window, dilation in A_PATTERNS:
        o, l = dilated_window_attention(q, k, v, dilation, window // dilation)
        outs.append(o)
        lses.append(l)
    w = jax.nn.softmax(jnp.stack(lses), axis=0)
    return jnp.einsum('gbsh,gbshd->bshd', w, jnp.stack(outs).astype(jnp.float32)).astype(q.dtype)


def causal_attention(q, k, v, scale):
    B, H, S, Dk = q.shape
    Dv = v.shape[-1]
    nb = S // BLK
    qb = q.reshape(B, H, nb, BLK, Dk).transpose(2, 0, 1, 3, 4)
    kpos = jnp.arange(S)

    def one_block(args):
        qblk, idx = args
        s = jnp.einsum('bhqd,bhkd->bhqk', qblk, k).astype(jnp.float32) * scale
        qpos = idx * BLK + jnp.arange(BLK)
        s = jnp.where(kpos[None, :] <= qpos[:, None], s, -jnp.inf)
        p = jax.nn.softmax(s, axis=-1).astype(v.dtype)
        return jnp.einsum('bhqk,bhkd->bhqd', p, v)

    out = lax.map(one_block, (qb, jnp.arange(nb)))
    return out.transpose(1, 2, 0, 3, 4).reshape(B, H, S, Dv)


def mixer_ab(h, w_in, w_out, conv_w, conv_b, ln_g, ln_b):
    B, S, _ = h.shape
    z = h @ w_in
    qa, ka, va, u, g = jnp.split(
        z, [A_WIDTH, 2 * A_WIDTH, 3 * A_WIDTH, 3 * A_WIDTH + B_CHANNELS], axis=-1)
    heads = lambda t: t.reshape(B, S, A_HEADS, A_HEAD_DIM)
    ya = dilated_mixture_attention(heads(qa), heads(ka), heads(va)).reshape(B, S, A_WIDTH)
    glu = u * jax.nn.sigmoid(g)
    conv = lax.conv_general_dilated(
        glu, conv_w, window_strides=(1,), padding=[(B_CONV_WIDTH - 1, 0)],
        dimension_numbers=('NWC', 'WIO', 'NWC'), feature_group_count=B_CHANNELS) + conv_b
    yb = jax.nn.silu(layer_norm(conv, ln_g, ln_b))
    return jnp.concatenate([ya, yb], axis=-1) @ w_out


def mixer_cd(h, positions, layer_idx, w_in, w_out, lq1, lk1, lq2, lk2, subln_g,
             q_norm_g, kv_norm_g, w_uq, w_uk, w_uv):
    B, S, _ = h.shape
    z = h @ w_in
    o1 = C_WIDTH
    o2 = 2 * C_WIDTH
    o3 = 3 * C_WIDTH
    o4 = o3 + D_Q_RANK
    o5 = o4 + D_KV_RANK
    qc, kc, vc, cq, ckv, kr = jnp.split(z, [o1, o2, o3, o4, o5], axis=-1)

    def two_maps(t):
        t = t.reshape(B, S, C_HEADS, 2, C_HEAD_DIM)
        return t.transpose(0, 3, 2, 1, 4).reshape(B, 2 * C_HEADS, S, C_HEAD_DIM)
    vch = vc.reshape(B, S, C_HEADS, 2 * C_HEAD_DIM).transpose(0, 2, 1, 3)
    a = causal_attention(two_maps(qc), two_maps(kc), jnp.concatenate([vch, vch], axis=1),
                         C_HEAD_DIM ** -0.5)
    lam_init = 0.8 - 0.6 * math.exp(-0.3 * layer_idx)
    lam = jnp.exp(jnp.sum(lq1 * lk1)) - jnp.exp(jnp.sum(lq2 * lk2)) + lam_init
    yc = rms_norm(a[:, :C_HEADS] - lam * a[:, C_HEADS:], subln_g) * (1.0 - lam_init)
    yc = yc.transpose(0, 2, 1, 3).reshape(B, S, C_WIDTH)

    q = (rms_norm(cq, q_norm_g) @ w_uq).reshape(B, S, D_HEADS, D_NOPE_DIM + D_ROPE_DIM)
    qd = jnp.concatenate([q[..., :D_NOPE_DIM], rope(q[..., D_NOPE_DIM:], positions)], axis=-1)
    ckv = rms_norm(ckv, kv_norm_g)
    k_nope = (ckv @ w_uk).reshape(B, S, D_HEADS, D_NOPE_DIM)
    vd = (ckv @ w_uv).reshape(B, S, D_HEADS, D_V_DIM)
    k_rope = rope(kr[:, :, None, :], positions)
    kd = jnp.concatenate(
        [k_nope, jnp.broadcast_to(k_rope, (B, S, D_HEADS, D_ROPE_DIM))], axis=-1)
    yd = causal_attention(qd.transpose(0, 2, 1, 3), kd.transpose(0, 2, 1, 3),
                          vd.transpose(0, 2, 1, 3), (D_NOPE_DIM + D_ROPE_DIM) ** -0.5)
    yd = yd.transpose(0, 2, 1, 3).reshape(B, S, D_HEADS * D_V_DIM)
    return jnp.concatenate([yc, yd], axis=-1) @ w_out


def memory_cross_attention(h, mem, mem_norm_g, wq, wkv, wo):
    B, S, _ = h.shape
    M = mem.shape[1]
    q = (h @ wq).reshape(B, S, X_HEADS, X_HEAD_DIM)
    k, v = jnp.split(rms_norm(mem, mem_norm_g) @ wkv, 2, axis=-1)
    k = k.reshape(B, M, X_HEADS, X_HEAD_DIM)
    v = v.reshape(B, M, X_HEADS, X_HEAD_DIM)
    s = jnp.einsum('bshd,bmhd->bhsm', q, k).astype(jnp.float32) * (X_HEAD_DIM ** -0.5)
    p = jax.nn.softmax(s, axis=-1).astype(v.dtype)
    o = jnp.einsum('bhsm,bmhd->bshd', p, v).reshape(B, S, X_HEADS * X_HEAD_DIM)
    return o @ wo


def squared_relu_mlp(h, w1, w2):
    return jnp.square(jax.nn.relu(h @ w1)) @ w2


def setup_inputs(seed: int = 0) -> dict:
    key = jax.random.key(seed)
    keys = iter(jax.random.split(key, 64))
    nrm = lambda shape, scale: jax.random.normal(next(keys), shape, jnp.float32) * scale
    gain = lambda shape: 1.0 + nrm(shape, 0.02)
    L, E, O, D = DEPTH, N_EVEN, N_ODD, D_MODEL
    offset = jax.random.randint(next(keys), (BATCH, 1), 0, 4096, dtype=jnp.int32)
    return {
        'x': nrm((BATCH, SEQ, D), 1.0),
        'mem': nrm((BATCH, MEM_LEN, D), 1.0),
        'positions': offset + jnp.arange(SEQ, dtype=jnp.int32)[None, :],
        'norm_mix_g': gain((L, D)),
        'norm_cross_g': gain((L, D)),
        'norm_mem_g': gain((L, D)),
        'cross_wq': nrm((L, D, X_HEADS * X_HEAD_DIM), D ** -0.5),
        'cross_wkv': nrm((L, D, 2 * X_HEADS * X_HEAD_DIM), D ** -0.5),
        'cross_wo': nrm((L, X_HEADS * X_HEAD_DIM, D), (X_HEADS * X_HEAD_DIM) ** -0.5),
        'norm_mlp_g': gain((L, D)),
        'mlp_w1': nrm((L, D, D_FF), D ** -0.5),
        'mlp_w2': nrm((L, D_FF, D), D_FF ** -0.5),
        'ab_w_in': nrm((E, D, AB_IN), D ** -0.5),
        'ab_w_out': nrm((E, AB_OUT, D), AB_OUT ** -0.5),
        'ab_conv_w': nrm((E, B_CONV_WIDTH, 1, B_CHANNELS), B_CONV_WIDTH ** -0.5),
        'ab_conv_b': nrm((E, B_CHANNELS), 0.02),
        'ab_ln_g': gain((E, B_CHANNELS)),
        'ab_ln_b': nrm((E, B_CHANNELS), 0.02),
        'cd_w_in': nrm((O, D, CD_IN), D ** -0.5),
        'cd_w_out': nrm((O, CD_OUT, D), CD_OUT ** -0.5),
        'diff_lq1': nrm((O, C_HEAD_DIM), 0.1),
        'diff_lk1': nrm((O, C_HEAD_DIM), 0.1),
        'diff_lq2': nrm((O, C_HEAD_DIM), 0.1),
        'diff_lk2': nrm((O, C_HEAD_DIM), 0.1),
        'diff_subln_g': gain((O, 2 * C_HEAD_DIM)),
        'mla_q_norm_g': gain((O, D_Q_RANK)),
        'mla_kv_norm_g': gain((O, D_KV_RANK)),
        'mla_w_uq': nrm((O, D_Q_RANK, D_HEADS * (D_NOPE_DIM + D_ROPE_DIM)), D_Q_RANK ** -0.5),
        'mla_w_uk': nrm((O, D_KV_RANK, D_HEADS * D_NOPE_DIM), D_KV_RANK ** -0.5),
        'mla_w_uv': nrm((O, D_KV_RANK, D_HEADS * D_V_DIM), D_KV_RANK ** -0.5),
        'final_norm_g': gain((D,)),
    }


def reference(x, mem, positions, norm_mix_g, norm_cross_g, norm_mem_g, cross_wq, cross_wkv,
              cross_wo, norm_mlp_g, mlp_w1, mlp_w2, ab_w_in, ab_w_out, ab_conv_w, ab_conv_b,
              ab_ln_g, ab_ln_b, cd_w_in, cd_w_out, diff_lq1, diff_lk1, diff_lq2, diff_lk2,
              diff_subln_g, mla_q_norm_g, mla_kv_norm_g, mla_w_uq, mla_w_uk, mla_w_uv,
              final_norm_g):
    for i in range(DEPTH):
        j = i // 2
        h = rms_norm(x, norm_mix_g[i])
        if i % 2 == 0:
            x = x + mixer_ab(h, ab_w_in[j], ab_w_out[j], ab_conv_w[j], ab_conv_b[j],
                             ab_ln_g[j], ab_ln_b[j])
        else:
            x = x + mixer_cd(h, positions, i, cd_w_in[j], cd_w_out[j], diff_lq1[j], diff_lk1[j],
                             diff_lq2[j], diff_lk2[j], diff_subln_g[j], mla_q_norm_g[j],
                             mla_kv_norm_g[j], mla_w_uq[j], mla_w_uk[j], mla_w_uv[j])
        x = x + memory_cross_attention(rms_norm(x, norm_cross_g[i]), mem, norm_mem_g[i],
                                       cross_wq[i], cross_wkv[i], cross_wo[i])
        x = x + squared_relu_mlp(rms_norm(x, norm_mlp_g[i]), mlp_w1[i], mlp_w2[i])
    return rms_norm(x, final_norm_g)
```

```python
import numpy as np
from contextlib import ExitStack
import concourse.bass as bass
import concourse.mybir as mybir
from concourse.bass_utils import run_bass_kernel_spmd

F32 = mybir.dt.float32
BF16 = mybir.dt.bfloat16
I32 = mybir.dt.int32
AF = mybir.ActivationFunctionType
ALU = mybir.AluOpType

D = 1024
SEQ = 8192
HALF = 4096
NT = 64
EPS = 1e-6
SEM_CH = 30000


class Ev:
    __slots__ = ("sem", "val", "eng")

    def __init__(self, sem=None, val=None, eng=None):
        self.sem, self.val, self.eng = sem, val, eng


class Buf:
    __slots__ = ("name", "w", "r", "dsem", "dcnt", "k")

    def __init__(self, k, name):
        self.k, self.name = k, name
        self.w = None
        self.r = []
        self.dsem = None
        self.dcnt = 0


class Eng:
    def __init__(self, k, name, eng, compute=True):
        self.k, self.name, self.eng = k, name, eng
        self.cnt = 0
        self.sems = []
        self.seen = {}
        self.last = None
        self.last_wkey = None
        self.pending = []

    def flush(self):
        if self.last is None:
            return
        ci = self.cnt // SEM_CH
        while len(self.sems) <= ci:
            self.sems.append(self.k.new_sem("t_%s%d" % (self.name, len(self.sems))))
        sem = self.sems[ci]
        val = self.cnt % SEM_CH + 1
        self.cnt += 1
        self.last.then_inc(sem, 1)
        for ev in self.pending:
            ev.sem, ev.val = sem, val
        self.last = None
        self.pending = []

    def wait(self, ev):
        if ev is None:
            return
        if ev.sem is None:
            ev.eng.flush()
        if self.seen.get(ev.sem, 0) >= ev.val:
            return
        self.eng.wait_ge(ev.sem, ev.val)
        self.seen[ev.sem] = ev.val


class K:
    def __init__(self, nc):
        self.nc = nc
        self.es = ExitStack()
        self.E = {
            "pe": Eng(self, "pe", nc.tensor),
            "act": Eng(self, "act", nc.scalar),
            "dve": Eng(self, "dve", nc.vector),
            "pool": Eng(self, "pool", nc.gpsimd),
            "sp": Eng(self, "sp", nc.sync),
        }
        self.nsem = 0
        self.dsem_pool = []
        self.bufs = []
        self.bar_sem = self.new_sem("bar")
        self.bar_cnt = 0
        self.uid = 0

    def new_sem(self, name):
        self.nsem += 1
        return self.es.enter_context(self.nc.semaphore("%s_%d" % (name, self.nsem)))

    def buf(self, name="b"):
        b = Buf(self, name)
        self.bufs.append(b)
        return b

    def bufs_n(self, n, name="b"):
        return [self.buf("%s%d" % (name, i)) for i in range(n)]

    def _dsem(self, b):
        if b.dsem is None:
            if self.dsem_pool:
                b.dsem, b.dcnt = self.dsem_pool.pop()
            else:
                b.dsem, b.dcnt = self.new_sem("d"), 0
        return b.dsem

    def release(self, blist):
        for b in blist:
            if b.dsem is not None:
                self.dsem_pool.append((b.dsem, b.dcnt))
                b.dsem = None
            if b in self.bufs:
                self.bufs.remove(b)

    def sb(self, st, name, shape, dtype):
        self.uid += 1
        return st.enter_context(self.nc.sbuf_tensor("%s_%d" % (name, self.uid), list(shape), dtype))

    def ps(self, st, name, shape, dtype=F32):
        self.uid += 1
        return st.enter_context(self.nc.psum_tensor("%s_%d" % (name, self.uid), list(shape), dtype))

    def op(self, en, fn, reads=(), writes=(), late=None):
        e = self.E[en]
        same_ok = (en == "pe")
        late_ev = None
        for b in reads:
            if b.w is not None and not (same_ok and b.w.eng is e):
                if b is late:
                    ev = b.w
                    if ev.sem is None:
                        ev.eng.flush()
                    if e.seen.get(ev.sem, 0) < ev.val:
                        late_ev = ev
                    continue
                e.wait(b.w)
        for b in writes:
            if b.w is not None and not (same_ok and b.w.eng is e):
                e.wait(b.w)
            for ev in b.r:
                if not (same_ok and ev.eng is e):
                    e.wait(ev)
        wkey = tuple(id(b) for b in writes)
        if e.last is not None and e.last_wkey != wkey:
            e.flush()
        ins = fn()
        if late_ev is not None:
            ins._wait_ge(late_ev.sem, late_ev.val)
            e.seen[late_ev.sem] = late_ev.val
        ev = Ev(eng=e)
        e.pending.append(ev)
        e.last = ins
        e.last_wkey = wkey
        for b in writes:
            b.w = ev
            b.r = []
        for b in reads:
            if b not in writes:
                b.r = [x for x in b.r if x.eng is not e] + [ev]
        return ins

    def dma(self, qn, sbuf_buf, write_sbuf, pairs, extra_reads=(), extra_writes=()):
        q = self.E[qn]
        b = sbuf_buf
        q.wait(b.w)
        if write_sbuf:
            for ev in b.r:
                q.wait(ev)
        for x in extra_reads:
            q.wait(x.w)
        for x in extra_writes:
            q.wait(x.w)
            for ev in x.r:
                q.wait(ev)
        sem = self._dsem(b)
        for (o, i) in pairs:
            b.dcnt += 1
            q.eng.dma_start(out=o, in_=i).then_inc(sem, 16)
        assert b.dcnt * 16 < 60000, "dma sem overflow %s" % b.name
        ev = Ev(sem, 16 * b.dcnt, None)
        if write_sbuf:
            b.w = ev
            b.r = []
        else:
            b.r = [x for x in b.r if x.sem is not sem] + [ev]
        for x in extra_writes:
            x.w = ev
            x.r = []
        for x in extra_reads:
            x.r = x.r + [ev]

    def barrier(self):
        sp = self.E["sp"]
        for e in self.E.values():
            e.flush()
        for e in self.E.values():
            if e is sp:
                continue
            for i, s in enumerate(e.sems):
                last = e.cnt - i * SEM_CH
                v = min(last, SEM_CH)
                if v > 0 and sp.seen.get(s, 0) < v:
                    sp.eng.wait_ge(s, v)
                    sp.seen[s] = v
        for b in self.bufs:
            if b.dsem is not None and b.dcnt > 0 and sp.seen.get(b.dsem, 0) < 16 * b.dcnt:
                sp.eng.wait_ge(b.dsem, 16 * b.dcnt)
                sp.seen[b.dsem] = 16 * b.dcnt
        self.bar_cnt += 1
        sp.eng.sem_inc(self.bar_sem, 1)
        for e in self.E.values():
            if e is sp:
                continue
            e.eng.wait_ge(self.bar_sem, self.bar_cnt)
            e.seen = dict(sp.seen)
        for b in self.bufs:
            b.w = None
            b.r = []


class Ctx:
    pass


def dram_in(nc, name, shape, dt=F32):
    return nc.dram_tensor(name, list(shape), dt, kind="ExternalInput").ap()


def build_consts(k, C, st):
    nc = k.nc
    C.b_const = k.buf("const")
    one_f = k.sb(st, "one_f", [128, 128], F32)
    C.ident_f = k.sb(st, "ident_f", [128, 128], F32)
    C.ident_b = k.sb(st, "ident_b", [128, 128], BF16)
    C.mask_le = k.sb(st, "mask_le", [128, 128], BF16)
    C.mask_ge = k.sb(st, "mask_ge", [128, 128], BF16)
    C.ones_b = k.sb(st, "ones_b", [128, 128], BF16)
    C.ones_f = k.sb(st, "ones_f", [128, 128], F32)
    b = C.b_const
    k.op("pool", lambda: nc.gpsimd.memset(one_f[:], 1.0), writes=[b])
    k.op("pool", lambda: nc.gpsimd.memset(C.ones_b[:], 1.0), writes=[b])
    k.op("pool", lambda: nc.gpsimd.memset(C.ones_f[:], 1.0), writes=[b])
    k.op("pool", lambda: nc.gpsimd.affine_select(out=C.ident_f[:], in_=one_f[:], pattern=[[1, 128]],
                                                 compare_op=ALU.is_equal, fill=0.0, base=0,
                                                 channel_multiplier=-1), reads=[b], writes=[b])
    k.op("pool", lambda: nc.gpsimd.tensor_copy(out=C.ident_b[:], in_=C.ident_f[:]), reads=[b], writes=[b])
    k.op("pool", lambda: nc.gpsimd.affine_select(out=C.mask_le[:], in_=one_f[:], pattern=[[1, 128]],
                                                 compare_op=ALU.is_ge, fill=0.0, base=0,
                                                 channel_multiplier=-1), reads=[b], writes=[b])
    k.op("pool", lambda: nc.gpsimd.affine_select(out=C.mask_ge[:], in_=one_f[:], pattern=[[-1, 128]],
                                                 compare_op=ALU.is_ge, fill=0.0, base=0,
                                                 channel_multiplier=1), reads=[b], writes=[b])


def load_weight(k, st, dst, w_dram, kc, n, scale_cols=None, neg_cols=None):
    nc = k.nc
    CH = 2048
    stg = [k.sb(st, "wstg", [128, CH], F32) for _ in range(4)]
    sb_ = k.bufs_n(4, "wstg")
    wb = k.buf("wdst")
    i = 0
    for c in range(kc):
        for n0 in range(0, n, CH):
            n1 = min(n, n0 + CH)
            s, sb = stg[i % 4], sb_[i % 4]
            k.dma("sp", sb, True, [(s[:, 0:n1 - n0], w_dram[c * 128:(c + 1) * 128, n0:n1])])
            if scale_cols is not None:
                en = ("pool", "dve", "act", "dve")[i % 4]
                if en == "act":
                    k.op("act", lambda s=s, c=c, n0=n0, n1=n1: nc.scalar.activation(
                        out=dst[:, c, n0:n1], in_=s[:, 0:n1 - n0], func=AF.Copy, scale=scale_cols[:, c:c + 1]),
                        reads=[sb], writes=[wb])
                else:
                    e_ = nc.gpsimd if en == "pool" else nc.vector
                    k.op(en, lambda s=s, c=c, n0=n0, n1=n1, e_=e_: e_.tensor_scalar(
                        out=dst[:, c, n0:n1], in0=s[:, 0:n1 - n0], scalar1=scale_cols[:, c:c + 1], scalar2=None,
                        op0=ALU.mult), reads=[sb], writes=[wb])
            else:
                k.op("pool", lambda s=s, c=c, n0=n0, n1=n1: nc.gpsimd.tensor_copy(
                    out=dst[:, c, n0:n1], in_=s[:, 0:n1 - n0]), reads=[sb], writes=[wb])
            i += 1
    return wb, sb_


class BG:
    def __init__(self, k, C, st):
        self.k, self.C = k, C
        self.wbuf = {}

    def add(self, name, kc, n, scale_cols=None):
        k, C = self.k, self.C
        b = k.buf("W_" + name)
        self.wbuf[name] = b
        src, dst = getattr(C, name), getattr(C, "W_" + name)
        pairs = []
        for n0 in range(0, n, 2048):
            n1 = min(n, n0 + 2048)
            for r0 in range(0, kc * 128, 1024):
                r1 = min(kc * 128, r0 + 1024)
                pairs.append((dst[r0:r1, n0:n1], src[r0:r1, n0:n1]))
        k.dma("pool", b, True, pairs)

    def step(self, n=1):
        pass

    def need(self, *names):
        pass


def load_weight_bf(k, C, dst, name, wb=None):
    C.bg.need(name)
    if wb is None:
        wb = k.buf("wdst")
    src = getattr(C, "W_" + name).rearrange("(c p) n -> p c n", p=128)
    k.dma("sp", wb, True, [(dst, src)], extra_reads=[C.bg.wbuf[name]])
    return wb


def load_gain(k, C, st, row):
    g = k.sb(st, "gbc", [128, D], F32)
    gb = k.buf("gbc")
    k.dma("sp", gb, True, [(g[:, :], C.gvec[row:row + 1, :].to_broadcast([128, D]))])
    return g, gb


def norm_rows(k, C, xt, xb, ntile, ssq, ssqb, rstd, junk, junkb, hb, hbb, dmodel=D, gain=None):
    nc = k.nc
    for j in range(ntile):
        k.op("act", lambda j=j: nc.scalar.activation(out=junk[:, :], in_=xt[:, j, :], func=AF.Square,
                                                     accum_out=ssq[:, j:j + 1]),
             reads=[xb], writes=[junkb, ssqb])
    k.op("act", lambda: nc.scalar.activation(out=rstd[:, 0:ntile], in_=ssq[:, 0:ntile], func=AF.Sqrt,
                                             bias=C.eps_col[:, 0:1], scale=1.0 / dmodel),
         reads=[ssqb, C.b_const], writes=[ssqb])
    k.op("dve", lambda: nc.vector.reciprocal(out=rstd[:, 0:ntile], in_=rstd[:, 0:ntile]),
         reads=[ssqb], writes=[ssqb])
    g, gb = gain
    for j in range(ntile):
        k.op("dve", lambda j=j: nc.vector.scalar_tensor_tensor(out=hb[:, j, :], in0=xt[:, j, :], scalar=rstd[:, j:j + 1],
                                                               in1=g[:, :], op0=ALU.mult, op1=ALU.mult),
             reads=[xb, ssqb, gb], writes=[hbb])


def transpose_to(k, C, src, srcb, ntile, nk, tps, tpsb, dstT, dstTb, col0=0, evac="act"):
    nc = k.nc
    for j in range(ntile):
        tp, tb = tps[j % len(tps)], tpsb[j % len(tps)]
        for c in range(nk):
            k.op("pe", lambda j=j, c=c, tp=tp: nc.tensor.transpose(
                out=tp[:, c, :], in_=src[:, j, c * 128:(c + 1) * 128], identity=C.ident_b[:]),
                reads=[srcb, C.b_const], writes=[tb])
        dst = dstT[:, 0:nk, col0 + j * 128:col0 + (j + 1) * 128]
        if evac == "act":
            k.op("act", lambda tp=tp, dst=dst: nc.scalar.copy(out=dst, in_=tp[:, 0:nk, :]),
                 reads=[tb], writes=[dstTb])
        else:
            k.op("dve", lambda tp=tp, dst=dst: nc.vector.tensor_copy(out=dst, in_=tp[:, 0:nk, :]),
                 reads=[tb], writes=[dstTb])


class PreStage:
    def __init__(self, k, C, st, G, xview, gain, tps, tpsb, t_lo, ngrp):
        self.k, self.C, self.G, self.xview, self.gain = k, C, G, xview, gain
        self.tps, self.tpsb, self.t_lo, self.ngrp = tps, tpsb, t_lo, ngrp
        self.xg = [k.sb(st, "xg", [128, G, D], F32) for _ in range(3)]
        self.xgb = k.bufs_n(3, "xg")
        self.hb = [k.sb(st, "hb", [128, G, D], BF16) for _ in range(2)]
        self.hbb = k.bufs_n(2, "hb")
        self.junk = k.sb(st, "junk", [128, D], BF16)
        self.junkb = k.buf("junk")
        self.ssq = [k.sb(st, "ssq", [128, 8], F32) for _ in range(2)]
        self.rstd = [k.sb(st, "rstd", [128, 8], F32) for _ in range(2)]
        self.ssqb = k.bufs_n(2, "ssq")
        self.hT = [k.sb(st, "hT", [128, 8, G * 128], BF16) for _ in range(2)]
        self.hTb = k.bufs_n(2, "hT")

    def bufs(self):
        return self.xgb + self.hbb + [self.junkb] + self.ssqb + self.hTb

    def load(self, gi):
        if gi >= self.ngrp:
            return
        G = self.G
        t0 = self.t_lo + gi * G
        self.k.dma("sp", self.xgb[gi % 3], True, [(self.xg[gi % 3][:, :, :], self.xview[:, t0:t0 + G, :])])

    def start(self):
        self.load(0)
        self.load(1)
        self.front(0, prefetch=False)
        self.back(0)

    def front(self, gi, prefetch=True):
        if prefetch:
            self.load(gi + 1)
        if gi >= self.ngrp:
            return
        k, i, G = self.k, gi % 2, self.G
        norm_rows(k, self.C, self.xg[gi % 3], self.xgb[gi % 3], G, self.ssq[i], self.ssqb[i], self.rstd[i], self.junk, self.junkb,
                  self.hb[i], self.hbb[i], gain=self.gain)

    def back(self, gi):
        if gi >= self.ngrp:
            return
        i = gi % 2
        transpose_to(self.k, self.C, self.hb[i], self.hbb[i], self.G, 8, self.tps, self.tpsb, self.hT[i], self.hTb[i])

    def get(self, gi):
        i = gi % 2
        return self.xg[gi % 3], self.xgb[gi % 3], self.hT[i], self.hTb[i]


def phase_a0(k, C, t_lo, t_hi):
    nc = k.nc
    with ExitStack() as st:
        w = k.sb(st, "w_in0", [128, 8, 2560], BF16)
        wb = k.buf("w_in0")
        k.dma("pool", wb, True, [(w[:, c, n0:n1], C.ab_w_in[c * 128:(c + 1) * 128, n0:n1])
                                 for c in range(8) for (n0, n1) in ((0, 2048), (2048, 2560))])
        gain = load_gain(k, C, st, 0)
        C.bg_register()
        C.build_diag()
        G = 4
        tps = [k.ps(st, "tp", [128, 8, 128], BF16) for _ in range(2)]
        tpsb = k.bufs_n(2, "tp")
        pm = [k.ps(st, "pm", [128, 512], F32) for _ in range(4)]
        pmb = k.bufs_n(4, "pm")
        stq = [k.sb(st, "stq", [128, 4, 512], BF16) for _ in range(2)]
        stqb = k.bufs_n(2, "stq")
        sig = [k.sb(st, "sig", [128, 512], F32) for _ in range(2)]
        sigb = k.bufs_n(2, "sig")
        glu = k.sb(st, "glu", [128, 4, 512], BF16)
        glub = k.buf("glu")
        va = [k.sb(st, "va", [128, 8, 65], BF16) for _ in range(2)]
        vab = k.bufs_n(2, "va")
        ngrp = (t_hi - t_lo) // G
        xv = C.XA.rearrange("(t p) d -> p t d", p=128)
        v0v = C.V0.rearrange("(t p) c -> p t c", p=128)
        pre = PreStage(k, C, st, G, xv, gain, tps, tpsb, t_lo, ngrp)
        pmi = 0
        stqi = 0
        pre.start()
        for gi in range(ngrp):
            t0 = t_lo + gi * G
            pre.front(gi + 1)
            x_, xb_, hT_, hTb_ = pre.get(gi)
            tok0 = t0 * 128
            for which, (col0, dst, scale) in enumerate(((0, C.QT0, 0.125), (512, C.KT0, 1.0))):
                s_, sb_ = stq[stqi % 2], stqb[stqi % 2]
                stqi += 1
                for c in range(4):
                    p_, pb_ = pm[pmi % 4], pmb[pmi % 4]
                    pmi += 1
                    for kk in range(8):
                        k.op("pe", lambda p_=p_, kk=kk, c=c, col0=col0: nc.tensor.matmul(
                            p_[:, :], lhsT=w[:, kk, col0 + c * 128:col0 + (c + 1) * 128], rhs=hT_[:, kk, :],
                            start=(kk == 0), stop=(kk == 7)), reads=[wb, hTb_], writes=[pb_])
                    k.op("act", lambda p_=p_, s_=s_, c=c, scale=scale: nc.scalar.activation(
                        out=s_[:, c, :], in_=p_[:, :], func=AF.Copy, scale=scale), reads=[pb_], writes=[sb_])
                k.dma("sp", sb_, False, [(dst.rearrange("(c p) n -> p c n", p=128)[:, :, tok0:tok0 + 512], s_[:, :, :])])
            pre.back(gi + 1)
            for c in range(4):
                pu, pub = pm[pmi % 4], pmb[pmi % 4]
                pmi += 1
                pg, pgb = pm[pmi % 4], pmb[pmi % 4]
                pmi += 1
                for kk in range(8):
                    k.op("pe", lambda pu=pu, kk=kk, c=c: nc.tensor.matmul(
                        pu[:, :], lhsT=w[:, kk, 1536 + c * 128:1536 + (c + 1) * 128], rhs=hT_[:, kk, :],
                        start=(kk == 0), stop=(kk == 7)), reads=[wb, hTb_], writes=[pub])
                for kk in range(8):
                    k.op("pe", lambda pg=pg, kk=kk, c=c: nc.tensor.matmul(
                        pg[:, :], lhsT=w[:, kk, 2048 + c * 128:2048 + (c + 1) * 128], rhs=hT_[:, kk, :],
                        start=(kk == 0), stop=(kk == 7)), reads=[wb, hTb_], writes=[pgb])
                sg, sgb = sig[c % 2], sigb[c % 2]
                k.op("act", lambda pg=pg, sg=sg: nc.scalar.activation(out=sg[:, :], in_=pg[:, :], func=AF.Sigmoid),
                     reads=[pgb], writes=[sgb])
                k.op("dve", lambda pu=pu, sg=sg, c=c: nc.vector.tensor_tensor(
                    out=glu[:, c, :], in0=pu[:, :], in1=sg[:, :], op=ALU.mult), reads=[pub, sgb], writes=[glub])
            k.dma("sp", glub, False, [(C.GT.rearrange("(c p) n -> p c n", p=128)[:, :, tok0:tok0 + 512], glu[:, :, :])])
            for j in range(G):
                p_, pb_ = pm[pmi % 4], pmb[pmi % 4]
                pmi += 1
                for kk in range(8):
                    k.op("pe", lambda p_=p_, kk=kk, j=j: nc.tensor.matmul(
                        p_[:, :], lhsT=hT_[:, kk, j * 128:(j + 1) * 128], rhs=w[:, kk, 1024:1536],
                        start=(kk == 0), stop=(kk == 7)), reads=[wb, hTb_], writes=[pb_])
                v_, vb_ = va[j % 2], vab[j % 2]
                t = t0 + j
                k.op("act", lambda p_=p_, v_=v_, t=t: nc.scalar.activation(
                    out=v_[:, :, 0:64], in_=p_[:, :].rearrange("p (h d) -> p h d", h=8), func=AF.Copy,
                    scale=C.valid[:, t:t + 1]), reads=[pb_, C.b_const], writes=[vb_])
                k.op("dve", lambda v_=v_, t=t: nc.vector.tensor_copy(
                    out=v_[:, :, 64:65], in_=C.valid[:, t:t + 1].unsqueeze(2).to_broadcast([128, 8, 1])),
                    reads=[C.b_const], writes=[vb_])
                k.dma("sp", vb_, False, [(v0v[:, t, :], v_[:, :, :].rearrange("p h d -> p (h d)"))])
        k.barrier()
        k.release(pre.bufs() + tpsb + pmb + stqb + sigb + [glub] + vab + [wb, gain[1]])


COLS = {}
_off = 0
for _name, _n in (("g_mix0", 8), ("g_mix1", 8), ("g_cross0", 8), ("g_cross1", 8), ("g_mem0", 8), ("g_mem1", 8),
                  ("g_mlp0", 8), ("g_mlp1", 8), ("conv_w", 4 * 31), ("conv_b", 4), ("ln_g", 4), ("ln_b", 4),
                  ("subln_g", 1), ("q_norm_g", 3), ("kv_norm_g", 2), ("lam", 4), ("valid", 64), ("posf", 64)):
    COLS[_name] = (_off, _n)
    _off += _n
NCOL = _off


def pcols(v):
    v = np.asarray(v, np.float32)
    return np.ascontiguousarray(v.reshape(-1, 128).T)


def pack_cols(inp, valid, pos):
    out = np.zeros((128, NCOL), np.float32)

    def put(name, a):
        o, n = COLS[name]
        assert a.shape == (128, n), (name, a.shape)
        out[:, o:o + n] = a
    for l in range(2):
        put("g_mix%d" % l, pcols(inp["norm_mix_g"][l]))
        put("g_cross%d" % l, pcols(inp["norm_cross_g"][l]))
        put("g_mem%d" % l, pcols(inp["norm_mem_g"][l]))
        put("g_mlp%d" % l, pcols(inp["norm_mlp_g"][l]))
    cw = np.asarray(inp["ab_conv_w"], np.float32)[0, :, 0, :]
    put("conv_w", np.ascontiguousarray(cw.T.reshape(4, 128, 31).transpose(1, 0, 2).reshape(128, 124)))
    put("conv_b", pcols(inp["ab_conv_b"][0]))
    put("ln_g", pcols(inp["ab_ln_g"][0]))
    put("ln_b", pcols(inp["ab_ln_b"][0]))
    put("subln_g", pcols(inp["diff_subln_g"][0]))
    put("q_norm_g", pcols(inp["mla_q_norm_g"][0]))
    put("kv_norm_g", pcols(inp["mla_kv_norm_g"][0]))
    lam = np.zeros((128, 4), np.float32)
    for i, nm in enumerate(("diff_lq1", "diff_lk1", "diff_lq2", "diff_lk2")):
        lam[:64, i] = np.asarray(inp[nm], np.float32)[0]
    put("lam", lam)
    put("valid", pcols(valid))
    put("posf", pcols(pos))
    return out


WEIGHTS = (("ab_w_in", [1024, 2560]), ("ab_w_out", [1024, 1024]), ("cd_w_in", [1024, 2208]),
           ("cd_w_out", [1024, 1024]), ("mla_w_uq", [384, 768]), ("mla_w_uk", [256, 512]),
           ("mla_w_uv", [256, 512]), ("cross_wq0", [1024, 512]), ("cross_wq1", [1024, 512]),
           ("cross_wkv0", [1024, 1024]), ("cross_wkv1", [1024, 1024]), ("cross_wo0", [512, 1024]),
           ("cross_wo1", [512, 1024]), ("mlp_w10", [1024, 4096]), ("mlp_w11", [1024, 4096]),
           ("mlp_w20", [4096, 1024]), ("mlp_w21", [4096, 1024]))

SCRATCH = (("XA", [SEQ, D], F32), ("QT0", [512, SEQ], BF16), ("KT0", [512, SEQ], BF16),
           ("V0", [SEQ, 520], BF16), ("GT", [512, SEQ], BF16), ("YBT", [512, SEQ], BF16),
           ("ND", [3, SEQ, 520], F32), ("KCT", [512, SEQ], BF16), ("QCT", [1024, HALF], BF16),
           ("VC", [SEQ, 512], BF16), ("KDT", [768, SEQ], BF16), ("QDT", [768, HALF], BF16),
           ("VD", [SEQ, 512], BF16), ("YT1", [1024, HALF], BF16))


def build_program(phases, dbg=()):
    nc = bass.Bass("TRN2", target_bir_lowering=False)
    k = K(nc)
    C = Ctx()
    C.x_in = dram_in(nc, "x", [SEQ, D])
    C.mem = dram_in(nc, "mem", [256, D])
    C.cols_d = dram_in(nc, "cols", [128, NCOL])
    C.fng = dram_in(nc, "final_norm_g", [1, D])
    C.pos_d = dram_in(nc, "pos", [1, SEQ], I32)
    C.gvec = dram_in(nc, "gvec", [8, D])
    C.yt1_tok0 = HALF
    for name, shape in WEIGHTS:
        setattr(C, name, dram_in(nc, name, shape))
    for name, shape, dt in SCRATCH:
        kind = "ExternalOutput" if name in dbg else "Internal"
        setattr(C, name, nc.dram_tensor(name, list(shape), dt, kind=kind).ap())
    for name, shape in WEIGHTS:
        if name != "ab_w_in":
            setattr(C, "W_" + name, nc.dram_tensor("W_" + name, list(shape), BF16, kind="Internal").ap())
    C.out = nc.dram_tensor("out", [HALF, D], F32, kind="ExternalOutput").ap()
    with k.es, ExitStack() as st:
        build_consts(k, C, st)
        C.cols = k.sb(st, "cols", [128, NCOL], F32)
        k.dma("sp", C.b_const, True, [(C.cols[:, :], C.cols_d[:, :])])
        C.eps_col = k.sb(st, "eps", [128, 1], F32)
        k.op("pool", lambda: nc.gpsimd.memset(C.eps_col[:], EPS), writes=[C.b_const])

        def col(name):
            o, n = COLS[name]
            return C.cols[:, o:o + n]
        C.g_mix = [col("g_mix0"), col("g_mix1")]
        C.g_cross = [col("g_cross0"), col("g_cross1")]
        C.g_mem = [col("g_mem0"), col("g_mem1")]
        C.g_mlp = [col("g_mlp0"), col("g_mlp1")]
        C.valid = col("valid")
        C.col = col
        C.bg = BG(k, C, st)
        bg = C.bg
        def bg_register():
            bg.add("ab_w_out", 8, 1024)
            bg.add("cross_wq0", 8, 512, C.g_cross[0])
            bg.add("cross_wkv0", 8, 1024, C.g_mem[0])
            bg.add("cross_wo0", 4, 1024)
            bg.add("mlp_w10", 8, 4096, C.g_mlp[0])
            bg.add("mlp_w20", 32, 1024)
            bg.add("cd_w_in", 8, 2208, C.g_mix[1])
            bg.add("mla_w_uq", 3, 768, col("q_norm_g"))
            bg.add("mla_w_uk", 2, 512, col("kv_norm_g"))
            bg.add("mla_w_uv", 2, 512, col("kv_norm_g"))
            bg.add("cd_w_out", 8, 1024)
            bg.add("cross_wq1", 8, 512, C.g_cross[1])
            bg.add("cross_wkv1", 8, 1024, C.g_mem[1])
            bg.add("cross_wo1", 4, 1024)
            bg.add("mlp_w11", 8, 4096, C.g_mlp[1])
            bg.add("mlp_w21", 32, 1024)
        C.bg_register = bg_register
        k.barrier()
        with ExitStack() as sh:
            C.diag = k.sb(sh, "diag", [128, 4, 31, 128], BF16)
            C.diagb = k.buf("diag")
            def build_diag():
                o_cw = COLS["conv_w"][0]
                for c in range(4):
                    for tap in range(31):
                        k.op("pool", lambda c=c, tap=tap: nc.gpsimd.tensor_scalar(
                            out=C.diag[:, c, tap, :], in0=C.ident_b[:, :],
                            scalar1=C.cols[:, o_cw + c * 31 + tap:o_cw + c * 31 + tap + 1], scalar2=None, op0=ALU.mult),
                            reads=[C.b_const], writes=[C.diagb])
            C.build_diag = build_diag
            if "a0" in phases:
                sv = C.XA
                C.XA = C.x_in
                phase_a0(k, C, 0, NT)
                C.XA = sv
            if "a0b" in phases:
                phase_a0b(k, C, 0, NT)
            k.release([C.diagb])
        if "a1" in phases:
            phase_a1(k, C, 0, NT)
        if "a2" in phases:
            phase_a2(k, C, 0, 0, NT, C.x_in, "ab")
        if "a3" in phases:
            phase_a3(k, C, 0, 0, NT)
        if "b0" in phases:
            phase_b0(k, C, 0, NT)
        if "b1" in phases:
            phase_b1(k, C)
        if "b2" in phases:
            phase_a2(k, C, 1, NT - HALF // 128, NT, C.XA, "cd")
        if "b3" in phases:
            phase_a3(k, C, 1, NT - HALF // 128, NT, final=True)
        k.barrier()
    return nc


def core_inputs(inp, core):
    b, h = core // 2, core % 2
    x = np.asarray(inp["x"], np.float32)
    pos = np.asarray(inp["positions"])
    xl = np.zeros((SEQ, D), np.float32)
    valid = np.zeros((SEQ,), np.float32)
    pl = np.zeros((SEQ,), np.int32)
    xl[HALF:] = x[b, h * HALF:(h + 1) * HALF]
    valid[HALF:] = 1.0
    pl[HALF:] = pos[b, h * HALF:(h + 1) * HALF]
    if h == 1:
        xl[:HALF] = x[b, :HALF]
        valid[:HALF] = 1.0
        pl[:HALF] = pos[b, :HALF]
    m = {"x": xl, "mem": np.ascontiguousarray(np.asarray(inp["mem"], np.float32)[b]),
         "cols": pack_cols(inp, valid, np.zeros((SEQ,), np.float32)), "pos": pl.reshape(1, SEQ),
         "final_norm_g": np.asarray(inp["final_norm_g"], np.float32).reshape(1, D),
         "gvec": np.ascontiguousarray(np.concatenate([np.asarray(inp[n], np.float32) for n in
                                                       ("norm_mix_g", "norm_cross_g", "norm_mem_g", "norm_mlp_g")], 0))}
    for name, shape in WEIGHTS:
        if name[-1] in "01" and name[:-1] in inp:
            a = np.asarray(inp[name[:-1]], np.float32)[int(name[-1])]
        else:
            a = np.asarray(inp[name], np.float32)[0]
        m[name] = np.ascontiguousarray(a.reshape(shape))
    return m


def phase_a0b(k, C, t_lo, t_hi):
    nc = k.nc
    with ExitStack() as st:
        G = 4
        diag, diagb = C.diag, C.diagb
        conv_b, ln_g, ln_b = C.col("conv_b"), C.col("ln_g"), C.col("ln_b")
        gl = [k.sb(st, "gl", [128, 4, 30 + G * 128], BF16) for _ in range(2)]
        glb = k.bufs_n(2, "gl")
        pc = [k.ps(st, "pc", [128, 512], F32) for _ in range(2)]
        pcb = k.bufs_n(2, "pc")
        pmean = k.ps(st, "pmean", [128, 512], F32)
        pvar = k.ps(st, "pvar", [128, 512], F32)
        pmeanb, pvarb = k.buf("pmean"), k.buf("pvar")
        cvs = [k.sb(st, "cv", [128, 4, 512], F32) for _ in range(2)]
        cvbs = k.bufs_n(2, "cv")
        sq = k.sb(st, "sq", [128, 4, 512], F32)
        sqb = k.buf("sq")
        rs = k.sb(st, "rs", [128, 512], F32)
        rsb = k.buf("rs")
        yb = [k.sb(st, "yb", [128, 4, 512], BF16) for _ in range(2)]
        ybb = k.bufs_n(2, "yb")
        gtv = C.GT.rearrange("(c p) n -> p c n", p=128)
        ybv = C.YBT.rearrange("(c p) n -> p c n", p=128)
        ngrp = (t_hi - t_lo) // G
        def conv_stage(gi):
            cv, cvb = cvs[gi % 2], cvbs[gi % 2]
            tok0 = (t_lo + gi * G) * 128
            g_, gb_ = gl[gi % 2], glb[gi % 2]
            if tok0 == 0:
                k.op("pool", lambda g_=g_: nc.gpsimd.memset(g_[:, :, 0:30], 0.0), writes=[gb_])
                k.dma("sp", gb_, True, [(g_[:, :, 30:30 + 512], gtv[:, :, 0:512])])
            else:
                k.dma("sp", gb_, True, [(g_[:, :, :], gtv[:, :, tok0 - 30:tok0 + 512])])
            for c in range(4):
                p_, pb_ = pc[c % 2], pcb[c % 2]
                for tap in range(31):
                    k.op("pe", lambda p_=p_, c=c, tap=tap, g_=g_: nc.tensor.matmul(
                        p_[:, :], lhsT=diag[:, c, tap, :], rhs=g_[:, c, tap:tap + 512],
                        start=(tap == 0), stop=(tap == 30)), reads=[diagb, gb_], writes=[pb_])
                k.op("act", lambda p_=p_, c=c: nc.scalar.activation(
                    out=cv[:, c, :], in_=p_[:, :], func=AF.Identity, bias=conv_b[:, c:c + 1]),
                    reads=[pb_, C.b_const], writes=[cvb])
                yield

        def ln_stage(gi):
            cv, cvb = cvs[gi % 2], cvbs[gi % 2]
            tok0 = (t_lo + gi * G) * 128
            for c in range(4):
                k.op("pe", lambda c=c: nc.tensor.matmul(pmean[:, :], lhsT=C.ones_f[:, :], rhs=cv[:, c, :],
                                                        start=(c == 0), stop=(c == 3)),
                     reads=[cvb, C.b_const], writes=[pmeanb])
            yield
            for c in range(4):
                k.op("dve", lambda c=c: nc.vector.scalar_tensor_tensor(
                    out=cv[:, c, :], in0=pmean[:, :], scalar=-1.0 / 512, in1=cv[:, c, :],
                    op0=ALU.mult, op1=ALU.add), reads=[pmeanb, cvb], writes=[cvb])
                k.op("act", lambda c=c: nc.scalar.activation(out=sq[:, c, :], in_=cv[:, c, :], func=AF.Square),
                     reads=[cvb], writes=[sqb])
            yield
            for c in range(4):
                k.op("pe", lambda c=c: nc.tensor.matmul(pvar[:, :], lhsT=C.ones_f[:, :], rhs=sq[:, c, :],
                                                        start=(c == 0), stop=(c == 3)),
                     reads=[sqb, C.b_const], writes=[pvarb])
            k.op("act", lambda: nc.scalar.activation(out=rs[:, :], in_=pvar[:, :], func=AF.Sqrt,
                                                     bias=C.eps_col[:, 0:1], scale=1.0 / 512),
                 reads=[pvarb, C.b_const], writes=[rsb])
            k.op("dve", lambda: nc.vector.reciprocal(out=rs[:, :], in_=rs[:, :]), reads=[rsb], writes=[rsb])
            yield
            y_, yb_ = yb[gi % 2], ybb[gi % 2]
            for c in range(4):
                k.op("dve", lambda c=c: nc.vector.tensor_tensor(out=cv[:, c, :], in0=cv[:, c, :], in1=rs[:, :],
                                                                op=ALU.mult), reads=[cvb, rsb], writes=[cvb])
                k.op("dve", lambda c=c: nc.vector.tensor_scalar(
                    out=cv[:, c, :], in0=cv[:, c, :], scalar1=ln_g[:, c:c + 1], scalar2=ln_b[:, c:c + 1],
                    op0=ALU.mult, op1=ALU.add), reads=[cvb, C.b_const], writes=[cvb])
                k.op("act", lambda c=c, y_=y_: nc.scalar.activation(out=y_[:, c, :], in_=cv[:, c, :], func=AF.Silu),
                     reads=[cvb], writes=[yb_])
            k.dma("sp", yb_, False, [(ybv[:, :, tok0:tok0 + 512], y_[:, :, :])])

        def drive(gens):
            gens = [g for g in gens if g is not None]
            while gens:
                for g in list(gens):
                    try:
                        next(g)
                    except StopIteration:
                        gens.remove(g)

        drive([conv_stage(0)])
        for gi in range(ngrp):
            drive([conv_stage(gi + 1) if gi + 1 < ngrp else None, ln_stage(gi)])
        k.barrier()
        k.release(glb + pcb + [pmeanb, pvarb, sqb, rsb] + cvbs + ybb)


A_PATTERNS = (1, 4, 16)


def phase_a1(k, C, q_lo, q_hi):
    nc = k.nc
    with ExitStack() as st:
        KT = k.sb(st, "KT", [128, 4, SEQ], BF16)
        QT = k.sb(st, "QT", [128, 4, SEQ], BF16)
        ktbs, qtbs = k.bufs_n(4, "KT"), k.bufs_n(4, "QT")
        ktv = C.KT0.rearrange("(c p) n -> p c n", p=128)
        qtv = C.QT0.rearrange("(c p) n -> p c n", p=128)
        for c in range(4):
            k.dma("sp", ktbs[c], True, [(KT[:, c, :], ktv[:, c, :])])
            k.dma("sp", qtbs[c], True, [(QT[:, c, :], qtv[:, c, :])])
        vt = [k.sb(st, "vt", [128, 8, 65], BF16) for _ in range(3)]
        vtb = k.bufs_n(3, "vt")
        pss2 = [k.ps(st, "pss", [128, 2, 2, 2, 128], F32) for _ in range(2)]
        pss = [[t[:, 0], t[:, 1]] for t in pss2]
        pssb = k.bufs_n(2, "pss")
        po = [k.ps(st, "po", [128, 512], F32)[:, 0:260].rearrange("p (h d) -> p h d", h=4) for _ in range(2)]
        pob = k.bufs_n(2, "po")
        pt = [k.sb(st, "pt", [128, 2, 2, 2, 128], BF16) for _ in range(2)]
        ptb = k.bufs_n(2, "pt")
        osb = [k.sb(st, "osb", [128, 8, 65], F32) for _ in range(2)]
        osbb = k.bufs_n(2, "osb")
        ui = 0
        bi = 0
        mge = C.mask_ge[:, :].unsqueeze(1).to_broadcast([128, 4, 128])
        mle = C.mask_le[:, :].unsqueeze(1).to_broadcast([128, 4, 128])
        units = []
        for gidx, d in enumerate(A_PATTERNS):
            span = 128 * d
            nlo, nhi = (q_lo * 128) // span, (q_hi * 128) // span
            for r in range(d):
                for n in range(nlo, nhi):
                    for hh in range(2):
                        units.append((gidx, d, r, n, hh, nlo, nhi))

        def vload(d, r, n):
            s0 = n * 128 * d + r
            k.dma("sp", vtb[n % 3], True,
                  [(vt[n % 3][:, :, :].rearrange("p h d -> p (h d)"), C.V0[s0:s0 + 127 * d + 1:d, :])])

        def qk(ui):
            gidx, d, r, n, hh, nlo, nhi = units[ui]
            span = 128 * d
            s0 = n * span + r
            sq_ = slice(s0, s0 + 127 * d + 1, d)
            sp_ = slice(s0 - span, s0 - span + 127 * d + 1, d)
            kbs = (0, 1) if n > 0 else (1,)
            ps_, psb_ = pss[ui % 2], pssb[ui % 2]
            for par in range(2):
                for hpl in range(2):
                    hp = hh * 2 + hpl
                    for kb in kbs:
                        ks = sp_ if kb == 0 else sq_
                        k.op("pe", lambda: nc.tensor.matmul(
                            ps_[par][:, hpl, kb, :], lhsT=KT[par * 64:(par + 1) * 64, hp, ks],
                            rhs=QT[par * 64:(par + 1) * 64, hp, sq_], start=True, stop=True),
                            reads=[ktbs[hp], qtbs[hp]], writes=[psb_])

        def rest(ui):
            gidx, d, r, n, hh, nlo, nhi = units[ui]
            span = 128 * d
            s0 = n * span + r
            sq_ = slice(s0, s0 + 127 * d + 1, d)
            kbs = (0, 1) if n > 0 else (1,)
            if hh == 0:
                if n == nlo:
                    if nlo > 0:
                        vload(d, r, nlo - 1)
                    vload(d, r, nlo)
                if n + 1 < nhi:
                    vload(d, r, n + 1)
            bi = ui // 2
            o_, ob_ = osb[bi % 2], osbb[bi % 2]
            ps_, psb_ = pss[ui % 2], pssb[ui % 2]
            pt_, ptb_ = pt[ui % 2], ptb[ui % 2]
            po_, pob_ = po[ui % 2], pob[ui % 2]
            psf = pss2[ui % 2]
            if n > 0:
                k.op("act", lambda: nc.scalar.activation(
                    out=pt_[:, :, :, :, :], in_=psf[:, :, :, :, :], func=AF.Exp),
                    reads=[psb_], writes=[ptb_])
            else:
                for par in range(2):
                    k.op("act", lambda: nc.scalar.activation(
                        out=pt_[:, par, :, 1, :], in_=ps_[par][:, :, 1, :], func=AF.Exp),
                        reads=[psb_], writes=[ptb_])
            ptv = pt_[:, :, :, :, :].rearrange("p a b c q -> p (a b) c q")
            if n > 0:
                k.op("dve", lambda: nc.vector.tensor_tensor(
                    out=ptv[:, :, 0, :], in0=ptv[:, :, 0, :], in1=mge, op=ALU.mult),
                    reads=[C.b_const], writes=[ptb_])
            k.op("dve", lambda: nc.vector.tensor_tensor(
                out=ptv[:, :, 1, :], in0=ptv[:, :, 1, :], in1=mle, op=ALU.mult),
                reads=[C.b_const], writes=[ptb_])
            for hl in range(4):
                h = hh * 4 + hl
                hp, par = h // 2, h % 2
                hpl = hp - hh * 2
                for i, kb in enumerate(kbs):
                    vb = (n - 1) % 3 if kb == 0 else n % 3
                    k.op("pe", lambda: nc.tensor.matmul(
                        po_[:, hl, :], lhsT=pt_[:, par, hpl, kb, :], rhs=vt[vb][:, h, :],
                        start=(i == 0), stop=(i == len(kbs) - 1)),
                        reads=[ptb_, vtb[vb]], writes=[pob_])

        def evac(ui):
            gidx, d, r, n, hh, nlo, nhi = units[ui]
            s0 = n * 128 * d + r
            sq_ = slice(s0, s0 + 127 * d + 1, d)
            bi = ui // 2
            o_, ob_ = osb[bi % 2], osbb[bi % 2]
            po_, pob_ = po[ui % 2], pob[ui % 2]
            k.op("act", lambda: nc.scalar.copy(out=o_[:, hh * 4:(hh + 1) * 4, :], in_=po_[:, :, :]), reads=[pob_], writes=[ob_])
            if hh == 1:
                k.dma("sp", ob_, False, [(C.ND[gidx, sq_, :], o_[:, :, :].rearrange("p h d -> p (h d)"))])

        qk(0)
        for ui in range(len(units)):
            if ui + 1 < len(units):
                qk(ui + 1)
            rest(ui)
            if ui > 0:
                evac(ui - 1)
        evac(len(units) - 1)
        k.barrier()
        k.release(ktbs + qtbs + vtb + pssb + pob + ptb + osbb)


def cross_kv(k, C, st, layer):
    nc = k.nc
    KmT = k.sb(st, "KmT", [128, 4, 256], BF16)
    Vm = k.sb(st, "Vm", [128, 2, 4, 129], BF16)
    kvb = k.buf("crosskv")
    with ExitStack() as s2:
        wkv = k.sb(s2, "wkv", [128, 8, 1024], BF16)
        wb = load_weight_bf(k, C, wkv[:, :, :], "cross_wkv%d" % layer)
        tmpb = []
        xm = k.sb(s2, "xm", [128, 2, D], F32)
        hb = k.sb(s2, "hbm", [128, 2, D], BF16)
        junk = k.sb(s2, "junkm", [128, D], BF16)
        ssq = k.sb(s2, "ssqm", [128, 8], F32)
        rstd = k.sb(s2, "rstdm", [128, 8], F32)
        mT = k.sb(s2, "mT", [128, 8, 256], BF16)
        tps = [k.ps(s2, "tpm", [128, 8, 128], BF16) for _ in range(2)]
        pm = [k.ps(s2, "pmm", [128, 512], F32) for _ in range(2)]
        xb, hbb, jb, sb_, mTb = k.buf("xm"), k.buf("hbm"), k.buf("jm"), k.buf("ssqm"), k.buf("mT")
        tpsb, pmb = k.bufs_n(2, "tpm"), k.bufs_n(2, "pmm")
        k.dma("sp", xb, True, [(xm[:, :, :], C.mem.rearrange("(t p) d -> p t d", p=128))])
        gain = load_gain(k, C, s2, 4 + layer)
        norm_rows(k, C, xm, xb, 2, ssq, sb_, rstd, junk, jb, hb, hbb, gain=gain)
        transpose_to(k, C, hb, hbb, 2, 8, tps, tpsb, mT, mTb)
        for hd in range(4):
            p_, pb_ = pm[hd % 2], pmb[hd % 2]
            for kk in range(8):
                k.op("pe", lambda p_=p_, kk=kk, hd=hd: nc.tensor.matmul(
                    p_[:, 0:256], lhsT=wkv[:, kk, hd * 128:(hd + 1) * 128], rhs=mT[:, kk, :],
                    start=(kk == 0), stop=(kk == 7)), reads=[wb, mTb], writes=[pb_])
            k.op("act", lambda p_=p_, hd=hd: nc.scalar.copy(out=KmT[:, hd, :], in_=p_[:, 0:256]),
                 reads=[pb_], writes=[kvb])
        for mt in range(2):
            p_, pb_ = pm[mt % 2], pmb[mt % 2]
            for kk in range(8):
                k.op("pe", lambda p_=p_, kk=kk, mt=mt: nc.tensor.matmul(
                    p_[:, :], lhsT=mT[:, kk, mt * 128:(mt + 1) * 128], rhs=wkv[:, kk, 512:1024],
                    start=(kk == 0), stop=(kk == 7)), reads=[wb, mTb], writes=[pb_])
            k.op("act", lambda p_=p_, mt=mt: nc.scalar.copy(
                out=Vm[:, mt, :, 0:128], in_=p_[:, :].rearrange("p (h d) -> p h d", h=4)), reads=[pb_], writes=[kvb])
            k.op("dve", lambda mt=mt: nc.vector.memset(Vm[:, mt, :, 128:129], 1.0), writes=[kvb])
        k.barrier()
        k.release(tmpb + [wb, xb, hbb, jb, sb_, mTb, gain[1]] + tpsb + pmb)
    return KmT, Vm, kvb


def phase_a2(k, C, layer, t_lo, t_hi, x_src, mix_kind):
    nc = k.nc
    with ExitStack() as st:
        KmT, Vm, kvb = cross_kv(k, C, st, layer)
        w_out = k.sb(st, "w_out", [128, 8, 1024], BF16)
        wq = k.sb(st, "wq", [128, 8, 512], BF16)
        wo = k.sb(st, "wo", [128, 4, 1024], BF16)
        wob = load_weight_bf(k, C, w_out[:, :, :], "ab_w_out" if mix_kind == "ab" else "cd_w_out")
        wqb = load_weight_bf(k, C, wq[:, :, :], "cross_wq%d" % layer)
        wcb = load_weight_bf(k, C, wo[:, :, :], "cross_wo%d" % layer)
        G = 4
        gain = load_gain(k, C, st, 2 + layer)
        xg = [k.sb(st, "xg", [128, G, D], F32) for _ in range(2)]
        xgb = k.bufs_n(2, "xg")
        nd = [k.sb(st, "nd", [128, 3, 520], F32) for _ in range(8 if mix_kind == "ab" else 1)]
        ndb = k.bufs_n(len(nd), "nd")
        rden = k.sb(st, "rden", [128, 8], F32)
        rdenb = k.buf("rden")
        rden2 = k.sb(st, "rden2", [128, 8], F32)
        rden2b = k.buf("rden2")
        ya = k.sb(st, "ya", [128, G, 512], BF16)
        yab = k.buf("ya")
        yT = [k.sb(st, "yT", [128, 8, G * 128], BF16) for _ in range(2)]
        yTb = k.bufs_n(2, "yT")
        hb = k.sb(st, "hb", [128, G, D], BF16)
        hbb = k.buf("hb")
        junk = k.sb(st, "junk", [128, D], BF16)
        junkb = k.buf("junk")
        ssq = k.sb(st, "ssq", [128, 8], F32)
        rstd = k.sb(st, "rstd", [128, 8], F32)
        ssqb = k.buf("ssq")
        hTs = [k.sb(st, "hT", [128, 8, G * 128], BF16) for _ in range(2)]
        hTbs = k.bufs_n(2, "hT")
        qT = k.sb(st, "qT", [128, 4, G * 128], BF16)
        qTb = k.buf("qT")
        pT = k.sb(st, "pT", [128, 4, 2, G * 128], BF16)
        pTb = k.buf("pT")
        oc = k.sb(st, "oc", [128, G, 512], BF16)
        ocb = k.buf("oc")
        oT = k.sb(st, "oT", [128, 4, G * 128], BF16)
        oTb = k.buf("oT")
        tps = [k.ps(st, "tp", [128, 8, 128], BF16) for _ in range(2)]
        tpsb = k.bufs_n(2, "tp")
        pm = [k.ps(st, "pm", [128, 512], F32) for _ in range(4)]
        pmb = k.bufs_n(4, "pm")
        pox = [k.ps(st, "pox", [128, 512], F32) for _ in range(2)]
        poxb = k.bufs_n(2, "pox")
        xsv = x_src.rearrange("(t p) d -> p t d", p=128)
        xdv = C.XA.rearrange("(t p) d -> p t d", p=128)
        cnt = {'pmi': 0, 'ndi': 0}
        ngrp = (t_hi - t_lo) // G

        def S1(gi):
            t0 = t_lo + gi * G
            tok0 = t0 * 128
            x_, xb_ = xg[gi % 2], xgb[gi % 2]
            hT, hTb = hTs[gi % 2], hTbs[gi % 2]
            yT_, yTb_ = yT[gi % 2], yTb[gi % 2]
            k.dma("sp", xb_, True, [(x_[:, :, :], xsv[:, t0:t0 + G, :])])
            if mix_kind == "ab":
                k.dma("sp", yTb_, True, [(yT_[:, 4:8, :], C.YBT.rearrange("(c p) n -> p c n", p=128)[:, :, tok0:tok0 + G * 128])])
                def nd_load(g2):
                    if g2 >= ngrp:
                        return
                    for j in range(G):
                        t = t_lo + g2 * G + j
                        i = (g2 % 2) * G + j
                        k.dma("sp", ndb[i], True, [(nd[i][:, :, :], C.ND[:, t * 128:(t + 1) * 128, :].rearrange("g p c -> p g c"))])
                if gi == 0:
                    nd_load(0)
                nd_load(gi + 1)
                yield
                for j in range(G):
                    n_, nb_ = nd[(gi % 2) * G + j], ndb[(gi % 2) * G + j]
                    t = t0 + j
                    k.op("dve", lambda n_=n_: nc.vector.tensor_tensor(out=n_[:, 0, :], in0=n_[:, 0, :], in1=n_[:, 1, :], op=ALU.add),
                         reads=[nb_], writes=[nb_])
                    k.op("dve", lambda n_=n_: nc.vector.tensor_tensor(out=n_[:, 0, :], in0=n_[:, 0, :], in1=n_[:, 2, :], op=ALU.add),
                         reads=[nb_], writes=[nb_])
                    nv = n_[:, 0, :].rearrange("p (h d) -> p h d", h=8)
                    k.op("dve", lambda nv=nv: nc.vector.tensor_scalar(out=rden[:, :].unsqueeze(2), in0=nv[:, :, 64:65], scalar1=1e-30, scalar2=None, op0=ALU.max),
                         reads=[nb_], writes=[rdenb])
                    k.op("dve", lambda: nc.vector.reciprocal(out=rden[:, :], in_=rden[:, :]), reads=[rdenb], writes=[rdenb])
                    k.op("dve", lambda nv=nv, j=j: nc.vector.tensor_tensor(
                        out=ya[:, j, :].rearrange("p (h d) -> p h d", h=8), in0=nv[:, :, 0:64],
                        in1=rden[:, :].unsqueeze(2).to_broadcast([128, 8, 64]), op=ALU.mult),
                        reads=[nb_, rdenb], writes=[yab])
                    yield
                transpose_to(k, C, ya, yab, G, 4, tps, tpsb, yT_, yTb_)
                yield
            else:
                o0 = tok0 - C.yt1_tok0
                k.dma("sp", yTb_, True, [(yT_[:, :, :], C.YT1.rearrange("(c p) n -> p c n", p=128)[:, :, o0:o0 + G * 128])])
            for j in range(G):
                for half in range(2):
                    p_, pb_ = pm[cnt['pmi'] % 4], pmb[cnt['pmi'] % 4]
                    cnt['pmi'] += 1
                    for kk in range(8):
                        k.op("pe", lambda p_=p_, kk=kk, j=j, half=half: nc.tensor.matmul(
                            p_[:, :], lhsT=yT_[:, kk, j * 128:(j + 1) * 128], rhs=w_out[:, kk, half * 512:(half + 1) * 512],
                            start=(kk == 0), stop=(kk == 7)), reads=[wob, yTb_], writes=[pb_])
                    k.op("dve", lambda p_=p_, j=j, half=half: nc.vector.tensor_tensor(
                        out=x_[:, j, half * 512:(half + 1) * 512], in0=p_[:, :], in1=x_[:, j, half * 512:(half + 1) * 512],
                        op=ALU.add), reads=[pb_, xb_], writes=[xb_])
                yield
            yield
            norm_rows(k, C, x_, xb_, G, ssq, ssqb, rstd, junk, junkb, hb, hbb, gain=gain)
            yield "hold"
            transpose_to(k, C, hb, hbb, G, 8, tps, tpsb, hT, hTb)

        def S2(gi):
            t0 = t_lo + gi * G
            tok0 = t0 * 128
            x_, xb_ = xg[gi % 2], xgb[gi % 2]
            hT, hTb = hTs[gi % 2], hTbs[gi % 2]
            for hd in range(4):
                p_, pb_ = pm[cnt['pmi'] % 4], pmb[cnt['pmi'] % 4]
                cnt['pmi'] += 1
                for kk in range(8):
                    k.op("pe", lambda p_=p_, kk=kk, hd=hd: nc.tensor.matmul(
                        p_[:, :], lhsT=wq[:, kk, hd * 128:(hd + 1) * 128], rhs=hT[:, kk, :],
                        start=(kk == 0), stop=(kk == 7)), reads=[wqb, hTb], writes=[pb_])
                k.op("act", lambda p_=p_, hd=hd: nc.scalar.copy(out=qT[:, hd, :], in_=p_[:, :]), reads=[pb_], writes=[qTb])
                yield
            for hd in range(4):
                for mt in range(2):
                    p_, pb_ = pm[cnt['pmi'] % 4], pmb[cnt['pmi'] % 4]
                    cnt['pmi'] += 1
                    k.op("pe", lambda p_=p_, hd=hd, mt=mt: nc.tensor.matmul(
                        p_[:, :], lhsT=KmT[:, hd, mt * 128:(mt + 1) * 128], rhs=qT[:, hd, :], start=True, stop=True),
                        reads=[kvb, qTb], writes=[pb_])
                    k.op("act", lambda p_=p_, hd=hd, mt=mt: nc.scalar.activation(
                        out=pT[:, hd, mt, :], in_=p_[:, :], func=AF.Exp, scale=128.0 ** -0.5), reads=[pb_], writes=[pTb])
                yield
            for j in range(G):
                for hh in range(2):
                    po_, pob_ = pox[hh], poxb[hh]
                    pov = po_[:, 0:258].rearrange("p (h d) -> p h d", h=2)
                    for hl in range(2):
                        hd = hh * 2 + hl
                        for mt in range(2):
                            k.op("pe", lambda pov=pov, hl=hl, hd=hd, mt=mt, j=j: nc.tensor.matmul(
                                pov[:, hl, :], lhsT=pT[:, hd, mt, j * 128:(j + 1) * 128], rhs=Vm[:, mt, hd, :],
                                start=(mt == 0), stop=(mt == 1)), reads=[pTb, kvb], writes=[pob_])
                    k.op("dve", lambda pov=pov, hh=hh: nc.vector.reciprocal(
                        out=rden2[:, hh * 2:hh * 2 + 2].unsqueeze(2), in_=pov[:, :, 128:129]), reads=[pob_], writes=[rden2b])
                    k.op("dve", lambda pov=pov, hh=hh, j=j: nc.vector.tensor_tensor(
                        out=oc[:, j, hh * 256:(hh + 1) * 256].rearrange("p (h d) -> p h d", h=2), in0=pov[:, :, 0:128],
                        in1=rden2[:, hh * 2:hh * 2 + 2].unsqueeze(2).to_broadcast([128, 2, 128]), op=ALU.mult),
                        reads=[pob_, rden2b], writes=[ocb])
                yield
            transpose_to(k, C, oc, ocb, G, 4, tps, tpsb, oT, oTb)
            yield
            for j in range(G):
                for half in range(2):
                    p_, pb_ = pm[cnt['pmi'] % 4], pmb[cnt['pmi'] % 4]
                    cnt['pmi'] += 1
                    for kk in range(4):
                        k.op("pe", lambda p_=p_, kk=kk, j=j, half=half: nc.tensor.matmul(
                            p_[:, :], lhsT=oT[:, kk, j * 128:(j + 1) * 128], rhs=wo[:, kk, half * 512:(half + 1) * 512],
                            start=(kk == 0), stop=(kk == 3)), reads=[wcb, oTb], writes=[pb_])
                    k.op("dve", lambda p_=p_, j=j, half=half: nc.vector.tensor_tensor(
                        out=x_[:, j, half * 512:(half + 1) * 512], in0=p_[:, :], in1=x_[:, j, half * 512:(half + 1) * 512],
                        op=ALU.add), reads=[pb_, xb_], writes=[xb_])
                yield
            k.dma("sp", xb_, False, [(xdv[:, t0:t0 + G, :], x_[:, :, :])])

        drive([S1(0)])
        for gi in range(ngrp):
            drive([S2(gi), S1(gi + 1) if gi + 1 < ngrp else None])
        k.barrier()
        k.release([kvb, wob, wqb, wcb, gain[1]] + xgb + ndb + [rdenb, rden2b, yab] + yTb + [hbb, junkb, ssqb, qTb, pTb, ocb, oTb] + hTbs
                  + tpsb + pmb + poxb)


def phase_a3(k, C, layer, t_lo, t_hi, final=False):
    nc = k.nc
    with ExitStack() as st:
        w1 = k.sb(st, "w1", [128, 8, 4096], BF16)
        w2 = k.sb(st, "w2", [128, 32, 1024], BF16)
        n1, n2 = "mlp_w1%d" % layer, "mlp_w2%d" % layer
        w1b, w2b = k.bufs_n(4, "w1c"), k.bufs_n(4, "w2c")
        s1 = getattr(C, "W_" + n1).rearrange("(c p) n -> p c n", p=128)
        s2 = getattr(C, "W_" + n2).rearrange("(c p) n -> p c n", p=128)
        for c in range(4):
            k.dma("sp", w1b[c], True, [(w1[:, :, c * 1024:(c + 1) * 1024], s1[:, :, c * 1024:(c + 1) * 1024])],
                  extra_reads=[C.bg.wbuf[n1]])
        for c in range(4):
            k.dma("sp", w2b[c], True, [(w2[:, c * 8:(c + 1) * 8, :], s2[:, c * 8:(c + 1) * 8, :])], extra_reads=[C.bg.wbuf[n2]])
        G = 2
        N = G * 128
        gain = load_gain(k, C, st, 6 + layer)
        hid = k.sb(st, "hid", [128, 32, N], BF16)
        hidb = k.bufs_n(32, "hid")
        rl = [k.sb(st, "rl", [128, N], BF16) for _ in range(2)]
        rlb = k.bufs_n(2, "rl")
        tps = [k.ps(st, "tp", [128, 8, 128], BF16) for _ in range(2)]
        tpsb = k.bufs_n(2, "tp")
        pm = [k.ps(st, "pm", [128, 512], F32) for _ in range(4)]
        pmb = k.bufs_n(4, "pm")
        if final:
            gfin = k.sb(st, "gfin", [128, D], F32)
            gfb = k.buf("gfin")
            k.dma("sp", gfb, True, [(gfin[:, :], C.fng.to_broadcast([128, D]))])
        xv = C.XA.rearrange("(t p) d -> p t d", p=128)
        pmi = 0
        ngrp = (t_hi - t_lo) // G
        pre = PreStage(k, C, st, G, xv, gain, tps, tpsb, t_lo, ngrp)
        junk, junkb, ssq, ssqb, rstd = pre.junk, pre.junkb, pre.ssq[0], pre.ssqb[0], pre.rstd[0]
        pre.start()
        for gi in range(ngrp):
            t0 = t_lo + gi * G
            pre.front(gi + 1)
            x_, xb_, hT, hTb = pre.get(gi)
            for f in range(32):
                p_, pb_ = pm[pmi % 4], pmb[pmi % 4]
                pmi += 1
                for kk in range(8):
                    k.op("pe", lambda p_=p_, kk=kk, f=f: nc.tensor.matmul(
                        p_[:, 0:N], lhsT=w1[:, kk, f * 128:(f + 1) * 128], rhs=hT[:, kk, :],
                        start=(kk == 0), stop=(kk == 7)), reads=[w1b[f // 8], hTb], writes=[pb_])
                r_, rb_ = rl[f % 2], rlb[f % 2]
                k.op("act", lambda p_=p_, r_=r_: nc.scalar.activation(out=r_[:, :], in_=p_[:, 0:N], func=AF.Relu),
                     reads=[pb_], writes=[rb_])
                k.op("dve", lambda r_=r_, f=f: nc.vector.tensor_tensor(out=hid[:, f, :], in0=r_[:, :], in1=r_[:, :], op=ALU.mult),
                     reads=[rb_], writes=[hidb[f]])
            pre.back(gi + 1)
            for j in range(G):
                for half in range(2):
                    p_, pb_ = pm[pmi % 4], pmb[pmi % 4]
                    pmi += 1
                    for f in range(32):
                        k.op("pe", lambda p_=p_, f=f, j=j, half=half: nc.tensor.matmul(
                            p_[:, :], lhsT=hid[:, f, j * 128:(j + 1) * 128], rhs=w2[:, f, half * 512:(half + 1) * 512],
                            start=(f == 0), stop=(f == 31)), reads=[w2b[f // 8], hidb[f]], writes=[pb_])
                    k.op("dve", lambda p_=p_, j=j, half=half: nc.vector.tensor_tensor(
                        out=x_[:, j, half * 512:(half + 1) * 512], in0=p_[:, :], in1=x_[:, j, half * 512:(half + 1) * 512],
                        op=ALU.add), reads=[pb_, xb_], writes=[xb_])
            if not final:
                k.dma("sp", xb_, False, [(xv[:, t0:t0 + G, :], x_[:, :, :])])
            else:
                for j in range(G):
                    k.op("act", lambda j=j: nc.scalar.activation(out=junk[:, :], in_=x_[:, j, :], func=AF.Square,
                                                                 accum_out=ssq[:, j:j + 1]), reads=[xb_], writes=[junkb, ssqb])
                k.op("act", lambda: nc.scalar.activation(out=rstd[:, 0:G], in_=ssq[:, 0:G], func=AF.Sqrt,
                                                         bias=C.eps_col[:, 0:1], scale=1.0 / D), reads=[ssqb, C.b_const], writes=[ssqb])
                k.op("dve", lambda: nc.vector.reciprocal(out=rstd[:, 0:G], in_=rstd[:, 0:G]), reads=[ssqb], writes=[ssqb])
                for j in range(G):
                    k.op("dve", lambda j=j: nc.vector.scalar_tensor_tensor(
                        out=x_[:, j, :], in0=x_[:, j, :], scalar=rstd[:, j:j + 1], in1=gfin[:, :],
                        op0=ALU.mult, op1=ALU.mult), reads=[xb_, ssqb, gfb], writes=[xb_])
                ot0 = t0 - (NT - HALF // 128)
                k.dma("sp", xb_, False, [(C.out.rearrange("(t p) d -> p t d", p=128)[:, ot0:ot0 + G, :], x_[:, :, :])])
        k.barrier()
        k.release(w1b + w2b + [gain[1]] + pre.bufs() + hidb + rlb + tpsb + pmb + ([gfb] if final else []))


def drive(gens):
    gens = [g for g in gens if g is not None]
    held = []
    while gens or held:
        if not gens:
            gens, held = held, []
        for g in list(gens):
            try:
                if next(g) == "hold" and len(gens) > 1:
                    gens.remove(g)
                    held.append(g)
            except StopIteration:
                gens.remove(g)


TWO_PI = 2.0 * np.pi
CW1 = 6.28125
CW2 = TWO_PI - 6.28125


def phase_b0(k, C, t_lo, t_hi):
    nc = k.nc
    QS = NT - HALF // 128
    with ExitStack() as st:
        w = k.sb(st, "w_in1", [128, 8, 2240], BF16)
        wqn = k.sb(st, "wqn", [128, 3, 512], BF16)
        wqr = k.sb(st, "wqr", [128, 3, 256], BF16)
        wqt = k.sb(st, "wqt", [128, 3, 256], BF16)
        wuk = k.sb(st, "wuk", [128, 2, 512], BF16)
        wuv = k.sb(st, "wuv", [128, 2, 512], BF16)
        invf = k.sb(st, "invf", [128, 1], F32)
        wb = k.buf("wb1")
        wb0 = load_weight_bf(k, C, w[:, :, 0:2208], "cd_w_in")
        t0_ = []
        for kk in range(8):
            k.op("pool", lambda kk=kk: nc.gpsimd.tensor_scalar(out=w[:, kk, 2208:2224], in0=w[:, kk, 2192:2208], scalar1=-1.0,
                                                               scalar2=None, op0=ALU.mult), reads=[wb0], writes=[wb])
            k.op("pool", lambda kk=kk: nc.gpsimd.tensor_copy(out=w[:, kk, 2224:2240], in_=w[:, kk, 2176:2192]),
                 reads=[wb0], writes=[wb])
        wbk = load_weight_bf(k, C, wuk[:, :, :], "mla_w_uk")
        wbv = load_weight_bf(k, C, wuv[:, :, :], "mla_w_uv")
        t1_, t2_ = [], []
        stg = k.sb(st, "uqs", [128, 3, 768], BF16)
        sgb = load_weight_bf(k, C, stg[:, :, :], "mla_w_uq")
        qg = k.sb(st, "one3", [128, 3], F32)
        k.op("pool", lambda: nc.gpsimd.memset(qg[:, :], 1.0), writes=[sgb])
        for c in range(3):
            sv = stg[:, c, :].rearrange("p (h e) -> p h e", h=8)
            k.op("pool", lambda c=c, sv=sv: nc.gpsimd.tensor_scalar(
                out=wqn[:, c, :].rearrange("p (h e) -> p h e", h=8), in0=sv[:, :, 0:64], scalar1=qg[:, c:c + 1],
                scalar2=None, op0=ALU.mult), reads=[sgb, C.b_const], writes=[wb])
            k.op("pool", lambda c=c, sv=sv: nc.gpsimd.tensor_scalar(
                out=wqr[:, c, :].rearrange("p (h e) -> p h e", h=8), in0=sv[:, :, 64:96], scalar1=qg[:, c:c + 1],
                scalar2=None, op0=ALU.mult), reads=[sgb, C.b_const], writes=[wb])
            k.op("pool", lambda c=c, sv=sv: nc.gpsimd.tensor_scalar(
                out=wqt[:, c, :].rearrange("p (h e) -> p h e", h=8)[:, :, 0:16], in0=sv[:, :, 80:96], scalar1=qg[:, c:c + 1],
                scalar2=-1.0, op0=ALU.mult, op1=ALU.mult), reads=[sgb, C.b_const], writes=[wb])
            k.op("pool", lambda c=c, sv=sv: nc.gpsimd.tensor_scalar(
                out=wqt[:, c, :].rearrange("p (h e) -> p h e", h=8)[:, :, 16:32], in0=sv[:, :, 64:80], scalar1=qg[:, c:c + 1],
                scalar2=None, op0=ALU.mult), reads=[sgb, C.b_const], writes=[wb])
        pid = k.sb(st, "pid", [128, 1], I32)
        pidf = k.sb(st, "pidf", [128, 1], F32)
        pb_ = k.buf("pid")
        k.op("pool", lambda: nc.gpsimd.iota(pid[:, :], pattern=[[0, 1]], base=0, channel_multiplier=1), writes=[pb_])
        k.op("dve", lambda: nc.vector.tensor_single_scalar(out=pid[:, :], in_=pid[:, :], scalar=15, op=ALU.bitwise_and),
             reads=[pb_], writes=[pb_])
        k.op("dve", lambda: nc.vector.tensor_copy(out=pidf[:, :], in_=pid[:, :]), reads=[pb_], writes=[pb_])
        k.op("act", lambda: nc.scalar.activation(out=invf[:, :], in_=pidf[:, :], func=AF.Exp, scale=-np.log(10000.0) / 16.0),
             reads=[pb_], writes=[wb])
        k.op("pool", lambda: nc.gpsimd.memset(qg[:, :], 1.0), reads=[wb0, wbk, wbv, sgb], writes=[wb])
        b0tmp = [wb0, wbk, wbv, sgb, pb_]
        G = 4
        N = G * 128
        gain = load_gain(k, C, st, 1)
        tps = [k.ps(st, "tp", [128, 8, 128], BF16) for _ in range(2)]
        tpsb = k.bufs_n(2, "tp")
        pm = [k.ps(st, "pm", [128, 512], F32) for _ in range(6)]
        pmb = k.bufs_n(6, "pm")
        stq = [k.sb(st, "stq", [128, 4, N], BF16) for _ in range(2)]
        stqb = k.bufs_n(2, "stq")
        vst = [k.sb(st, "vst", [128, 512], BF16) for _ in range(2)]
        vstb = k.bufs_n(2, "vst")
        lat = k.sb(st, "lat", [128, 3, N], F32)
        latb = k.buf("lat")
        lsq = k.sb(st, "lsq", [128, 3, N], BF16)
        lsqb = k.buf("lsq")
        rs = k.sb(st, "rs", [128, N], F32)
        rsb = k.buf("rs")
        ckvn = k.sb(st, "ckvn", [128, 2, N], BF16)
        ckvnb = k.buf("ckvn")
        cqn = k.sb(st, "cqn", [128, 3, N], BF16)
        cqnb = k.buf("cqn")
        posi = k.sb(st, "posi", [128, N], I32)
        posb = k.buf("posi")
        ang = k.sb(st, "ang", [128, N], F32)
        kq = k.sb(st, "kq", [128, N], F32)
        kqi = k.sb(st, "kqi", [128, N], I32)
        tmpm = k.sb(st, "tmpm", [128, N], F32)
        angb = k.buf("ang")
        css = [k.sb(st, "cs", [128, 2, N], F32) for _ in range(2)]
        csbs = k.bufs_n(2, "cs")
        rt = [k.sb(st, "rt", [128, N], F32) for _ in range(2)]
        rtb = k.bufs_n(2, "rt")
        ro = [k.sb(st, "ro", [128, N], BF16) for _ in range(2)]
        rob = k.bufs_n(2, "ro")
        xv = C.XA.rearrange("(t p) d -> p t d", p=128)
        kctv = C.KCT.rearrange("(c p) n -> p c n", p=128)
        qctv = C.QCT.rearrange("(c m p) n -> p c m n", m=2, p=128)
        zq = [k.sb(st, "zq", [128, 4, 2, N], BF16) for _ in range(2)]
        zqb = k.bufs_n(2, "zq")
        for z_, zb_ in zip(zq, zqb):
            k.op("pool", lambda z_=z_: nc.gpsimd.memset(z_[:, :, :, :], 0.0), writes=[zb_])
        kdtv = C.KDT.rearrange("(h e) n -> h e n", e=96)
        qdtv = C.QDT.rearrange("(h e) n -> h e n", e=96)
        vcv = C.VC.rearrange("(t p) c -> p t c", p=128)
        vdv = C.VD.rearrange("(t p) c -> p t c", p=128)
        cnt = {"pm": 0, "stq": 0, "vst": 0, "rt": 0, "ro": 0}

        def nxt(name, arr, arrb):
            i = cnt[name]
            cnt[name] += 1
            return arr[i % len(arr)], arrb[i % len(arr)]

        def fm_proj(wt, wtb, kc, col0, M, rhsT, rhsb):
            p_, pb_ = nxt("pm", pm, pmb)
            for kk in range(kc):
                k.op("pe", lambda p_=p_, kk=kk: nc.tensor.matmul(
                    p_[0:M, 0:N], lhsT=wt[:, kk, col0:col0 + M], rhs=rhsT[:, kk, :], start=(kk == 0), stop=(kk == kc - 1)),
                    reads=[wtb, rhsb], writes=[pb_])
            return p_, pb_

        def latent_norm(p_list, nchunk, dim, dst, dstb, gcol):
            for c, (p_, pb_) in enumerate(p_list):
                k.op("act", lambda p_=p_, c=c: nc.scalar.copy(out=lat[:, c, :], in_=p_[:, 0:N]), reads=[pb_], writes=[latb])
                k.op("act", lambda p_=p_, c=c: nc.scalar.activation(out=lsq[:, c, :], in_=p_[:, 0:N], func=AF.Square),
                     reads=[pb_], writes=[lsqb])
            ps_, psb_ = nxt("pm", pm, pmb)
            for c in range(nchunk):
                k.op("pe", lambda ps_=ps_, c=c: nc.tensor.matmul(ps_[:, 0:N], lhsT=C.ones_b[:, :], rhs=lsq[:, c, :],
                                                                 start=(c == 0), stop=(c == nchunk - 1)),
                     reads=[lsqb, C.b_const], writes=[psb_])
            k.op("act", lambda ps_=ps_: nc.scalar.activation(out=rs[:, :], in_=ps_[:, 0:N], func=AF.Sqrt, bias=C.eps_col[:, 0:1],
                                                             scale=1.0 / dim), reads=[psb_, C.b_const], writes=[rsb])
            k.op("dve", lambda: nc.vector.reciprocal(out=rs[:, :], in_=rs[:, :]), reads=[rsb], writes=[rsb])
            for c in range(nchunk):
                k.op("dve", lambda c=c: nc.vector.scalar_tensor_tensor(out=dst[:, c, :], in0=lat[:, c, :], scalar=gcol[:, c:c + 1],
                                                                       in1=rs[:, :], op0=ALU.mult, op1=ALU.mult),
                     reads=[latb, rsb, C.b_const], writes=[dstb])

        def sincos(tok0, idx):
            cs, csb = css[idx], csbs[idx]
            k.dma("sp", posb, True, [(posi[:, :], C.pos_d[:, tok0:tok0 + N].to_broadcast([128, N]))])
            for which in range(2):
                yield
                k.op("dve", lambda: nc.vector.tensor_copy(out=ang[:, :], in_=posi[:, :]), reads=[posb], writes=[angb])
                k.op("dve", lambda which=which: nc.vector.tensor_scalar(
                    out=ang[:, :], in0=ang[:, :], scalar1=invf[:, 0:1], scalar2=(0.5 * np.pi if which == 1 else 0.0),
                    op0=ALU.mult, op1=ALU.add), reads=[angb, wb], writes=[angb])
                k.op("dve", lambda: nc.vector.tensor_scalar(out=kq[:, :], in0=ang[:, :], scalar1=1.0 / TWO_PI, scalar2=None,
                                                            op0=ALU.mult), reads=[angb], writes=[angb])
                k.op("dve", lambda: nc.vector.tensor_copy(out=kqi[:, :], in_=kq[:, :]), reads=[angb], writes=[angb])
                k.op("dve", lambda: nc.vector.tensor_copy(out=kq[:, :], in_=kqi[:, :]), reads=[angb], writes=[angb])
                yield
                k.op("dve", lambda: nc.vector.scalar_tensor_tensor(out=ang[:, :], in0=kq[:, :], scalar=-CW1, in1=ang[:, :],
                                                                   op0=ALU.mult, op1=ALU.add), reads=[angb], writes=[angb])
                k.op("dve", lambda: nc.vector.scalar_tensor_tensor(out=ang[:, :], in0=kq[:, :], scalar=-CW2, in1=ang[:, :],
                                                                   op0=ALU.mult, op1=ALU.add), reads=[angb], writes=[angb])
                k.op("dve", lambda: nc.vector.tensor_scalar(out=tmpm[:, :], in0=ang[:, :], scalar1=float(np.pi), scalar2=-TWO_PI,
                                                            op0=ALU.is_gt, op1=ALU.mult), reads=[angb], writes=[angb])
                k.op("dve", lambda: nc.vector.tensor_tensor(out=ang[:, :], in0=ang[:, :], in1=tmpm[:, :], op=ALU.add),
                     reads=[angb], writes=[angb])
                yield
                k.op("dve", lambda: nc.vector.tensor_scalar(out=tmpm[:, :], in0=ang[:, :], scalar1=-float(np.pi), scalar2=TWO_PI,
                                                            op0=ALU.is_lt, op1=ALU.mult), reads=[angb], writes=[angb])
                k.op("dve", lambda: nc.vector.tensor_tensor(out=ang[:, :], in0=ang[:, :], in1=tmpm[:, :], op=ALU.add),
                     reads=[angb], writes=[angb])
                k.op("dve", lambda: nc.vector.tensor_scalar(out=ang[:, :], in0=ang[:, :], scalar1=3.141592, scalar2=-3.141592,
                                                            op0=ALU.min, op1=ALU.max), reads=[angb], writes=[angb])
                k.op("act", lambda which=which: nc.scalar.activation(out=cs[:, which, :], in_=ang[:, :], func=AF.Sin),
                     reads=[angb], writes=[csb])

        def rope_combine(pr, prb, pt_, ptb_, M, scale):
            cs, csb = css[cur["i"]], csbs[cur["i"]]
            r1, r1b = nxt("rt", rt, rtb)
            r2, r2b = nxt("rt", rt, rtb)
            o_, ob_ = nxt("ro", ro, rob)
            k.op("dve", lambda: nc.vector.tensor_tensor(out=r1[0:M, :], in0=pr[0:M, 0:N], in1=cs[0:M, 1, :], op=ALU.mult),
                 reads=[prb, csb], writes=[r1b])
            k.op("dve", lambda: nc.vector.tensor_tensor(out=r2[0:M, :], in0=pt_[0:M, 0:N], in1=cs[0:M, 0, :], op=ALU.mult),
                 reads=[ptb_, csb], writes=[r2b])
            if scale == 1.0:
                k.op("dve", lambda: nc.vector.tensor_tensor(out=o_[0:M, :], in0=r1[0:M, :], in1=r2[0:M, :], op=ALU.add),
                     reads=[r1b, r2b], writes=[ob_])
            else:
                k.op("dve", lambda: nc.vector.tensor_tensor(out=r1[0:M, :], in0=r1[0:M, :], in1=r2[0:M, :], op=ALU.add),
                     reads=[r1b, r2b], writes=[r1b])
                k.op("dve", lambda: nc.vector.tensor_scalar(out=o_[0:M, :], in0=r1[0:M, :], scalar1=scale, scalar2=None,
                                                            op0=ALU.mult), reads=[r1b], writes=[ob_])
            return o_, ob_

        ngrp = (t_hi - t_lo) // G
        pre = PreStage(k, C, st, G, xv, gain, tps, tpsb, t_lo, ngrp)
        pre.start()
        cur = {"i": 0}
        drive([sincos(t_lo * 128, 0)])
        for gi in range(ngrp):
            t0 = t_lo + gi * G
            tok0 = t0 * 128
            own = t0 >= QS
            qtok0 = tok0 - QS * 128
            pre.front(gi + 1)
            x_, xb_, hT_, hTb_ = pre.get(gi)
            cur["i"] = gi % 2

            def chainC():
                s_, sb_ = nxt("stq", stq, stqb)
                for c in range(4):
                    p_, pb_ = fm_proj(w, wb, 8, 512 + c * 128, 128, hT_, hTb_)
                    k.op("act", lambda p_=p_, s_=s_, c=c: nc.scalar.copy(out=s_[:, c, :], in_=p_[:, 0:N]), reads=[pb_], writes=[sb_])
                k.dma("sp", sb_, False, [(kctv[:, :, tok0:tok0 + N], s_[:, :, :])])
                yield
                if own:
                    z_, zb_ = zq[gi % 2], zqb[gi % 2]
                    for c in range(4):
                        p_, pb_ = fm_proj(w, wb, 8, c * 128, 128, hT_, hTb_)
                        k.op("act", lambda p_=p_, z_=z_, c=c: nc.scalar.activation(
                            out=z_[0:64, c, 0, :], in_=p_[0:64, 0:N], func=AF.Copy, scale=0.125), reads=[pb_], writes=[zb_])
                        k.op("act", lambda p_=p_, z_=z_, c=c: nc.scalar.activation(
                            out=z_[64:128, c, 1, :], in_=p_[64:128, 0:N], func=AF.Copy, scale=0.125), reads=[pb_], writes=[zb_])
                    k.dma("sp", zb_, False, [(qctv[:, :, :, qtok0:qtok0 + N], z_[:, :, :, :])])
                    yield
                for j in range(G):
                    p_, pb_ = nxt("pm", pm, pmb)
                    for kk in range(8):
                        k.op("pe", lambda p_=p_, kk=kk, j=j: nc.tensor.matmul(
                            p_[:, :], lhsT=hT_[:, kk, j * 128:(j + 1) * 128], rhs=w[:, kk, 1024:1536],
                            start=(kk == 0), stop=(kk == 7)), reads=[wb, hTb_], writes=[pb_])
                    v_, vb_ = nxt("vst", vst, vstb)
                    t = t0 + j
                    k.op("act", lambda p_=p_, v_=v_, t=t: nc.scalar.activation(out=v_[:, :], in_=p_[:, :], func=AF.Copy,
                                                                               scale=C.valid[:, t:t + 1]),
                         reads=[pb_, C.b_const], writes=[vb_])
                    k.dma("sp", vb_, False, [(vcv[:, t, :], v_[:, :])])
                    yield
                pre.back(gi + 1)
                yield

            def chainD():
                pl = [fm_proj(w, wb, 8, 1920 + c * 128, 128, hT_, hTb_) for c in range(2)]
                latent_norm(pl, 2, 256, ckvn, ckvnb, C.col("kv_norm_g"))
                for hp in range(4):
                    p_, pb_ = fm_proj(wuk, wb, 2, hp * 128, 128, ckvn, ckvnb)
                    s_, sb_ = nxt("vst", vst, vstb)
                    k.op("act", lambda p_=p_, s_=s_: nc.scalar.copy(out=s_[:, 0:N], in_=p_[:, 0:N]), reads=[pb_], writes=[sb_])
                    k.dma("sp", sb_, False, [(kdtv[2 * hp, 0:64, tok0:tok0 + N], s_[0:64, 0:N]),
                                             (kdtv[2 * hp + 1, 0:64, tok0:tok0 + N], s_[64:128, 0:N])])
                    yield
                for j in range(G):
                    p_, pb_ = nxt("pm", pm, pmb)
                    for c in range(2):
                        k.op("pe", lambda p_=p_, c=c, j=j: nc.tensor.matmul(
                            p_[:, :], lhsT=ckvn[:, c, j * 128:(j + 1) * 128], rhs=wuv[:, c, :], start=(c == 0), stop=(c == 1)),
                            reads=[wb, ckvnb], writes=[pb_])
                    v_, vb_ = nxt("vst", vst, vstb)
                    t = t0 + j
                    k.op("act", lambda p_=p_, v_=v_, t=t: nc.scalar.activation(out=v_[:, :], in_=p_[:, :], func=AF.Copy,
                                                                               scale=C.valid[:, t:t + 1]),
                         reads=[pb_, C.b_const], writes=[vb_])
                    k.dma("sp", vb_, False, [(vdv[:, t, :], v_[:, :])])
                    yield
                pr, prb = fm_proj(w, wb, 8, 2176, 32, hT_, hTb_)
                pt_, ptb_ = fm_proj(w, wb, 8, 2208, 32, hT_, hTb_)
                o_, ob_ = rope_combine(pr, prb, pt_, ptb_, 32, 1.0)
                k.dma("sp", ob_, False, [(kdtv[h, 64:96, tok0:tok0 + N], o_[0:32, :]) for h in range(8)])
                yield
                if own:
                    sc = 96.0 ** -0.5
                    pl = [fm_proj(w, wb, 8, 1536 + c * 128, 128, hT_, hTb_) for c in range(3)]
                    latent_norm(pl, 3, 384, cqn, cqnb, C.col("q_norm_g"))
                    for hp in range(4):
                        p_, pb_ = fm_proj(wqn, wb, 3, hp * 128, 128, cqn, cqnb)
                        s_, sb_ = nxt("vst", vst, vstb)
                        k.op("act", lambda p_=p_, s_=s_: nc.scalar.activation(out=s_[:, 0:N], in_=p_[:, 0:N], func=AF.Copy, scale=sc),
                             reads=[pb_], writes=[sb_])
                        k.dma("sp", sb_, False, [(qdtv[2 * hp, 0:64, qtok0:qtok0 + N], s_[0:64, 0:N]),
                                                 (qdtv[2 * hp + 1, 0:64, qtok0:qtok0 + N], s_[64:128, 0:N])])
                        yield
                    for rc in range(2):
                        pr, prb = fm_proj(wqr, wb, 3, rc * 128, 128, cqn, cqnb)
                        pt_, ptb_ = fm_proj(wqt, wb, 3, rc * 128, 128, cqn, cqnb)
                        o_, ob_ = rope_combine(pr, prb, pt_, ptb_, 128, sc)
                        k.dma("sp", ob_, False, [(qdtv[rc * 4 + hl, 64:96, qtok0:qtok0 + N], o_[hl * 32:(hl + 1) * 32, :])
                                                 for hl in range(4)])
                        yield
                yield

            drive([chainC(), chainD(), sincos(tok0 + N, (gi + 1) % 2) if gi + 1 < ngrp else None])
        k.barrier()
        k.release(b0tmp + [wb, gain[1]] + pre.bufs() + tpsb + pmb + stqb + vstb + [latb, lsqb, rsb, ckvnb, cqnb, posb, angb] + csbs
                  + rtb + rob + zqb)


LAM_INIT = 0.8 - 0.6 * float(np.exp(-0.3 * 1))


def phase_b1(k, C):
    nc = k.nc
    QS = NT - HALF // 128
    with ExitStack() as st:
        ovalid = k.sb(st, "ovalid", [128, NT, 128], BF16)
        lamc = k.sb(st, "lamc", [128, 4], F32)
        cb = k.buf("b1const")
        k.op("dve", lambda: nc.vector.tensor_copy(out=ovalid[:, :, :], in_=C.valid.unsqueeze(2).to_broadcast([128, NT, 128])),
             reads=[C.b_const], writes=[cb])
        lm = C.col("lam")
        pq = k.ps(st, "pq", [128, 512], F32)
        pqb = k.buf("pq")
        k.op("dve", lambda: nc.vector.tensor_tensor(out=lamc[:, 0:1], in0=lm[:, 0:1], in1=lm[:, 1:2], op=ALU.mult),
             reads=[C.b_const], writes=[cb])
        k.op("dve", lambda: nc.vector.tensor_tensor(out=lamc[:, 1:2], in0=lm[:, 2:3], in1=lm[:, 3:4], op=ALU.mult),
             reads=[C.b_const], writes=[cb])
        k.op("pe", lambda: nc.tensor.matmul(pq[:, 0:2], lhsT=C.ones_f[:, :], rhs=lamc[:, 0:2], start=True, stop=True),
             reads=[cb, C.b_const], writes=[pqb])
        k.op("act", lambda: nc.scalar.activation(out=lamc[:, 2:4], in_=pq[:, 0:2], func=AF.Exp), reads=[pqb], writes=[cb])
        k.op("dve", lambda: nc.vector.tensor_tensor(out=lamc[:, 0:1], in0=lamc[:, 3:4], in1=lamc[:, 2:3], op=ALU.subtract),
             reads=[cb], writes=[cb])
        k.op("dve", lambda: nc.vector.tensor_scalar(out=lamc[:, 0:1], in0=lamc[:, 0:1], scalar1=-LAM_INIT, scalar2=None, op0=ALU.add),
             reads=[cb], writes=[cb])
        neglam = lamc[:, 0:1]
        sgc = C.col("subln_g")
        KTu = [k.sb(st, "KTu", [128, SEQ], BF16) for _ in range(2)]
        QTu = [k.sb(st, "QTu", [128, 2, HALF], BF16) for _ in range(2)]
        Vu = [k.sb(st, "Vu", [128, NT, 128], BF16) for _ in range(2)]
        ub = k.bufs_n(2, "unit")
        pss = [k.ps(st, "pss", [128, 512], F32) for _ in range(3)]
        pssb = k.bufs_n(3, "pss")
        acc = [k.ps(st, "acc", [128, 512], F32) for _ in range(2)]
        accb = k.bufs_n(2, "acc")
        den = [k.ps(st, "den", [128, 512], F32) for _ in range(2)]
        denb = k.bufs_n(2, "den")
        pt = [k.sb(st, "pt", [128, 512], BF16) for _ in range(4)]
        ptb = k.bufs_n(4, "pt")
        rd = [k.sb(st, "rd", [128, 512], F32) for _ in range(2)]
        rdb = k.bufs_n(2, "rd")
        an = [k.sb(st, "an", [128, 512], F32) for _ in range(2)]
        anb = k.bufs_n(2, "an")
        sq = k.sb(st, "sqd", [128, 512], F32)
        sqb = k.buf("sqd")
        yo = [k.sb(st, "yo", [128, 512], BF16) for _ in range(2)]
        yob = k.bufs_n(2, "yo")
        vcv = C.VC.rearrange("(t p) c -> p t c", p=128)
        vdv = C.VD.rearrange("(t p) c -> p t c", p=128)
        units = [("c", h) for h in range(4)] + [("d", h) for h in range(8)]

        def load_unit(ui):
            kind, h = units[ui]
            kt_, qt_, v_, b_ = KTu[ui % 2], QTu[ui % 2], Vu[ui % 2], ub[ui % 2]
            pairs = []
            if kind == "c":
                pairs.append((kt_[:, :], C.KCT[h * 128:(h + 1) * 128, :]))
                pairs.append((qt_[:, :, :], C.QCT.rearrange("(c m p) n -> p c m n", m=2, p=128)[:, h, :, :]))
                for t8 in range(0, NT, 8):
                    pairs.append((v_[:, t8:t8 + 8, :], vcv[:, t8:t8 + 8, h * 128:(h + 1) * 128]))
            else:
                pairs.append((kt_[0:96, :], C.KDT[h * 96:(h + 1) * 96, :]))
                pairs.append((qt_[0:96, 0, :], C.QDT[h * 96:(h + 1) * 96, :]))
                for t8 in range(0, NT, 8):
                    pairs.append((v_[:, t8:t8 + 8, 0:64], vdv[:, t8:t8 + 8, h * 64:(h + 1) * 64]))
            k.dma("sp", b_, True, pairs)
            if kind == "d":
                k.op("pool", lambda: nc.gpsimd.tensor_copy(out=v_[:, :, 64:128], in_=ovalid[:, :, 0:64]), reads=[cb], writes=[b_])

        cnt = {"s": 0, "p": 0, "y": 0}
        load_unit(0)
        for ui, (kind, h) in enumerate(units):
            if ui + 1 < len(units):
                load_unit(ui + 1)
            kt_, qt_, v_, b_ = KTu[ui % 2], QTu[ui % 2], Vu[ui % 2], ub[ui % 2]
            nmap = 2 if kind == "c" else 1
            kd = 128 if kind == "c" else 96
            dv = 128
            for G in range(HALF // 512):
                C.bg.step(1)
                q0 = G * 512
                kdiag = QS + 4 * G
                kmax = kdiag + 3
                steps = [(kt, m) for kt in range(kmax + 1) for m in range(nmap)]

                def qk(step):
                    kt, m = step
                    j0 = max(0, kt - kdiag)
                    ncol = (4 - j0) * 128
                    s_, sb_ = pss[cnt["s"] % 3], pssb[cnt["s"] % 3]
                    cnt["s"] += 1
                    k.op("pe", lambda: nc.tensor.matmul(
                        s_[:, 0:ncol], lhsT=kt_[0:kd, kt * 128:(kt + 1) * 128],
                        rhs=qt_[0:kd, m, q0 + j0 * 128:q0 + 512], start=True, stop=True), reads=[b_], writes=[sb_])
                    return s_, sb_, j0, ncol

                def rest(step, qkres):
                    kt, m = step
                    s_, sb_, j0, ncol = qkres
                    p_, pb_ = pt[cnt["p"] % 4], ptb[cnt["p"] % 4]
                    cnt["p"] += 1
                    k.op("act", lambda: nc.scalar.activation(out=p_[:, 0:ncol], in_=s_[:, 0:ncol], func=AF.Exp),
                         reads=[sb_], writes=[pb_])
                    if kt >= kdiag:
                        k.op("dve", lambda: nc.vector.tensor_tensor(out=p_[:, 0:128], in0=p_[:, 0:128], in1=C.mask_le[:, :],
                                                                    op=ALU.mult), reads=[C.b_const], writes=[pb_])
                    k.op("pe", lambda: nc.tensor.matmul(acc[m][0:dv, j0 * 128:512], lhsT=v_[:, kt, 0:dv], rhs=p_[:, 0:ncol],
                                                        start=(kt == 0), stop=(kt == kmax)), reads=[b_, pb_], writes=[accb[m]],
                         late=pb_)
                    if kind == "c":
                        k.op("pe", lambda: nc.tensor.matmul(den[m][0:dv, j0 * 128:512], lhsT=ovalid[:, kt, 0:dv], rhs=p_[:, 0:ncol],
                                                            start=(kt == 0), stop=(kt == kmax)), reads=[cb, pb_], writes=[denb[m]])

                LA = 2
                pend = [qk(steps[i]) for i in range(min(LA, len(steps)))]
                for si, step in enumerate(steps):
                    if si + LA < len(steps):
                        pend.append(qk(steps[si + LA]))
                    rest(step, pend.pop(0))
                y_, yb_ = yo[cnt["y"] % 2], yob[cnt["y"] % 2]
                cnt["y"] += 1
                if kind == "d":
                    k.op("dve", lambda: nc.vector.tensor_copy(out=an[0][:, :], in_=acc[0][:, :]), reads=[accb[0]], writes=[anb[0]])
                    k.op("pe", lambda: nc.tensor.matmul(den[0][0:64, :], lhsT=C.ident_f[:, 64:128], rhs=an[0][:, :], start=True, stop=True),
                         reads=[anb[0], C.b_const], writes=[denb[0]])
                    k.op("dve", lambda: nc.vector.tensor_scalar(out=rd[0][0:64, :], in0=den[0][0:64, :], scalar1=1e-30, scalar2=None,
                                                                op0=ALU.max), reads=[denb[0]], writes=[rdb[0]])
                    k.op("dve", lambda: nc.vector.reciprocal(out=rd[0][0:64, :], in_=rd[0][0:64, :]), reads=[rdb[0]], writes=[rdb[0]])
                    k.op("dve", lambda: nc.vector.tensor_tensor(out=y_[0:64, :], in0=an[0][0:64, :], in1=rd[0][0:64, :], op=ALU.mult),
                         reads=[anb[0], rdb[0]], writes=[yb_])
                    k.dma("sp", yb_, False, [(C.YT1[512 + h * 64:512 + (h + 1) * 64, q0:q0 + 512], y_[0:64, :])])
                else:
                    for m in range(nmap):
                        k.op("dve", lambda m=m: nc.vector.tensor_scalar(out=rd[m][0:dv, :], in0=den[m][0:dv, :], scalar1=1e-30, scalar2=None,
                                                                        op0=ALU.max), reads=[denb[m]], writes=[rdb[m]])
                        k.op("dve", lambda m=m: nc.vector.reciprocal(out=rd[m][0:dv, :], in_=rd[m][0:dv, :]), reads=[rdb[m]], writes=[rdb[m]])
                    for m in range(2):
                        k.op("dve", lambda m=m: nc.vector.tensor_tensor(out=an[m][:, :], in0=acc[m][:, :], in1=rd[m][:, :], op=ALU.mult),
                             reads=[accb[m], rdb[m]], writes=[anb[m]])
                    k.op("dve", lambda: nc.vector.scalar_tensor_tensor(out=an[0][:, :], in0=an[1][:, :], scalar=neglam, in1=an[0][:, :],
                                                                       op0=ALU.mult, op1=ALU.add), reads=[anb[0], anb[1], cb], writes=[anb[0]])
                    k.op("dve", lambda: nc.vector.tensor_tensor(out=sq[:, :], in0=an[0][:, :], in1=an[0][:, :], op=ALU.mult),
                         reads=[anb[0]], writes=[sqb])
                    k.op("pe", lambda: nc.tensor.matmul(pq[:, :], lhsT=C.ones_f[:, :], rhs=sq[:, :], start=True, stop=True),
                         reads=[sqb, C.b_const], writes=[pqb])
                    k.op("act", lambda: nc.scalar.activation(out=sq[:, :], in_=pq[:, :], func=AF.Sqrt, bias=C.eps_col[:, 0:1],
                                                             scale=1.0 / 128), reads=[pqb, C.b_const], writes=[sqb])
                    k.op("dve", lambda: nc.vector.reciprocal(out=sq[:, :], in_=sq[:, :]), reads=[sqb], writes=[sqb])
                    k.op("dve", lambda: nc.vector.tensor_tensor(out=an[0][:, :], in0=an[0][:, :], in1=sq[:, :], op=ALU.mult),
                         reads=[anb[0], sqb], writes=[anb[0]])
                    k.op("dve", lambda: nc.vector.tensor_scalar(out=y_[:, :], in0=an[0][:, :], scalar1=sgc[:, 0:1], scalar2=1.0 - LAM_INIT,
                                                                op0=ALU.mult, op1=ALU.mult), reads=[anb[0], C.b_const], writes=[yb_])
                    k.dma("sp", yb_, False, [(C.YT1[h * 128:(h + 1) * 128, q0:q0 + 512], y_[:, :])])
        k.barrier()
        k.release([cb, pqb] + ub + pssb + accb + denb + ptb + rdb + anb + [sqb] + yob)


ALL_PHASES = ("a0", "a0b", "a1", "a2", "a3", "b0", "b1", "b2", "b3")
_NC_CACHE = {}


def kernel(**inputs):
    if "nc" not in _NC_CACHE:
        _NC_CACHE["nc"] = build_program(ALL_PHASES)
    nc = _NC_CACHE["nc"]
    in_maps = [core_inputs(inputs, c) for c in range(8)]
    res = run_bass_kernel_spmd(nc, in_maps, core_ids=list(range(8)))
    out = np.empty((4, SEQ, D), np.float32)
    for c in range(8):
        b, h = c // 2, c % 2
        out[b, h * HALF:(h + 1) * HALF] = np.asarray(res.results[c]["out"], np.float32)
    return out
```

```python
import numpy as np
from contextlib import ExitStack
import concourse.bass as bass
import concourse.mybir as mybir
from concourse.bass_utils import run_bass_kernel_spmd

F32 = mybir.dt.float32
BF16 = mybir.dt.bfloat16
I32 = mybir.dt.int32
AF = mybir.ActivationFunctionType
ALU = mybir.AluOpType

D = 1024
SEQ = 8192
HALF = 4096
NT = 64
EPS = 1e-6
SEM_CH = 30000


class Ev:
    __slots__ = ("sem", "val", "eng")

    def __init__(self, sem=None, val=None, eng=None):
        self.sem, self.val, self.eng = sem, val, eng


class Buf:
    __slots__ = ("name", "w", "r", "dsem", "dcnt", "k")

    def __init__(self, k, name):
        self.k, self.name = k, name
        self.w = None
        self.r = []
        self.dsem = None
        self.dcnt = 0


class Eng:
    def __init__(self, k, name, eng, compute=True):
        self.k, self.name, self.eng = k, name, eng
        self.cnt = 0
        self.sems = []
        self.seen = {}
        self.last = None
        self.last_wkey = None
        self.pending = []

    def flush(self):
        if self.last is None:
            return
        ci = self.cnt // SEM_CH
        while len(self.sems) <= ci:
            self.sems.append(self.k.new_sem("t_%s%d" % (self.name, len(self.sems))))
        sem = self.sems[ci]
        val = self.cnt % SEM_CH + 1
        self.cnt += 1
        self.last.then_inc(sem, 1)
        for ev in self.pending:
            ev.sem, ev.val = sem, val
        self.last = None
        self.pending = []

    def wait(self, ev):
        if ev is None:
            return
        if ev.sem is None:
            ev.eng.flush()
        if self.seen.get(ev.sem, 0) >= ev.val:
            return
        self.eng.wait_ge(ev.sem, ev.val)
        self.seen[ev.sem] = ev.val


class K:
    def __init__(self, nc):
        self.nc = nc
        self.es = ExitStack()
        self.E = {
            "pe": Eng(self, "pe", nc.tensor),
            "act": Eng(self, "act", nc.scalar),
            "dve": Eng(self, "dve", nc.vector),
            "pool": Eng(self, "pool", nc.gpsimd),
            "sp": Eng(self, "sp", nc.sync),
        }
        self.nsem = 0
        self.dsem_pool = []
        self.bufs = []
        self.bar_sem = self.new_sem("bar")
        self.bar_cnt = 0
        self.uid = 0

    def new_sem(self, name):
        self.nsem += 1
        return self.es.enter_context(self.nc.semaphore("%s_%d" % (name, self.nsem)))

    def buf(self, name="b"):
        b = Buf(self, name)
        self.bufs.append(b)
        return b

    def bufs_n(self, n, name="b"):
        return [self.buf("%s%d" % (name, i)) for i in range(n)]

    def _dsem(self, b):
        if b.dsem is None:
            if self.dsem_pool:
                b.dsem, b.dcnt = self.dsem_pool.pop()
            else:
                b.dsem, b.dcnt = self.new_sem("d"), 0
        return b.dsem

    def release(self, blist):
        for b in blist:
            if b.dsem is not None:
                self.dsem_pool.append((b.dsem, b.dcnt))
                b.dsem = None
            if b in self.bufs:
                self.bufs.remove(b)

    def sb(self, st, name, shape, dtype):
        self.uid += 1
        return st.enter_context(self.nc.sbuf_tensor("%s_%d" % (name, self.uid), list(shape), dtype))

    def ps(self, st, name, shape, dtype=F32):
        self.uid += 1
        return st.enter_context(self.nc.psum_tensor("%s_%d" % (name, self.uid), list(shape), dtype))

    def op(self, en, fn, reads=(), writes=(), late=None):
        e = self.E[en]
        same_ok = (en == "pe")
        late_ev = None
        for b in reads:
            if b.w is not None and not (same_ok and b.w.eng is e):
                if b is late:
                    ev = b.w
                    if ev.sem is None:
                        ev.eng.flush()
                    if e.seen.get(ev.sem, 0) < ev.val:
                        late_ev = ev
                    continue
                e.wait(b.w)
        for b in writes:
            if b.w is not None and not (same_ok and b.w.eng is e):
                e.wait(b.w)
            for ev in b.r:
                if not (same_ok and ev.eng is e):
                    e.wait(ev)
        wkey = tuple(id(b) for b in writes)
        if e.last is not None and e.last_wkey != wkey:
            e.flush()
        ins = fn()
        if late_ev is not None:
            ins._wait_ge(late_ev.sem, late_ev.val)
            e.seen[late_ev.sem] = late_ev.val
        ev = Ev(eng=e)
        e.pending.append(ev)
        e.last = ins
        e.last_wkey = wkey
        for b in writes:
            b.w = ev
            b.r = []
        for b in reads:
            if b not in writes:
                b.r = [x for x in b.r if x.eng is not e] + [ev]
        return ins

    def dma(self, qn, sbuf_buf, write_sbuf, pairs, extra_reads=(), extra_writes=()):
        q = self.E[qn]
        b = sbuf_buf
        q.wait(b.w)
        if write_sbuf:
            for ev in b.r:
                q.wait(ev)
        for x in extra_reads:
            q.wait(x.w)
        for x in extra_writes:
            q.wait(x.w)
            for ev in x.r:
                q.wait(ev)
        sem = self._dsem(b)
        for (o, i) in pairs:
            b.dcnt += 1
            q.eng.dma_start(out=o, in_=i).then_inc(sem, 16)
        assert b.dcnt * 16 < 60000, "dma sem overflow %s" % b.name
        ev = Ev(sem, 16 * b.dcnt, None)
        if write_sbuf:
            b.w = ev
            b.r = []
        else:
            b.r = [x for x in b.r if x.sem is not sem] + [ev]
        for x in extra_writes:
            x.w = ev
            x.r = []
        for x in extra_reads:
            x.r = x.r + [ev]

    def barrier(self):
        sp = self.E["sp"]
        for e in self.E.values():
            e.flush()
        for e in self.E.values():
            if e is sp:
                continue
            for i, s in enumerate(e.sems):
                last = e.cnt - i * SEM_CH
                v = min(last, SEM_CH)
                if v > 0 and sp.seen.get(s, 0) < v:
                    sp.eng.wait_ge(s, v)
                    sp.seen[s] = v
        for b in self.bufs:
            if b.dsem is not None and b.dcnt > 0 and sp.seen.get(b.dsem, 0) < 16 * b.dcnt:
                sp.eng.wait_ge(b.dsem, 16 * b.dcnt)
                sp.seen[b.dsem] = 16 * b.dcnt
        self.bar_cnt += 1
        sp.eng.sem_inc(self.bar_sem, 1)
        for e in self.E.values():
            if e is sp:
                continue
            e.eng.wait_ge(self.bar_sem, self.bar_cnt)
            e.seen = dict(sp.seen)
        for b in self.bufs:
            b.w = None
            b.r = []


class Ctx:
    pass


def dram_in(nc, name, shape, dt=F32):
    return nc.dram_tensor(name, list(shape), dt, kind="ExternalInput").ap()


def build_consts(k, C, st):
    nc = k.nc
    C.b_const = k.buf("const")
    one_f = k.sb(st, "one_f", [128, 128], F32)
    C.ident_f = k.sb(st, "ident_f", [128, 128], F32)
    C.ident_b = k.sb(st, "ident_b", [128, 128], BF16)
    C.mask_le = k.sb(st, "mask_le", [128, 128], BF16)
    C.mask_ge = k.sb(st, "mask_ge", [128, 128], BF16)
    C.ones_b = k.sb(st, "ones_b", [128, 128], BF16)
    C.ones_f = k.sb(st, "ones_f", [128, 128], F32)
    b = C.b_const
    k.op("pool", lambda: nc.gpsimd.memset(one_f[:], 1.0), writes=[b])
    k.op("pool", lambda: nc.gpsimd.memset(C.ones_b[:], 1.0), writes=[b])
    k.op("pool", lambda: nc.gpsimd.memset(C.ones_f[:], 1.0), writes=[b])
    k.op("pool", lambda: nc.gpsimd.affine_select(out=C.ident_f[:], in_=one_f[:], pattern=[[1, 128]],
                                                 compare_op=ALU.is_equal, fill=0.0, base=0,
                                                 channel_multiplier=-1), reads=[b], writes=[b])
    k.op("pool", lambda: nc.gpsimd.tensor_copy(out=C.ident_b[:], in_=C.ident_f[:]), reads=[b], writes=[b])
    k.op("pool", lambda: nc.gpsimd.affine_select(out=C.mask_le[:], in_=one_f[:], pattern=[[1, 128]],
                                                 compare_op=ALU.is_ge, fill=0.0, base=0,
                                                 channel_multiplier=-1), reads=[b], writes=[b])
    k.op("pool", lambda: nc.gpsimd.affine_select(out=C.mask_ge[:], in_=one_f[:], pattern=[[-1, 128]],
                                                 compare_op=ALU.is_ge, fill=0.0, base=0,
                                                 channel_multiplier=1), reads=[b], writes=[b])


def load_weight(k, st, dst, w_dram, kc, n, scale_cols=None, neg_cols=None):
    nc = k.nc
    CH = 2048
    stg = [k.sb(st, "wstg", [128, CH], F32) for _ in range(4)]
    sb_ = k.bufs_n(4, "wstg")
    wb = k.buf("wdst")
    i = 0
    for c in range(kc):
        for n0 in range(0, n, CH):
            n1 = min(n, n0 + CH)
            s, sb = stg[i % 4], sb_[i % 4]
            k.dma("sp", sb, True, [(s[:, 0:n1 - n0], w_dram[c * 128:(c + 1) * 128, n0:n1])])
            if scale_cols is not None:
                en = ("pool", "dve", "act", "dve")[i % 4]
                if en == "act":
                    k.op("act", lambda s=s, c=c, n0=n0, n1=n1: nc.scalar.activation(
                        out=dst[:, c, n0:n1], in_=s[:, 0:n1 - n0], func=AF.Copy, scale=scale_cols[:, c:c + 1]),
                        reads=[sb], writes=[wb])
                else:
                    e_ = nc.gpsimd if en == "pool" else nc.vector
                    k.op(en, lambda s=s, c=c, n0=n0, n1=n1, e_=e_: e_.tensor_scalar(
                        out=dst[:, c, n0:n1], in0=s[:, 0:n1 - n0], scalar1=scale_cols[:, c:c + 1], scalar2=None,
                        op0=ALU.mult), reads=[sb], writes=[wb])
            else:
                k.op("pool", lambda s=s, c=c, n0=n0, n1=n1: nc.gpsimd.tensor_copy(
                    out=dst[:, c, n0:n1], in_=s[:, 0:n1 - n0]), reads=[sb], writes=[wb])
            i += 1
    return wb, sb_


class BG:
    def __init__(self, k, C, st):
        self.k, self.C = k, C
        self.wbuf = {}

    def add(self, name, kc, n, scale_cols=None):
        k, C = self.k, self.C
        b = k.buf("W_" + name)
        self.wbuf[name] = b
        src, dst = getattr(C, name), getattr(C, "W_" + name)
        pairs = []
        for n0 in range(0, n, 2048):
            n1 = min(n, n0 + 2048)
            for r0 in range(0, kc * 128, 1024):
                r1 = min(kc * 128, r0 + 1024)
                pairs.append((dst[r0:r1, n0:n1], src[r0:r1, n0:n1]))
        k.dma("pool", b, True, pairs)

    def step(self, n=1):
        pass

    def need(self, *names):
        pass


def load_weight_bf(k, C, dst, name, wb=None):
    C.bg.need(name)
    if wb is None:
        wb = k.buf("wdst")
    src = getattr(C, "W_" + name).rearrange("(c p) n -> p c n", p=128)
    k.dma("sp", wb, True, [(dst, src)], extra_reads=[C.bg.wbuf[name]])
    return wb


def load_gain(k, C, st, row):
    g = k.sb(st, "gbc", [128, D], F32)
    gb = k.buf("gbc")
    k.dma("sp", gb, True, [(g[:, :], C.gvec[row:row + 1, :].to_broadcast([128, D]))])
    return g, gb


def norm_rows(k, C, xt, xb, ntile, ssq, ssqb, rstd, junk, junkb, hb, hbb, dmodel=D, gain=None):
    nc = k.nc
    for j in range(ntile):
        k.op("act", lambda j=j: nc.scalar.activation(out=junk[:, :], in_=xt[:, j, :], func=AF.Square,
                                                     accum_out=ssq[:, j:j + 1]),
             reads=[xb], writes=[junkb, ssqb])
    k.op("act", lambda: nc.scalar.activation(out=rstd[:, 0:ntile], in_=ssq[:, 0:ntile], func=AF.Sqrt,
                                             bias=C.eps_col[:, 0:1], scale=1.0 / dmodel),
         reads=[ssqb, C.b_const], writes=[ssqb])
    k.op("dve", lambda: nc.vector.reciprocal(out=rstd[:, 0:ntile], in_=rstd[:, 0:ntile]),
         reads=[ssqb], writes=[ssqb])
    g, gb = gain
    for j in range(ntile):
        k.op("dve", lambda j=j: nc.vector.scalar_tensor_tensor(out=hb[:, j, :], in0=xt[:, j, :], scalar=rstd[:, j:j + 1],
                                                               in1=g[:, :], op0=ALU.mult, op1=ALU.mult),
             reads=[xb, ssqb, gb], writes=[hbb])


def transpose_to(k, C, src, srcb, ntile, nk, tps, tpsb, dstT, dstTb, col0=0, evac="act"):
    nc = k.nc
    for j in range(ntile):
        tp, tb = tps[j % len(tps)], tpsb[j % len(tps)]
        for c in range(nk):
            k.op("pe", lambda j=j, c=c, tp=tp: nc.tensor.transpose(
                out=tp[:, c, :], in_=src[:, j, c * 128:(c + 1) * 128], identity=C.ident_b[:]),
                reads=[srcb, C.b_const], writes=[tb])
        dst = dstT[:, 0:nk, col0 + j * 128:col0 + (j + 1) * 128]
        if evac == "act":
            k.op("act", lambda tp=tp, dst=dst: nc.scalar.copy(out=dst, in_=tp[:, 0:nk, :]),
                 reads=[tb], writes=[dstTb])
        else:
            k.op("dve", lambda tp=tp, dst=dst: nc.vector.tensor_copy(out=dst, in_=tp[:, 0:nk, :]),
                 reads=[tb], writes=[dstTb])


class PreStage:
    def __init__(self, k, C, st, G, xview, gain, tps, tpsb, t_lo, ngrp):
        self.k, self.C, self.G, self.xview, self.gain = k, C, G, xview, gain
        self.tps, self.tpsb, self.t_lo, self.ngrp = tps, tpsb, t_lo, ngrp
        self.xg = [k.sb(st, "xg", [128, G, D], F32) for _ in range(3)]
        self.xgb = k.bufs_n(3, "xg")
        self.hb = [k.sb(st, "hb", [128, G, D], BF16) for _ in range(2)]
        self.hbb = k.bufs_n(2, "hb")
        self.junk = k.sb(st, "junk", [128, D], BF16)
        self.junkb = k.buf("junk")
        self.ssq = [k.sb(st, "ssq", [128, 8], F32) for _ in range(2)]
        self.rstd = [k.sb(st, "rstd", [128, 8], F32) for _ in range(2)]
        self.ssqb = k.bufs_n(2, "ssq")
        self.hT = [k.sb(st, "hT", [128, 8, G * 128], BF16) for _ in range(2)]
        self.hTb = k.bufs_n(2, "hT")

    def bufs(self):
        return self.xgb + self.hbb + [self.junkb] + self.ssqb + self.hTb

    def load(self, gi):
        if gi >= self.ngrp:
            return
        G = self.G
        t0 = self.t_lo + gi * G
        self.k.dma("sp", self.xgb[gi % 3], True, [(self.xg[gi % 3][:, :, :], self.xview[:, t0:t0 + G, :])])

    def start(self):
        self.load(0)
        self.load(1)
        self.front(0, prefetch=False)
        self.back(0)

    def front(self, gi, prefetch=True):
        if prefetch:
            self.load(gi + 1)
        if gi >= self.ngrp:
            return
        k, i, G = self.k, gi % 2, self.G
        norm_rows(k, self.C, self.xg[gi % 3], self.xgb[gi % 3], G, self.ssq[i], self.ssqb[i], self.rstd[i], self.junk, self.junkb,
                  self.hb[i], self.hbb[i], gain=self.gain)

    def back(self, gi):
        if gi >= self.ngrp:
            return
        i = gi % 2
        transpose_to(self.k, self.C, self.hb[i], self.hbb[i], self.G, 8, self.tps, self.tpsb, self.hT[i], self.hTb[i])

    def get(self, gi):
        i = gi % 2
        return self.xg[gi % 3], self.xgb[gi % 3], self.hT[i], self.hTb[i]


def phase_a0(k, C, t_lo, t_hi):
    nc = k.nc
    with ExitStack() as st:
        w = k.sb(st, "w_in0", [128, 8, 2560], BF16)
        wb = k.buf("w_in0")
        k.dma("pool", wb, True, [(w[:, c, n0:n1], C.ab_w_in[c * 128:(c + 1) * 128, n0:n1])
                                 for c in range(8) for (n0, n1) in ((0, 2048), (2048, 2560))])
        gain = load_gain(k, C, st, 0)
        C.bg_register()
        C.build_diag()
        G = 4
        tps = [k.ps(st, "tp", [128, 8, 128], BF16) for _ in range(2)]
        tpsb = k.bufs_n(2, "tp")
        pm = [k.ps(st, "pm", [128, 512], F32) for _ in range(6)]
        pmb = k.bufs_n(6, "pm")
        stq = [k.sb(st, "stq", [128, 4, 512], BF16) for _ in range(2)]
        stqb = k.bufs_n(2, "stq")
        sig = [k.sb(st, "sig", [128, 512], F32) for _ in range(2)]
        sigb = k.bufs_n(2, "sig")
        glu = k.sb(st, "glu", [128, 4, 512], BF16)
        glub = k.buf("glu")
        va = [k.sb(st, "va", [128, 8, 65], BF16) for _ in range(2)]
        vab = k.bufs_n(2, "va")
        ngrp = (t_hi - t_lo) // G
        xv = C.XA.rearrange("(t p) d -> p t d", p=128)
        v0v = C.V0.rearrange("(t p) c -> p t c", p=128)
        pre = PreStage(k, C, st, G, xv, gain, tps, tpsb, t_lo, ngrp)
        pmi = 0
        stqi = 0
        pre.start()
        for gi in range(ngrp):
            t0 = t_lo + gi * G
            pre.front(gi + 1)
            x_, xb_, hT_, hTb_ = pre.get(gi)
            tok0 = t0 * 128
            for which, (col0, dst, scale) in enumerate(((0, C.QT0, 0.125), (512, C.KT0, 1.0))):
                s_, sb_ = stq[stqi % 2], stqb[stqi % 2]
                stqi += 1
                for c in range(4):
                    p_, pb_ = pm[pmi % 6], pmb[pmi % 6]
                    pmi += 1
                    for kk in range(8):
                        k.op("pe", lambda p_=p_, kk=kk, c=c, col0=col0: nc.tensor.matmul(
                            p_[:, :], lhsT=w[:, kk, col0 + c * 128:col0 + (c + 1) * 128], rhs=hT_[:, kk, :],
                            start=(kk == 0), stop=(kk == 7)), reads=[wb, hTb_], writes=[pb_])
                    k.op("act", lambda p_=p_, s_=s_, c=c, scale=scale: nc.scalar.activation(
                        out=s_[:, c, :], in_=p_[:, :], func=AF.Copy, scale=scale), reads=[pb_], writes=[sb_])
                k.dma("sp", sb_, False, [(dst.rearrange("(c p) n -> p c n", p=128)[:, :, tok0:tok0 + 512], s_[:, :, :])])
            pre.back(gi + 1)
            for c in range(4):
                pu, pub = pm[pmi % 6], pmb[pmi % 6]
                pmi += 1
                pg, pgb = pm[pmi % 6], pmb[pmi % 6]
                pmi += 1
                for kk in range(8):
                    k.op("pe", lambda pu=pu, kk=kk, c=c: nc.tensor.matmul(
                        pu[:, :], lhsT=w[:, kk, 1536 + c * 128:1536 + (c + 1) * 128], rhs=hT_[:, kk, :],
                        start=(kk == 0), stop=(kk == 7)), reads=[wb, hTb_], writes=[pub])
                for kk in range(8):
                    k.op("pe", lambda pg=pg, kk=kk, c=c: nc.tensor.matmul(
                        pg[:, :], lhsT=w[:, kk, 2048 + c * 128:2048 + (c + 1) * 128], rhs=hT_[:, kk, :],
                        start=(kk == 0), stop=(kk == 7)), reads=[wb, hTb_], writes=[pgb])
                sg, sgb = sig[c % 2], sigb[c % 2]
                k.op("act", lambda pg=pg, sg=sg: nc.scalar.activation(out=sg[:, :], in_=pg[:, :], func=AF.Sigmoid),
                     reads=[pgb], writes=[sgb])
                k.op("dve", lambda pu=pu, sg=sg, c=c: nc.vector.tensor_tensor(
                    out=glu[:, c, :], in0=pu[:, :], in1=sg[:, :], op=ALU.mult), reads=[pub, sgb], writes=[glub])
            k.dma("sp", glub, False, [(C.GT.rearrange("(c p) n -> p c n", p=128)[:, :, tok0:tok0 + 512], glu[:, :, :])])
            for j in range(G):
                p_, pb_ = pm[pmi % 6], pmb[pmi % 6]
                pmi += 1
                for kk in range(8):
                    k.op("pe", lambda p_=p_, kk=kk, j=j: nc.tensor.matmul(
                        p_[:, :], lhsT=hT_[:, kk, j * 128:(j + 1) * 128], rhs=w[:, kk, 1024:1536],
                        start=(kk == 0), stop=(kk == 7)), reads=[wb, hTb_], writes=[pb_])
                v_, vb_ = va[j % 2], vab[j % 2]
                t = t0 + j
                k.op("act", lambda p_=p_, v_=v_, t=t: nc.scalar.activation(
                    out=v_[:, :, 0:64], in_=p_[:, :].rearrange("p (h d) -> p h d", h=8), func=AF.Copy,
                    scale=C.valid[:, t:t + 1]), reads=[pb_, C.b_const], writes=[vb_])
                k.op("dve", lambda v_=v_, t=t: nc.vector.tensor_copy(
                    out=v_[:, :, 64:65], in_=C.valid[:, t:t + 1].unsqueeze(2).to_broadcast([128, 8, 1])),
                    reads=[C.b_const], writes=[vb_])
                k.dma("sp", vb_, False, [(v0v[:, t, :], v_[:, :, :].rearrange("p h d -> p (h d)"))])
        k.barrier()
        k.release(pre.bufs() + tpsb + pmb + stqb + sigb + [glub] + vab + [wb, gain[1]])


COLS = {}
_off = 0
for _name, _n in (("g_mix0", 8), ("g_mix1", 8), ("g_cross0", 8), ("g_cross1", 8), ("g_mem0", 8), ("g_mem1", 8),
                  ("g_mlp0", 8), ("g_mlp1", 8), ("conv_w", 4 * 31), ("conv_b", 4), ("ln_g", 4), ("ln_b", 4),
                  ("subln_g", 1), ("q_norm_g", 3), ("kv_norm_g", 2), ("lam", 4), ("valid", 64), ("posf", 64)):
    COLS[_name] = (_off, _n)
    _off += _n
NCOL = _off


def pcols(v):
    v = np.asarray(v, np.float32)
    return np.ascontiguousarray(v.reshape(-1, 128).T)


def pack_cols(inp, valid, pos):
    out = np.zeros((128, NCOL), np.float32)

    def put(name, a):
        o, n = COLS[name]
        assert a.shape == (128, n), (name, a.shape)
        out[:, o:o + n] = a
    for l in range(2):
        put("g_mix%d" % l, pcols(inp["norm_mix_g"][l]))
        put("g_cross%d" % l, pcols(inp["norm_cross_g"][l]))
        put("g_mem%d" % l, pcols(inp["norm_mem_g"][l]))
        put("g_mlp%d" % l, pcols(inp["norm_mlp_g"][l]))
    cw = np.asarray(inp["ab_conv_w"], np.float32)[0, :, 0, :]
    put("conv_w", np.ascontiguousarray(cw.T.reshape(4, 128, 31).transpose(1, 0, 2).reshape(128, 124)))
    put("conv_b", pcols(inp["ab_conv_b"][0]))
    put("ln_g", pcols(inp["ab_ln_g"][0]))
    put("ln_b", pcols(inp["ab_ln_b"][0]))
    put("subln_g", pcols(inp["diff_subln_g"][0]))
    put("q_norm_g", pcols(inp["mla_q_norm_g"][0]))
    put("kv_norm_g", pcols(inp["mla_kv_norm_g"][0]))
    lam = np.zeros((128, 4), np.float32)
    for i, nm in enumerate(("diff_lq1", "diff_lk1", "diff_lq2", "diff_lk2")):
        lam[:64, i] = np.asarray(inp[nm], np.float32)[0]
    put("lam", lam)
    put("valid", pcols(valid))
    put("posf", pcols(pos))
    return out


WEIGHTS = (("ab_w_in", [1024, 2560]), ("ab_w_out", [1024, 1024]), ("cd_w_in", [1024, 2208]),
           ("cd_w_out", [1024, 1024]), ("mla_w_uq", [384, 768]), ("mla_w_uk", [256, 512]),
           ("mla_w_uv", [256, 512]), ("cross_wq0", [1024, 512]), ("cross_wq1", [1024, 512]),
           ("cross_wkv0", [1024, 1024]), ("cross_wkv1", [1024, 1024]), ("cross_wo0", [512, 1024]),
           ("cross_wo1", [512, 1024]), ("mlp_w10", [1024, 4096]), ("mlp_w11", [1024, 4096]),
           ("mlp_w20", [4096, 1024]), ("mlp_w21", [4096, 1024]))

SCRATCH = (("XA", [SEQ, D], F32), ("QT0", [512, SEQ], BF16), ("KT0", [512, SEQ], BF16),
           ("V0", [SEQ, 520], BF16), ("GT", [512, SEQ], BF16), ("YBT", [512, SEQ], BF16),
           ("ND", [3, SEQ, 520], F32), ("KCT", [512, SEQ], BF16), ("QCT", [1024, HALF], BF16),
           ("VC", [SEQ, 512], BF16), ("KDT", [768, SEQ], BF16), ("QDT", [768, HALF], BF16),
           ("VD", [SEQ, 512], BF16), ("YT1", [1024, HALF], BF16))


def build_program(phases, dbg=()):
    nc = bass.Bass("TRN2", target_bir_lowering=False)
    k = K(nc)
    C = Ctx()
    C.x_in = dram_in(nc, "x", [SEQ, D])
    C.mem = dram_in(nc, "mem", [256, D])
    C.cols_d = dram_in(nc, "cols", [128, NCOL])
    C.fng = dram_in(nc, "final_norm_g", [1, D])
    C.pos_d = dram_in(nc, "pos", [1, SEQ], I32)
    C.gvec = dram_in(nc, "gvec", [8, D])
    C.yt1_tok0 = HALF
    for name, shape in WEIGHTS:
        setattr(C, name, dram_in(nc, name, shape))
    for name, shape, dt in SCRATCH:
        kind = "ExternalOutput" if name in dbg else "Internal"
        setattr(C, name, nc.dram_tensor(name, list(shape), dt, kind=kind).ap())
    for name, shape in WEIGHTS:
        if name != "ab_w_in":
            setattr(C, "W_" + name, nc.dram_tensor("W_" + name, list(shape), BF16, kind="Internal").ap())
    C.out = nc.dram_tensor("out", [HALF, D], F32, kind="ExternalOutput").ap()
    with k.es, ExitStack() as st:
        build_consts(k, C, st)
        C.cols = k.sb(st, "cols", [128, NCOL], F32)
        k.dma("sp", C.b_const, True, [(C.cols[:, :], C.cols_d[:, :])])
        C.eps_col = k.sb(st, "eps", [128, 1], F32)
        k.op("pool", lambda: nc.gpsimd.memset(C.eps_col[:], EPS), writes=[C.b_const])

        def col(name):
            o, n = COLS[name]
            return C.cols[:, o:o + n]
        C.g_mix = [col("g_mix0"), col("g_mix1")]
        C.g_cross = [col("g_cross0"), col("g_cross1")]
        C.g_mem = [col("g_mem0"), col("g_mem1")]
        C.g_mlp = [col("g_mlp0"), col("g_mlp1")]
        C.valid = col("valid")
        C.col = col
        C.bg = BG(k, C, st)
        bg = C.bg
        def bg_register():
            bg.add("ab_w_out", 8, 1024)
            bg.add("cross_wq0", 8, 512, C.g_cross[0])
            bg.add("cross_wkv0", 8, 1024, C.g_mem[0])
            bg.add("cross_wo0", 4, 1024)
            bg.add("mlp_w10", 8, 4096, C.g_mlp[0])
            bg.add("mlp_w20", 32, 1024)
            bg.add("cd_w_in", 8, 2208, C.g_mix[1])
            bg.add("mla_w_uq", 3, 768, col("q_norm_g"))
            bg.add("mla_w_uk", 2, 512, col("kv_norm_g"))
            bg.add("mla_w_uv", 2, 512, col("kv_norm_g"))
            bg.add("cd_w_out", 8, 1024)
            bg.add("cross_wq1", 8, 512, C.g_cross[1])
            bg.add("cross_wkv1", 8, 1024, C.g_mem[1])
            bg.add("cross_wo1", 4, 1024)
            bg.add("mlp_w11", 8, 4096, C.g_mlp[1])
            bg.add("mlp_w21", 32, 1024)
        C.bg_register = bg_register
        k.barrier()
        with ExitStack() as sh:
            C.diag = k.sb(sh, "diag", [128, 4, 31, 128], BF16)
            C.diagb = k.buf("diag")
            def build_diag():
                o_cw = COLS["conv_w"][0]
                for c in range(4):
                    for tap in range(31):
                        k.op("pool", lambda c=c, tap=tap: nc.gpsimd.tensor_scalar(
                            out=C.diag[:, c, tap, :], in0=C.ident_b[:, :],
                            scalar1=C.cols[:, o_cw + c * 31 + tap:o_cw + c * 31 + tap + 1], scalar2=None, op0=ALU.mult),
                            reads=[C.b_const], writes=[C.diagb])
            C.build_diag = build_diag
            if "a0" in phases:
                sv = C.XA
                C.XA = C.x_in
                phase_a0(k, C, 0, NT)
                C.XA = sv
            if "a0b" in phases:
                phase_a0b(k, C, 0, NT)
            k.release([C.diagb])
        if "a1" in phases:
            phase_a1(k, C, 0, NT)
        if "a2" in phases:
            phase_a2(k, C, 0, 0, NT, C.x_in, "ab")
        if "a3" in phases:
            phase_a3(k, C, 0, 0, NT)
        if "b0" in phases:
            phase_b0(k, C, 0, NT)
        if "b1" in phases:
            phase_b1(k, C)
        if "b2" in phases:
            phase_a2(k, C, 1, NT - HALF // 128, NT, C.XA, "cd")
        if "b3" in phases:
            phase_a3(k, C, 1, NT - HALF // 128, NT, final=True)
        k.barrier()
    return nc


def core_inputs(inp, core):
    b, h = core // 2, core % 2
    x = np.asarray(inp["x"], np.float32)
    pos = np.asarray(inp["positions"])
    xl = np.zeros((SEQ, D), np.float32)
    valid = np.zeros((SEQ,), np.float32)
    pl = np.zeros((SEQ,), np.int32)
    xl[HALF:] = x[b, h * HALF:(h + 1) * HALF]
    valid[HALF:] = 1.0
    pl[HALF:] = pos[b, h * HALF:(h + 1) * HALF]
    if h == 1:
        xl[:HALF] = x[b, :HALF]
        valid[:HALF] = 1.0
        pl[:HALF] = pos[b, :HALF]
    m = {"x": xl, "mem": np.ascontiguousarray(np.asarray(inp["mem"], np.float32)[b]),
         "cols": pack_cols(inp, valid, np.zeros((SEQ,), np.float32)), "pos": pl.reshape(1, SEQ),
         "final_norm_g": np.asarray(inp["final_norm_g"], np.float32).reshape(1, D),
         "gvec": np.ascontiguousarray(np.concatenate([np.asarray(inp[n], np.float32) for n in
                                                       ("norm_mix_g", "norm_cross_g", "norm_mem_g", "norm_mlp_g")], 0))}
    for name, shape in WEIGHTS:
        if name[-1] in "01" and name[:-1] in inp:
            a = np.asarray(inp[name[:-1]], np.float32)[int(name[-1])]
        else:
            a = np.asarray(inp[name], np.float32)[0]
        m[name] = np.ascontiguousarray(a.reshape(shape))
    return m


def phase_a0b(k, C, t_lo, t_hi):
    nc = k.nc
    with ExitStack() as st:
        G = 4
        diag, diagb = C.diag, C.diagb
        conv_b, ln_g, ln_b = C.col("conv_b"), C.col("ln_g"), C.col("ln_b")
        gl = [k.sb(st, "gl", [128, 4, 30 + G * 128], BF16) for _ in range(2)]
        glb = k.bufs_n(2, "gl")
        pc = [k.ps(st, "pc", [128, 512], F32) for _ in range(2)]
        pcb = k.bufs_n(2, "pc")
        pmean = k.ps(st, "pmean", [128, 512], F32)
        pvar = k.ps(st, "pvar", [128, 512], F32)
        pmeanb, pvarb = k.buf("pmean"), k.buf("pvar")
        cvs = [k.sb(st, "cv", [128, 4, 512], F32) for _ in range(2)]
        cvbs = k.bufs_n(2, "cv")
        sq = k.sb(st, "sq", [128, 4, 512], F32)
        sqb = k.buf("sq")
        rs = k.sb(st, "rs", [128, 512], F32)
        rsb = k.buf("rs")
        yb = [k.sb(st, "yb", [128, 4, 512], BF16) for _ in range(2)]
        ybb = k.bufs_n(2, "yb")
        gtv = C.GT.rearrange("(c p) n -> p c n", p=128)
        ybv = C.YBT.rearrange("(c p) n -> p c n", p=128)
        ngrp = (t_hi - t_lo) // G
        def conv_stage(gi):
            cv, cvb = cvs[gi % 2], cvbs[gi % 2]
            tok0 = (t_lo + gi * G) * 128
            g_, gb_ = gl[gi % 2], glb[gi % 2]
            if tok0 == 0:
                k.op("pool", lambda g_=g_: nc.gpsimd.memset(g_[:, :, 0:30], 0.0), writes=[gb_])
                k.dma("sp", gb_, True, [(g_[:, :, 30:30 + 512], gtv[:, :, 0:512])])
            else:
                k.dma("sp", gb_, True, [(g_[:, :, :], gtv[:, :, tok0 - 30:tok0 + 512])])
            for c in range(4):
                p_, pb_ = pc[c % 2], pcb[c % 2]
                for tap in range(31):
                    k.op("pe", lambda p_=p_, c=c, tap=tap, g_=g_: nc.tensor.matmul(
                        p_[:, :], lhsT=diag[:, c, tap, :], rhs=g_[:, c, tap:tap + 512],
                        start=(tap == 0), stop=(tap == 30)), reads=[diagb, gb_], writes=[pb_])
                k.op("act", lambda p_=p_, c=c: nc.scalar.activation(
                    out=cv[:, c, :], in_=p_[:, :], func=AF.Identity, bias=conv_b[:, c:c + 1]),
                    reads=[pb_, C.b_const], writes=[cvb])
                yield

        def ln_stage(gi):
            cv, cvb = cvs[gi % 2], cvbs[gi % 2]
            tok0 = (t_lo + gi * G) * 128
            for c in range(4):
                k.op("pe", lambda c=c: nc.tensor.matmul(pmean[:, :], lhsT=C.ones_f[:, :], rhs=cv[:, c, :],
                                                        start=(c == 0), stop=(c == 3)),
                     reads=[cvb, C.b_const], writes=[pmeanb])
            yield
            for c in range(4):
                k.op("dve", lambda c=c: nc.vector.scalar_tensor_tensor(
                    out=cv[:, c, :], in0=pmean[:, :], scalar=-1.0 / 512, in1=cv[:, c, :],
                    op0=ALU.mult, op1=ALU.add), reads=[pmeanb, cvb], writes=[cvb])
                k.op("act", lambda c=c: nc.scalar.activation(out=sq[:, c, :], in_=cv[:, c, :], func=AF.Square),
                     reads=[cvb], writes=[sqb])
            yield
            for c in range(4):
                k.op("pe", lambda c=c: nc.tensor.matmul(pvar[:, :], lhsT=C.ones_f[:, :], rhs=sq[:, c, :],
                                                        start=(c == 0), stop=(c == 3)),
                     reads=[sqb, C.b_const], writes=[pvarb])
            k.op("act", lambda: nc.scalar.activation(out=rs[:, :], in_=pvar[:, :], func=AF.Sqrt,
                                                     bias=C.eps_col[:, 0:1], scale=1.0 / 512),
                 reads=[pvarb, C.b_const], writes=[rsb])
            k.op("dve", lambda: nc.vector.reciprocal(out=rs[:, :], in_=rs[:, :]), reads=[rsb], writes=[rsb])
            yield
            y_, yb_ = yb[gi % 2], ybb[gi % 2]
            for c in range(4):
                k.op("dve", lambda c=c: nc.vector.tensor_tensor(out=cv[:, c, :], in0=cv[:, c, :], in1=rs[:, :],
                                                                op=ALU.mult), reads=[cvb, rsb], writes=[cvb])
                k.op("dve", lambda c=c: nc.vector.tensor_scalar(
                    out=cv[:, c, :], in0=cv[:, c, :], scalar1=ln_g[:, c:c + 1], scalar2=ln_b[:, c:c + 1],
                    op0=ALU.mult, op1=ALU.add), reads=[cvb, C.b_const], writes=[cvb])
                k.op("act", lambda c=c, y_=y_: nc.scalar.activation(out=y_[:, c, :], in_=cv[:, c, :], func=AF.Silu),
                     reads=[cvb], writes=[yb_])
            k.dma("sp", yb_, False, [(ybv[:, :, tok0:tok0 + 512], y_[:, :, :])])

        def drive(gens):
            gens = [g for g in gens if g is not None]
            while gens:
                for g in list(gens):
                    try:
                        next(g)
                    except StopIteration:
                        gens.remove(g)

        drive([conv_stage(0)])
        for gi in range(ngrp):
            drive([conv_stage(gi + 1) if gi + 1 < ngrp else None, ln_stage(gi)])
        k.barrier()
        k.release(glb + pcb + [pmeanb, pvarb, sqb, rsb] + cvbs + ybb)


A_PATTERNS = (1, 4, 16)


def phase_a1(k, C, q_lo, q_hi):
    nc = k.nc
    with ExitStack() as st:
        KT = k.sb(st, "KT", [128, 4, SEQ], BF16)
        QT = k.sb(st, "QT", [128, 4, SEQ], BF16)
        ktbs, qtbs = k.bufs_n(4, "KT"), k.bufs_n(4, "QT")
        ktv = C.KT0.rearrange("(c p) n -> p c n", p=128)
        qtv = C.QT0.rearrange("(c p) n -> p c n", p=128)
        for c in range(4):
            k.dma("sp", ktbs[c], True, [(KT[:, c, :], ktv[:, c, :])])
            k.dma("sp", qtbs[c], True, [(QT[:, c, :], qtv[:, c, :])])
        vt = [k.sb(st, "vt", [128, 8, 65], BF16) for _ in range(3)]
        vtb = k.bufs_n(3, "vt")
        pss2 = [k.ps(st, "pss", [128, 2, 2, 2, 128], F32) for _ in range(2)]
        pss = [[t[:, 0], t[:, 1]] for t in pss2]
        pssb = k.bufs_n(2, "pss")
        po = [k.ps(st, "po", [128, 512], F32)[:, 0:260].rearrange("p (h d) -> p h d", h=4) for _ in range(2)]
        pob = k.bufs_n(2, "po")
        pt = [k.sb(st, "pt", [128, 2, 2, 2, 128], BF16) for _ in range(2)]
        ptb = k.bufs_n(2, "pt")
        osb = [k.sb(st, "osb", [128, 8, 65], F32) for _ in range(2)]
        osbb = k.bufs_n(2, "osb")
        ui = 0
        bi = 0
        mge = C.mask_ge[:, :].unsqueeze(1).to_broadcast([128, 4, 128])
        mle = C.mask_le[:, :].unsqueeze(1).to_broadcast([128, 4, 128])
        units = []
        for gidx, d in enumerate(A_PATTERNS):
            span = 128 * d
            nlo, nhi = (q_lo * 128) // span, (q_hi * 128) // span
            for r in range(d):
                for n in range(nlo, nhi):
                    for hh in range(2):
                        units.append((gidx, d, r, n, hh, nlo, nhi))

        def vload(d, r, n):
            s0 = n * 128 * d + r
            k.dma("sp", vtb[n % 3], True,
                  [(vt[n % 3][:, :, :].rearrange("p h d -> p (h d)"), C.V0[s0:s0 + 127 * d + 1:d, :])])

        def qk(ui):
            gidx, d, r, n, hh, nlo, nhi = units[ui]
            span = 128 * d
            s0 = n * span + r
            sq_ = slice(s0, s0 + 127 * d + 1, d)
            sp_ = slice(s0 - span, s0 - span + 127 * d + 1, d)
            kbs = (0, 1) if n > 0 else (1,)
            ps_, psb_ = pss[ui % 2], pssb[ui % 2]
            for par in range(2):
                for hpl in range(2):
                    hp = hh * 2 + hpl
                    for kb in kbs:
                        ks = sp_ if kb == 0 else sq_
                        k.op("pe", lambda: nc.tensor.matmul(
                            ps_[par][:, hpl, kb, :], lhsT=KT[par * 64:(par + 1) * 64, hp, ks],
                            rhs=QT[par * 64:(par + 1) * 64, hp, sq_], start=True, stop=True),
                            reads=[ktbs[hp], qtbs[hp]], writes=[psb_])

        def rest(ui):
            gidx, d, r, n, hh, nlo, nhi = units[ui]
            span = 128 * d
            s0 = n * span + r
            sq_ = slice(s0, s0 + 127 * d + 1, d)
            kbs = (0, 1) if n > 0 else (1,)
            if hh == 0:
                if n == nlo:
                    if nlo > 0:
                        vload(d, r, nlo - 1)
                    vload(d, r, nlo)
                if n + 1 < nhi:
                    vload(d, r, n + 1)
            bi = ui // 2
            o_, ob_ = osb[bi % 2], osbb[bi % 2]
            ps_, psb_ = pss[ui % 2], pssb[ui % 2]
            pt_, ptb_ = pt[ui % 2], ptb[ui % 2]
            po_, pob_ = po[ui % 2], pob[ui % 2]
            psf = pss2[ui % 2]
            if n > 0:
                k.op("act", lambda: nc.scalar.activation(
                    out=pt_[:, :, :, :, :], in_=psf[:, :, :, :, :], func=AF.Exp),
                    reads=[psb_], writes=[ptb_])
            else:
                for par in range(2):
                    k.op("act", lambda: nc.scalar.activation(
                        out=pt_[:, par, :, 1, :], in_=ps_[par][:, :, 1, :], func=AF.Exp),
                        reads=[psb_], writes=[ptb_])
            ptv = pt_[:, :, :, :, :].rearrange("p a b c q -> p (a b) c q")
            if n > 0:
                k.op("dve", lambda: nc.vector.tensor_tensor(
                    out=ptv[:, :, 0, :], in0=ptv[:, :, 0, :], in1=mge, op=ALU.mult),
                    reads=[C.b_const], writes=[ptb_])
            k.op("dve", lambda: nc.vector.tensor_tensor(
                out=ptv[:, :, 1, :], in0=ptv[:, :, 1, :], in1=mle, op=ALU.mult),
                reads=[C.b_const], writes=[ptb_])
            for hl in range(4):
                h = hh * 4 + hl
                hp, par = h // 2, h % 2
                hpl = hp - hh * 2
                for i, kb in enumerate(kbs):
                    vb = (n - 1) % 3 if kb == 0 else n % 3
                    k.op("pe", lambda: nc.tensor.matmul(
                        po_[:, hl, :], lhsT=pt_[:, par, hpl, kb, :], rhs=vt[vb][:, h, :],
                        start=(i == 0), stop=(i == len(kbs) - 1)),
                        reads=[ptb_, vtb[vb]], writes=[pob_])

        def evac(ui):
            gidx, d, r, n, hh, nlo, nhi = units[ui]
            s0 = n * 128 * d + r
            sq_ = slice(s0, s0 + 127 * d + 1, d)
            bi = ui // 2
            o_, ob_ = osb[bi % 2], osbb[bi % 2]
            po_, pob_ = po[ui % 2], pob[ui % 2]
            k.op("act", lambda: nc.scalar.copy(out=o_[:, hh * 4:(hh + 1) * 4, :], in_=po_[:, :, :]), reads=[pob_], writes=[ob_])
            if hh == 1:
                k.dma("sp", ob_, False, [(C.ND[gidx, sq_, :], o_[:, :, :].rearrange("p h d -> p (h d)"))])

        qk(0)
        for ui in range(len(units)):
            if ui + 1 < len(units):
                qk(ui + 1)
            rest(ui)
            if ui > 0:
                evac(ui - 1)
        evac(len(units) - 1)
        k.barrier()
        k.release(ktbs + qtbs + vtb + pssb + pob + ptb + osbb)


def cross_kv(k, C, st, layer):
    nc = k.nc
    KmT = k.sb(st, "KmT", [128, 4, 256], BF16)
    Vm = k.sb(st, "Vm", [128, 2, 4, 129], BF16)
    kvb = k.buf("crosskv")
    with ExitStack() as s2:
        wkv = k.sb(s2, "wkv", [128, 8, 1024], BF16)
        wb = load_weight_bf(k, C, wkv[:, :, :], "cross_wkv%d" % layer)
        tmpb = []
        xm = k.sb(s2, "xm", [128, 2, D], F32)
        hb = k.sb(s2, "hbm", [128, 2, D], BF16)
        junk = k.sb(s2, "junkm", [128, D], BF16)
        ssq = k.sb(s2, "ssqm", [128, 8], F32)
        rstd = k.sb(s2, "rstdm", [128, 8], F32)
        mT = k.sb(s2, "mT", [128, 8, 256], BF16)
        tps = [k.ps(s2, "tpm", [128, 8, 128], BF16) for _ in range(2)]
        pm = [k.ps(s2, "pmm", [128, 512], F32) for _ in range(2)]
        xb, hbb, jb, sb_, mTb = k.buf("xm"), k.buf("hbm"), k.buf("jm"), k.buf("ssqm"), k.buf("mT")
        tpsb, pmb = k.bufs_n(2, "tpm"), k.bufs_n(2, "pmm")
        k.dma("sp", xb, True, [(xm[:, :, :], C.mem.rearrange("(t p) d -> p t d", p=128))])
        gain = load_gain(k, C, s2, 4 + layer)
        norm_rows(k, C, xm, xb, 2, ssq, sb_, rstd, junk, jb, hb, hbb, gain=gain)
        transpose_to(k, C, hb, hbb, 2, 8, tps, tpsb, mT, mTb)
        for hd in range(4):
            p_, pb_ = pm[hd % 2], pmb[hd % 2]
            for kk in range(8):
                k.op("pe", lambda p_=p_, kk=kk, hd=hd: nc.tensor.matmul(
                    p_[:, 0:256], lhsT=wkv[:, kk, hd * 128:(hd + 1) * 128], rhs=mT[:, kk, :],
                    start=(kk == 0), stop=(kk == 7)), reads=[wb, mTb], writes=[pb_])
            k.op("act", lambda p_=p_, hd=hd: nc.scalar.copy(out=KmT[:, hd, :], in_=p_[:, 0:256]),
                 reads=[pb_], writes=[kvb])
        for mt in range(2):
            p_, pb_ = pm[mt % 2], pmb[mt % 2]
            for kk in range(8):
                k.op("pe", lambda p_=p_, kk=kk, mt=mt: nc.tensor.matmul(
                    p_[:, :], lhsT=mT[:, kk, mt * 128:(mt + 1) * 128], rhs=wkv[:, kk, 512:1024],
                    start=(kk == 0), stop=(kk == 7)), reads=[wb, mTb], writes=[pb_])
            k.op("act", lambda p_=p_, mt=mt: nc.scalar.copy(
                out=Vm[:, mt, :, 0:128], in_=p_[:, :].rearrange("p (h d) -> p h d", h=4)), reads=[pb_], writes=[kvb])
            k.op("dve", lambda mt=mt: nc.vector.memset(Vm[:, mt, :, 128:129], 1.0), writes=[kvb])
        k.barrier()
        k.release(tmpb + [wb, xb, hbb, jb, sb_, mTb, gain[1]] + tpsb + pmb)
    return KmT, Vm, kvb


def phase_a2(k, C, layer, t_lo, t_hi, x_src, mix_kind):
    nc = k.nc
    with ExitStack() as st:
        KmT, Vm, kvb = cross_kv(k, C, st, layer)
        w_out = k.sb(st, "w_out", [128, 8, 1024], BF16)
        wq = k.sb(st, "wq", [128, 8, 512], BF16)
        wo = k.sb(st, "wo", [128, 4, 1024], BF16)
        wob = load_weight_bf(k, C, w_out[:, :, :], "ab_w_out" if mix_kind == "ab" else "cd_w_out")
        wqb = load_weight_bf(k, C, wq[:, :, :], "cross_wq%d" % layer)
        wcb = load_weight_bf(k, C, wo[:, :, :], "cross_wo%d" % layer)
        G = 4
        gain = load_gain(k, C, st, 2 + layer)
        xg = [k.sb(st, "xg", [128, G, D], F32) for _ in range(2)]
        xgb = k.bufs_n(2, "xg")
        nd = [k.sb(st, "nd", [128, 3, 520], F32) for _ in range(8 if mix_kind == "ab" else 1)]
        ndb = k.bufs_n(len(nd), "nd")
        rden = k.sb(st, "rden", [128, 8], F32)
        rdenb = k.buf("rden")
        rden2 = k.sb(st, "rden2", [128, 8], F32)
        rden2b = k.buf("rden2")
        ya = k.sb(st, "ya", [128, G, 512], BF16)
        yab = k.buf("ya")
        yT = [k.sb(st, "yT", [128, 8, G * 128], BF16) for _ in range(2)]
        yTb = k.bufs_n(2, "yT")
        hb = k.sb(st, "hb", [128, G, D], BF16)
        hbb = k.buf("hb")
        junk = k.sb(st, "junk", [128, D], BF16)
        junkb = k.buf("junk")
        ssq = k.sb(st, "ssq", [128, 8], F32)
        rstd = k.sb(st, "rstd", [128, 8], F32)
        ssqb = k.buf("ssq")
        hTs = [k.sb(st, "hT", [128, 8, G * 128], BF16) for _ in range(2)]
        hTbs = k.bufs_n(2, "hT")
        qT = k.sb(st, "qT", [128, 4, G * 128], BF16)
        qTb = k.buf("qT")
        pT = k.sb(st, "pT", [128, 4, 2, G * 128], BF16)
        pTb = k.buf("pT")
        oc = k.sb(st, "oc", [128, G, 512], BF16)
        ocb = k.buf("oc")
        oT = k.sb(st, "oT", [128, 4, G * 128], BF16)
        oTb = k.buf("oT")
        tps = [k.ps(st, "tp", [128, 8, 128], BF16) for _ in range(2)]
        tpsb = k.bufs_n(2, "tp")
        pm = [k.ps(st, "pm", [128, 512], F32) for _ in range(4)]
        pmb = k.bufs_n(4, "pm")
        pox = [k.ps(st, "pox", [128, 512], F32) for _ in range(2)]
        poxb = k.bufs_n(2, "pox")
        xsv = x_src.rearrange("(t p) d -> p t d", p=128)
        xdv = C.XA.rearrange("(t p) d -> p t d", p=128)
        cnt = {'pmi': 0, 'ndi': 0}
        ngrp = (t_hi - t_lo) // G

        def S1(gi):
            t0 = t_lo + gi * G
            tok0 = t0 * 128
            x_, xb_ = xg[gi % 2], xgb[gi % 2]
            hT, hTb = hTs[gi % 2], hTbs[gi % 2]
            yT_, yTb_ = yT[gi % 2], yTb[gi % 2]
            k.dma("sp", xb_, True, [(x_[:, :, :], xsv[:, t0:t0 + G, :])])
            if mix_kind == "ab":
                k.dma("sp", yTb_, True, [(yT_[:, 4:8, :], C.YBT.rearrange("(c p) n -> p c n", p=128)[:, :, tok0:tok0 + G * 128])])
                def nd_load(g2):
                    if g2 >= ngrp:
                        return
                    for j in range(G):
                        t = t_lo + g2 * G + j
                        i = (g2 % 2) * G + j
                        k.dma("sp", ndb[i], True, [(nd[i][:, :, :], C.ND[:, t * 128:(t + 1) * 128, :].rearrange("g p c -> p g c"))])
                if gi == 0:
                    nd_load(0)
                nd_load(gi + 1)
                yield
                for j in range(G):
                    n_, nb_ = nd[(gi % 2) * G + j], ndb[(gi % 2) * G + j]
                    t = t0 + j
                    k.op("dve", lambda n_=n_: nc.vector.tensor_tensor(out=n_[:, 0, :], in0=n_[:, 0, :], in1=n_[:, 1, :], op=ALU.add),
                         reads=[nb_], writes=[nb_])
                    k.op("dve", lambda n_=n_: nc.vector.tensor_tensor(out=n_[:, 0, :], in0=n_[:, 0, :], in1=n_[:, 2, :], op=ALU.add),
                         reads=[nb_], writes=[nb_])
                    nv = n_[:, 0, :].rearrange("p (h d) -> p h d", h=8)
                    k.op("dve", lambda nv=nv: nc.vector.tensor_scalar(out=rden[:, :].unsqueeze(2), in0=nv[:, :, 64:65], scalar1=1e-30, scalar2=None, op0=ALU.max),
                         reads=[nb_], writes=[rdenb])
                    k.op("dve", lambda: nc.vector.reciprocal(out=rden[:, :], in_=rden[:, :]), reads=[rdenb], writes=[rdenb])
                    k.op("dve", lambda nv=nv, j=j: nc.vector.tensor_tensor(
                        out=ya[:, j, :].rearrange("p (h d) -> p h d", h=8), in0=nv[:, :, 0:64],
                        in1=rden[:, :].unsqueeze(2).to_broadcast([128, 8, 64]), op=ALU.mult),
                        reads=[nb_, rdenb], writes=[yab])
                    yield
                transpose_to(k, C, ya, yab, G, 4, tps, tpsb, yT_, yTb_)
                yield
            else:
                o0 = tok0 - C.yt1_tok0
                k.dma("sp", yTb_, True, [(yT_[:, :, :], C.YT1.rearrange("(c p) n -> p c n", p=128)[:, :, o0:o0 + G * 128])])
            for j in range(G):
                for half in range(2):
                    p_, pb_ = pm[cnt['pmi'] % 4], pmb[cnt['pmi'] % 4]
                    cnt['pmi'] += 1
                    for kk in range(8):
                        k.op("pe", lambda p_=p_, kk=kk, j=j, half=half: nc.tensor.matmul(
                            p_[:, :], lhsT=yT_[:, kk, j * 128:(j + 1) * 128], rhs=w_out[:, kk, half * 512:(half + 1) * 512],
                            start=(kk == 0), stop=(kk == 7)), reads=[wob, yTb_], writes=[pb_])
                    k.op("dve", lambda p_=p_, j=j, half=half: nc.vector.tensor_tensor(
                        out=x_[:, j, half * 512:(half + 1) * 512], in0=p_[:, :], in1=x_[:, j, half * 512:(half + 1) * 512],
                        op=ALU.add), reads=[pb_, xb_], writes=[xb_])
                yield
            yield
            norm_rows(k, C, x_, xb_, G, ssq, ssqb, rstd, junk, junkb, hb, hbb, gain=gain)
            yield "hold"
            transpose_to(k, C, hb, hbb, G, 8, tps, tpsb, hT, hTb)

        def S2(gi):
            t0 = t_lo + gi * G
            tok0 = t0 * 128
            x_, xb_ = xg[gi % 2], xgb[gi % 2]
            hT, hTb = hTs[gi % 2], hTbs[gi % 2]
            for hd in range(4):
                p_, pb_ = pm[cnt['pmi'] % 4], pmb[cnt['pmi'] % 4]
                cnt['pmi'] += 1
                for kk in range(8):
                    k.op("pe", lambda p_=p_, kk=kk, hd=hd: nc.tensor.matmul(
                        p_[:, :], lhsT=wq[:, kk, hd * 128:(hd + 1) * 128], rhs=hT[:, kk, :],
                        start=(kk == 0), stop=(kk == 7)), reads=[wqb, hTb], writes=[pb_])
                k.op("act", lambda p_=p_, hd=hd: nc.scalar.copy(out=qT[:, hd, :], in_=p_[:, :]), reads=[pb_], writes=[qTb])
                yield
            for hd in range(4):
                for mt in range(2):
                    p_, pb_ = pm[cnt['pmi'] % 4], pmb[cnt['pmi'] % 4]
                    cnt['pmi'] += 1
                    k.op("pe", lambda p_=p_, hd=hd, mt=mt: nc.tensor.matmul(
                        p_[:, :], lhsT=KmT[:, hd, mt * 128:(mt + 1) * 128], rhs=qT[:, hd, :], start=True, stop=True),
                        reads=[kvb, qTb], writes=[pb_])
                    k.op("act", lambda p_=p_, hd=hd, mt=mt: nc.scalar.activation(
                        out=pT[:, hd, mt, :], in_=p_[:, :], func=AF.Exp, scale=128.0 ** -0.5), reads=[pb_], writes=[pTb])
                yield
            for j in range(G):
                for hh in range(2):
                    po_, pob_ = pox[hh], poxb[hh]
                    pov = po_[:, 0:258].rearrange("p (h d) -> p h d", h=2)
                    for hl in range(2):
                        hd = hh * 2 + hl
                        for mt in range(2):
                            k.op("pe", lambda pov=pov, hl=hl, hd=hd, mt=mt, j=j: nc.tensor.matmul(
                                pov[:, hl, :], lhsT=pT[:, hd, mt, j * 128:(j + 1) * 128], rhs=Vm[:, mt, hd, :],
                                start=(mt == 0), stop=(mt == 1)), reads=[pTb, kvb], writes=[pob_])
                    k.op("dve", lambda pov=pov, hh=hh: nc.vector.reciprocal(
                        out=rden2[:, hh * 2:hh * 2 + 2].unsqueeze(2), in_=pov[:, :, 128:129]), reads=[pob_], writes=[rden2b])
                    k.op("dve", lambda pov=pov, hh=hh, j=j: nc.vector.tensor_tensor(
                        out=oc[:, j, hh * 256:(hh + 1) * 256].rearrange("p (h d) -> p h d", h=2), in0=pov[:, :, 0:128],
                        in1=rden2[:, hh * 2:hh * 2 + 2].unsqueeze(2).to_broadcast([128, 2, 128]), op=ALU.mult),
                        reads=[pob_, rden2b], writes=[ocb])
                yield
            transpose_to(k, C, oc, ocb, G, 4, tps, tpsb, oT, oTb)
            yield
            for j in range(G):
                for half in range(2):
                    p_, pb_ = pm[cnt['pmi'] % 4], pmb[cnt['pmi'] % 4]
                    cnt['pmi'] += 1
                    for kk in range(4):
                        k.op("pe", lambda p_=p_, kk=kk, j=j, half=half: nc.tensor.matmul(
                            p_[:, :], lhsT=oT[:, kk, j * 128:(j + 1) * 128], rhs=wo[:, kk, half * 512:(half + 1) * 512],
                            start=(kk == 0), stop=(kk == 3)), reads=[wcb, oTb], writes=[pb_])
                    k.op("dve", lambda p_=p_, j=j, half=half: nc.vector.tensor_tensor(
                        out=x_[:, j, half * 512:(half + 1) * 512], in0=p_[:, :], in1=x_[:, j, half * 512:(half + 1) * 512],
                        op=ALU.add), reads=[pb_, xb_], writes=[xb_])
                yield
            k.dma("sp", xb_, False, [(xdv[:, t0:t0 + G, :], x_[:, :, :])])

        drive([S1(0)])
        for gi in range(ngrp):
            drive([S2(gi), S1(gi + 1) if gi + 1 < ngrp else None])
        k.barrier()
        k.release([kvb, wob, wqb, wcb, gain[1]] + xgb + ndb + [rdenb, rden2b, yab] + yTb + [hbb, junkb, ssqb, qTb, pTb, ocb, oTb] + hTbs
                  + tpsb + pmb + poxb)


def phase_a3(k, C, layer, t_lo, t_hi, final=False):
    nc = k.nc
    with ExitStack() as st:
        w1 = k.sb(st, "w1", [128, 8, 4096], BF16)
        w2 = k.sb(st, "w2", [128, 32, 1024], BF16)
        n1, n2 = "mlp_w1%d" % layer, "mlp_w2%d" % layer
        w1b, w2b = k.bufs_n(4, "w1c"), k.bufs_n(4, "w2c")
        s1 = getattr(C, "W_" + n1).rearrange("(c p) n -> p c n", p=128)
        s2 = getattr(C, "W_" + n2).rearrange("(c p) n -> p c n", p=128)
        for c in range(4):
            k.dma("sp", w1b[c], True, [(w1[:, :, c * 1024:(c + 1) * 1024], s1[:, :, c * 1024:(c + 1) * 1024])],
                  extra_reads=[C.bg.wbuf[n1]])
        for c in range(4):
            k.dma("sp", w2b[c], True, [(w2[:, c * 8:(c + 1) * 8, :], s2[:, c * 8:(c + 1) * 8, :])], extra_reads=[C.bg.wbuf[n2]])
        G = 2
        N = G * 128
        gain = load_gain(k, C, st, 6 + layer)
        hid = k.sb(st, "hid", [128, 32, N], BF16)
        hidb = k.bufs_n(32, "hid")
        rl = [k.sb(st, "rl", [128, N], BF16) for _ in range(2)]
        rlb = k.bufs_n(2, "rl")
        tps = [k.ps(st, "tp", [128, 8, 128], BF16) for _ in range(2)]
        tpsb = k.bufs_n(2, "tp")
        pm = [k.ps(st, "pm", [128, 512], F32) for _ in range(6)]
        pmb = k.bufs_n(6, "pm")
        if final:
            gfin = k.sb(st, "gfin", [128, D], F32)
            gfb = k.buf("gfin")
            k.dma("sp", gfb, True, [(gfin[:, :], C.fng.to_broadcast([128, D]))])
        xv = C.XA.rearrange("(t p) d -> p t d", p=128)
        pmi = 0
        ngrp = (t_hi - t_lo) // G
        pre = PreStage(k, C, st, G, xv, gain, tps, tpsb, t_lo, ngrp)
        junk, junkb, ssq, ssqb, rstd = pre.junk, pre.junkb, pre.ssq[0], pre.ssqb[0], pre.rstd[0]
        pre.start()
        for gi in range(ngrp):
            t0 = t_lo + gi * G
            pre.front(gi + 1)
            x_, xb_, hT, hTb = pre.get(gi)
            for f in range(32):
                p_, pb_ = pm[pmi % 6], pmb[pmi % 6]
                pmi += 1
                for kk in range(8):
                    k.op("pe", lambda p_=p_, kk=kk, f=f: nc.tensor.matmul(
                        p_[:, 0:N], lhsT=w1[:, kk, f * 128:(f + 1) * 128], rhs=hT[:, kk, :],
                        start=(kk == 0), stop=(kk == 7)), reads=[w1b[f // 8], hTb], writes=[pb_])
                r_, rb_ = rl[f % 2], rlb[f % 2]
                k.op("act", lambda p_=p_, r_=r_: nc.scalar.activation(out=r_[:, :], in_=p_[:, 0:N], func=AF.Relu),
                     reads=[pb_], writes=[rb_])
                k.op("dve", lambda r_=r_, f=f: nc.vector.tensor_tensor(out=hid[:, f, :], in0=r_[:, :], in1=r_[:, :], op=ALU.mult),
                     reads=[rb_], writes=[hidb[f]])
            pre.back(gi + 1)
            for j in range(G):
                for half in range(2):
                    p_, pb_ = pm[pmi % 6], pmb[pmi % 6]
                    pmi += 1
                    for f in range(32):
                        k.op("pe", lambda p_=p_, f=f, j=j, half=half: nc.tensor.matmul(
                            p_[:, :], lhsT=hid[:, f, j * 128:(j + 1) * 128], rhs=w2[:, f, half * 512:(half + 1) * 512],
                            start=(f == 0), stop=(f == 31)), reads=[w2b[f // 8], hidb[f]], writes=[pb_])
                    k.op("dve", lambda p_=p_, j=j, half=half: nc.vector.tensor_tensor(
                        out=x_[:, j, half * 512:(half + 1) * 512], in0=p_[:, :], in1=x_[:, j, half * 512:(half + 1) * 512],
                        op=ALU.add), reads=[pb_, xb_], writes=[xb_])
            if not final:
                k.dma("sp", xb_, False, [(xv[:, t0:t0 + G, :], x_[:, :, :])])
            else:
                for j in range(G):
                    k.op("act", lambda j=j: nc.scalar.activation(out=junk[:, :], in_=x_[:, j, :], func=AF.Square,
                                                                 accum_out=ssq[:, j:j + 1]), reads=[xb_], writes=[junkb, ssqb])
                k.op("act", lambda: nc.scalar.activation(out=rstd[:, 0:G], in_=ssq[:, 0:G], func=AF.Sqrt,
                                                         bias=C.eps_col[:, 0:1], scale=1.0 / D), reads=[ssqb, C.b_const], writes=[ssqb])
                k.op("dve", lambda: nc.vector.reciprocal(out=rstd[:, 0:G], in_=rstd[:, 0:G]), reads=[ssqb], writes=[ssqb])
                for j in range(G):
                    k.op("dve", lambda j=j: nc.vector.scalar_tensor_tensor(
                        out=x_[:, j, :], in0=x_[:, j, :], scalar=rstd[:, j:j + 1], in1=gfin[:, :],
                        op0=ALU.mult, op1=ALU.mult), reads=[xb_, ssqb, gfb], writes=[xb_])
                ot0 = t0 - (NT - HALF // 128)
                k.dma("sp", xb_, False, [(C.out.rearrange("(t p) d -> p t d", p=128)[:, ot0:ot0 + G, :], x_[:, :, :])])
        k.barrier()
        k.release(w1b + w2b + [gain[1]] + pre.bufs() + hidb + rlb + tpsb + pmb + ([gfb] if final else []))


def drive(gens):
    gens = [g for g in gens if g is not None]
    held = []
    while gens or held:
        if not gens:
            gens, held = held, []
        for g in list(gens):
            try:
                if next(g) == "hold" and len(gens) > 1:
                    gens.remove(g)
                    held.append(g)
            except StopIteration:
                gens.remove(g)


TWO_PI = 2.0 * np.pi
CW1 = 6.28125
CW2 = TWO_PI - 6.28125


def phase_b0(k, C, t_lo, t_hi):
    nc = k.nc
    QS = NT - HALF // 128
    with ExitStack() as st:
        w = k.sb(st, "w_in1", [128, 8, 2240], BF16)
        wqn = k.sb(st, "wqn", [128, 3, 512], BF16)
        wqr = k.sb(st, "wqr", [128, 3, 256], BF16)
        wqt = k.sb(st, "wqt", [128, 3, 256], BF16)
        wuk = k.sb(st, "wuk", [128, 2, 512], BF16)
        wuv = k.sb(st, "wuv", [128, 2, 512], BF16)
        invf = k.sb(st, "invf", [128, 1], F32)
        wb = k.buf("wb1")
        wb0 = load_weight_bf(k, C, w[:, :, 0:2208], "cd_w_in")
        t0_ = []
        for kk in range(8):
            k.op("pool", lambda kk=kk: nc.gpsimd.tensor_scalar(out=w[:, kk, 2208:2224], in0=w[:, kk, 2192:2208], scalar1=-1.0,
                                                               scalar2=None, op0=ALU.mult), reads=[wb0], writes=[wb])
            k.op("pool", lambda kk=kk: nc.gpsimd.tensor_copy(out=w[:, kk, 2224:2240], in_=w[:, kk, 2176:2192]),
                 reads=[wb0], writes=[wb])
        wbk = load_weight_bf(k, C, wuk[:, :, :], "mla_w_uk")
        wbv = load_weight_bf(k, C, wuv[:, :, :], "mla_w_uv")
        t1_, t2_ = [], []
        stg = k.sb(st, "uqs", [128, 3, 768], BF16)
        sgb = load_weight_bf(k, C, stg[:, :, :], "mla_w_uq")
        qg = k.sb(st, "one3", [128, 3], F32)
        k.op("pool", lambda: nc.gpsimd.memset(qg[:, :], 1.0), writes=[sgb])
        for c in range(3):
            sv = stg[:, c, :].rearrange("p (h e) -> p h e", h=8)
            k.op("pool", lambda c=c, sv=sv: nc.gpsimd.tensor_scalar(
                out=wqn[:, c, :].rearrange("p (h e) -> p h e", h=8), in0=sv[:, :, 0:64], scalar1=qg[:, c:c + 1],
                scalar2=None, op0=ALU.mult), reads=[sgb, C.b_const], writes=[wb])
            k.op("pool", lambda c=c, sv=sv: nc.gpsimd.tensor_scalar(
                out=wqr[:, c, :].rearrange("p (h e) -> p h e", h=8), in0=sv[:, :, 64:96], scalar1=qg[:, c:c + 1],
                scalar2=None, op0=ALU.mult), reads=[sgb, C.b_const], writes=[wb])
            k.op("pool", lambda c=c, sv=sv: nc.gpsimd.tensor_scalar(
                out=wqt[:, c, :].rearrange("p (h e) -> p h e", h=8)[:, :, 0:16], in0=sv[:, :, 80:96], scalar1=qg[:, c:c + 1],
                scalar2=-1.0, op0=ALU.mult, op1=ALU.mult), reads=[sgb, C.b_const], writes=[wb])
            k.op("pool", lambda c=c, sv=sv: nc.gpsimd.tensor_scalar(
                out=wqt[:, c, :].rearrange("p (h e) -> p h e", h=8)[:, :, 16:32], in0=sv[:, :, 64:80], scalar1=qg[:, c:c + 1],
                scalar2=None, op0=ALU.mult), reads=[sgb, C.b_const], writes=[wb])
        pid = k.sb(st, "pid", [128, 1], I32)
        pidf = k.sb(st, "pidf", [128, 1], F32)
        pb_ = k.buf("pid")
        k.op("pool", lambda: nc.gpsimd.iota(pid[:, :], pattern=[[0, 1]], base=0, channel_multiplier=1), writes=[pb_])
        k.op("dve", lambda: nc.vector.tensor_single_scalar(out=pid[:, :], in_=pid[:, :], scalar=15, op=ALU.bitwise_and),
             reads=[pb_], writes=[pb_])
        k.op("dve", lambda: nc.vector.tensor_copy(out=pidf[:, :], in_=pid[:, :]), reads=[pb_], writes=[pb_])
        k.op("act", lambda: nc.scalar.activation(out=invf[:, :], in_=pidf[:, :], func=AF.Exp, scale=-np.log(10000.0) / 16.0),
             reads=[pb_], writes=[wb])
        k.op("pool", lambda: nc.gpsimd.memset(qg[:, :], 1.0), reads=[wb0, wbk, wbv, sgb], writes=[wb])
        b0tmp = [wb0, wbk, wbv, sgb, pb_]
        G = 4
        N = G * 128
        gain = load_gain(k, C, st, 1)
        tps = [k.ps(st, "tp", [128, 8, 128], BF16) for _ in range(2)]
        tpsb = k.bufs_n(2, "tp")
        pm = [k.ps(st, "pm", [128, 512], F32) for _ in range(6)]
        pmb = k.bufs_n(6, "pm")
        stq = [k.sb(st, "stq", [128, 4, N], BF16) for _ in range(2)]
        stqb = k.bufs_n(2, "stq")
        vst = [k.sb(st, "vst", [128, 512], BF16) for _ in range(2)]
        vstb = k.bufs_n(2, "vst")
        lat = k.sb(st, "lat", [128, 3, N], F32)
        latb = k.buf("lat")
        lsq = k.sb(st, "lsq", [128, 3, N], BF16)
        lsqb = k.buf("lsq")
        rs = k.sb(st, "rs", [128, N], F32)
        rsb = k.buf("rs")
        ckvn = k.sb(st, "ckvn", [128, 2, N], BF16)
        ckvnb = k.buf("ckvn")
        cqn = k.sb(st, "cqn", [128, 3, N], BF16)
        cqnb = k.buf("cqn")
        posi = k.sb(st, "posi", [128, N], I32)
        posb = k.buf("posi")
        ang = k.sb(st, "ang", [128, N], F32)
        kq = k.sb(st, "kq", [128, N], F32)
        kqi = k.sb(st, "kqi", [128, N], I32)
        tmpm = k.sb(st, "tmpm", [128, N], F32)
        angb = k.buf("ang")
        css = [k.sb(st, "cs", [128, 2, N], F32) for _ in range(2)]
        csbs = k.bufs_n(2, "cs")
        rt = [k.sb(st, "rt", [128, N], F32) for _ in range(2)]
        rtb = k.bufs_n(2, "rt")
        ro = [k.sb(st, "ro", [128, N], BF16) for _ in range(2)]
        rob = k.bufs_n(2, "ro")
        xv = C.XA.rearrange("(t p) d -> p t d", p=128)
        kctv = C.KCT.rearrange("(c p) n -> p c n", p=128)
        qctv = C.QCT.rearrange("(c m p) n -> p c m n", m=2, p=128)
        zq = [k.sb(st, "zq", [128, 4, 2, N], BF16) for _ in range(2)]
        zqb = k.bufs_n(2, "zq")
        for z_, zb_ in zip(zq, zqb):
            k.op("pool", lambda z_=z_: nc.gpsimd.memset(z_[:, :, :, :], 0.0), writes=[zb_])
        kdtv = C.KDT.rearrange("(h e) n -> h e n", e=96)
        qdtv = C.QDT.rearrange("(h e) n -> h e n", e=96)
        vcv = C.VC.rearrange("(t p) c -> p t c", p=128)
        vdv = C.VD.rearrange("(t p) c -> p t c", p=128)
        cnt = {"pm": 0, "stq": 0, "vst": 0, "rt": 0, "ro": 0}

        def nxt(name, arr, arrb):
            i = cnt[name]
            cnt[name] += 1
            return arr[i % len(arr)], arrb[i % len(arr)]

        def fm_proj(wt, wtb, kc, col0, M, rhsT, rhsb):
            p_, pb_ = nxt("pm", pm, pmb)
            for kk in range(kc):
                k.op("pe", lambda p_=p_, kk=kk: nc.tensor.matmul(
                    p_[0:M, 0:N], lhsT=wt[:, kk, col0:col0 + M], rhs=rhsT[:, kk, :], start=(kk == 0), stop=(kk == kc - 1)),
                    reads=[wtb, rhsb], writes=[pb_])
            return p_, pb_

        def latent_norm(p_list, nchunk, dim, dst, dstb, gcol):
            for c, (p_, pb_) in enumerate(p_list):
                k.op("act", lambda p_=p_, c=c: nc.scalar.copy(out=lat[:, c, :], in_=p_[:, 0:N]), reads=[pb_], writes=[latb])
                k.op("act", lambda p_=p_, c=c: nc.scalar.activation(out=lsq[:, c, :], in_=p_[:, 0:N], func=AF.Square),
                     reads=[pb_], writes=[lsqb])
            ps_, psb_ = nxt("pm", pm, pmb)
            for c in range(nchunk):
                k.op("pe", lambda ps_=ps_, c=c: nc.tensor.matmul(ps_[:, 0:N], lhsT=C.ones_b[:, :], rhs=lsq[:, c, :],
                                                                 start=(c == 0), stop=(c == nchunk - 1)),
                     reads=[lsqb, C.b_const], writes=[psb_])
            k.op("act", lambda ps_=ps_: nc.scalar.activation(out=rs[:, :], in_=ps_[:, 0:N], func=AF.Sqrt, bias=C.eps_col[:, 0:1],
                                                             scale=1.0 / dim), reads=[psb_, C.b_const], writes=[rsb])
            k.op("dve", lambda: nc.vector.reciprocal(out=rs[:, :], in_=rs[:, :]), reads=[rsb], writes=[rsb])
            for c in range(nchunk):
                k.op("dve", lambda c=c: nc.vector.scalar_tensor_tensor(out=dst[:, c, :], in0=lat[:, c, :], scalar=gcol[:, c:c + 1],
                                                                       in1=rs[:, :], op0=ALU.mult, op1=ALU.mult),
                     reads=[latb, rsb, C.b_const], writes=[dstb])

        def sincos(tok0, idx):
            cs, csb = css[idx], csbs[idx]
            k.dma("sp", posb, True, [(posi[:, :], C.pos_d[:, tok0:tok0 + N].to_broadcast([128, N]))])
            for which in range(2):
                yield
                k.op("dve", lambda: nc.vector.tensor_copy(out=ang[:, :], in_=posi[:, :]), reads=[posb], writes=[angb])
                k.op("dve", lambda which=which: nc.vector.tensor_scalar(
                    out=ang[:, :], in0=ang[:, :], scalar1=invf[:, 0:1], scalar2=(0.5 * np.pi if which == 1 else 0.0),
                    op0=ALU.mult, op1=ALU.add), reads=[angb, wb], writes=[angb])
                k.op("dve", lambda: nc.vector.tensor_scalar(out=kq[:, :], in0=ang[:, :], scalar1=1.0 / TWO_PI, scalar2=None,
                                                            op0=ALU.mult), reads=[angb], writes=[angb])
                k.op("dve", lambda: nc.vector.tensor_copy(out=kqi[:, :], in_=kq[:, :]), reads=[angb], writes=[angb])
                k.op("dve", lambda: nc.vector.tensor_copy(out=kq[:, :], in_=kqi[:, :]), reads=[angb], writes=[angb])
                yield
                k.op("dve", lambda: nc.vector.scalar_tensor_tensor(out=ang[:, :], in0=kq[:, :], scalar=-CW1, in1=ang[:, :],
                                                                   op0=ALU.mult, op1=ALU.add), reads=[angb], writes=[angb])
                k.op("dve", lambda: nc.vector.scalar_tensor_tensor(out=ang[:, :], in0=kq[:, :], scalar=-CW2, in1=ang[:, :],
                                                                   op0=ALU.mult, op1=ALU.add), reads=[angb], writes=[angb])
                k.op("dve", lambda: nc.vector.tensor_scalar(out=tmpm[:, :], in0=ang[:, :], scalar1=float(np.pi), scalar2=-TWO_PI,
                                                            op0=ALU.is_gt, op1=ALU.mult), reads=[angb], writes=[angb])
                k.op("dve", lambda: nc.vector.tensor_tensor(out=ang[:, :], in0=ang[:, :], in1=tmpm[:, :], op=ALU.add),
                     reads=[angb], writes=[angb])
                yield
                k.op("dve", lambda: nc.vector.tensor_scalar(out=tmpm[:, :], in0=ang[:, :], scalar1=-float(np.pi), scalar2=TWO_PI,
                                                            op0=ALU.is_lt, op1=ALU.mult), reads=[angb], writes=[angb])
                k.op("dve", lambda: nc.vector.tensor_tensor(out=ang[:, :], in0=ang[:, :], in1=tmpm[:, :], op=ALU.add),
                     reads=[angb], writes=[angb])
                k.op("dve", lambda: nc.vector.tensor_scalar(out=ang[:, :], in0=ang[:, :], scalar1=3.141592, scalar2=-3.141592,
                                                            op0=ALU.min, op1=ALU.max), reads=[angb], writes=[angb])
                k.op("act", lambda which=which: nc.scalar.activation(out=cs[:, which, :], in_=ang[:, :], func=AF.Sin),
                     reads=[angb], writes=[csb])

        def rope_combine(pr, prb, pt_, ptb_, M, scale):
            cs, csb = css[cur["i"]], csbs[cur["i"]]
            r1, r1b = nxt("rt", rt, rtb)
            r2, r2b = nxt("rt", rt, rtb)
            o_, ob_ = nxt("ro", ro, rob)
            k.op("dve", lambda: nc.vector.tensor_tensor(out=r1[0:M, :], in0=pr[0:M, 0:N], in1=cs[0:M, 1, :], op=ALU.mult),
                 reads=[prb, csb], writes=[r1b])
            k.op("dve", lambda: nc.vector.tensor_tensor(out=r2[0:M, :], in0=pt_[0:M, 0:N], in1=cs[0:M, 0, :], op=ALU.mult),
                 reads=[ptb_, csb], writes=[r2b])
            if scale == 1.0:
                k.op("dve", lambda: nc.vector.tensor_tensor(out=o_[0:M, :], in0=r1[0:M, :], in1=r2[0:M, :], op=ALU.add),
                     reads=[r1b, r2b], writes=[ob_])
            else:
                k.op("dve", lambda: nc.vector.tensor_tensor(out=r1[0:M, :], in0=r1[0:M, :], in1=r2[0:M, :], op=ALU.add),
                     reads=[r1b, r2b], writes=[r1b])
                k.op("dve", lambda: nc.vector.tensor_scalar(out=o_[0:M, :], in0=r1[0:M, :], scalar1=scale, scalar2=None,
                                                            op0=ALU.mult), reads=[r1b], writes=[ob_])
            return o_, ob_

        ngrp = (t_hi - t_lo) // G
        pre = PreStage(k, C, st, G, xv, gain, tps, tpsb, t_lo, ngrp)
        pre.start()
        cur = {"i": 0}
        drive([sincos(t_lo * 128, 0)])
        for gi in range(ngrp):
            t0 = t_lo + gi * G
            tok0 = t0 * 128
            own = t0 >= QS
            qtok0 = tok0 - QS * 128
            pre.front(gi + 1)
            x_, xb_, hT_, hTb_ = pre.get(gi)
            cur["i"] = gi % 2

            def chainC():
                s_, sb_ = nxt("stq", stq, stqb)
                for c in range(4):
                    p_, pb_ = fm_proj(w, wb, 8, 512 + c * 128, 128, hT_, hTb_)
                    k.op("act", lambda p_=p_, s_=s_, c=c: nc.scalar.copy(out=s_[:, c, :], in_=p_[:, 0:N]), reads=[pb_], writes=[sb_])
                k.dma("sp", sb_, False, [(kctv[:, :, tok0:tok0 + N], s_[:, :, :])])
                yield
                if own:
                    z_, zb_ = zq[gi % 2], zqb[gi % 2]
                    for c in range(4):
                        p_, pb_ = fm_proj(w, wb, 8, c * 128, 128, hT_, hTb_)
                        k.op("act", lambda p_=p_, z_=z_, c=c: nc.scalar.activation(
                            out=z_[0:64, c, 0, :], in_=p_[0:64, 0:N], func=AF.Copy, scale=0.125), reads=[pb_], writes=[zb_])
                        k.op("act", lambda p_=p_, z_=z_, c=c: nc.scalar.activation(
                            out=z_[64:128, c, 1, :], in_=p_[64:128, 0:N], func=AF.Copy, scale=0.125), reads=[pb_], writes=[zb_])
                    k.dma("sp", zb_, False, [(qctv[:, :, :, qtok0:qtok0 + N], z_[:, :, :, :])])
                    yield
                for j in range(G):
                    p_, pb_ = nxt("pm", pm, pmb)
                    for kk in range(8):
                        k.op("pe", lambda p_=p_, kk=kk, j=j: nc.tensor.matmul(
                            p_[:, :], lhsT=hT_[:, kk, j * 128:(j + 1) * 128], rhs=w[:, kk, 1024:1536],
                            start=(kk == 0), stop=(kk == 7)), reads=[wb, hTb_], writes=[pb_])
                    v_, vb_ = nxt("vst", vst, vstb)
                    t = t0 + j
                    k.op("act", lambda p_=p_, v_=v_, t=t: nc.scalar.activation(out=v_[:, :], in_=p_[:, :], func=AF.Copy,
                                                                               scale=C.valid[:, t:t + 1]),
                         reads=[pb_, C.b_const], writes=[vb_])
                    k.dma("sp", vb_, False, [(vcv[:, t, :], v_[:, :])])
                    yield
                pre.back(gi + 1)
                yield

            def chainD():
                pl = [fm_proj(w, wb, 8, 1920 + c * 128, 128, hT_, hTb_) for c in range(2)]
                latent_norm(pl, 2, 256, ckvn, ckvnb, C.col("kv_norm_g"))
                for hp in range(4):
                    p_, pb_ = fm_proj(wuk, wb, 2, hp * 128, 128, ckvn, ckvnb)
                    s_, sb_ = nxt("vst", vst, vstb)
                    k.op("act", lambda p_=p_, s_=s_: nc.scalar.copy(out=s_[:, 0:N], in_=p_[:, 0:N]), reads=[pb_], writes=[sb_])
                    k.dma("sp", sb_, False, [(kdtv[2 * hp, 0:64, tok0:tok0 + N], s_[0:64, 0:N]),
                                             (kdtv[2 * hp + 1, 0:64, tok0:tok0 + N], s_[64:128, 0:N])])
                    yield
                for j in range(G):
                    p_, pb_ = nxt("pm", pm, pmb)
                    for c in range(2):
                        k.op("pe", lambda p_=p_, c=c, j=j: nc.tensor.matmul(
                            p_[:, :], lhsT=ckvn[:, c, j * 128:(j + 1) * 128], rhs=wuv[:, c, :], start=(c == 0), stop=(c == 1)),
                            reads=[wb, ckvnb], writes=[pb_])
                    v_, vb_ = nxt("vst", vst, vstb)
                    t = t0 + j
                    k.op("act", lambda p_=p_, v_=v_, t=t: nc.scalar.activation(out=v_[:, :], in_=p_[:, :], func=AF.Copy,
                                                                               scale=C.valid[:, t:t + 1]),
                         reads=[pb_, C.b_const], writes=[vb_])
                    k.dma("sp", vb_, False, [(vdv[:, t, :], v_[:, :])])
                    yield
                pr, prb = fm_proj(w, wb, 8, 2176, 32, hT_, hTb_)
                pt_, ptb_ = fm_proj(w, wb, 8, 2208, 32, hT_, hTb_)
                o_, ob_ = rope_combine(pr, prb, pt_, ptb_, 32, 1.0)
                k.dma("sp", ob_, False, [(kdtv[h, 64:96, tok0:tok0 + N], o_[0:32, :]) for h in range(8)])
                yield
                if own:
                    sc = 96.0 ** -0.5
                    pl = [fm_proj(w, wb, 8, 1536 + c * 128, 128, hT_, hTb_) for c in range(3)]
                    latent_norm(pl, 3, 384, cqn, cqnb, C.col("q_norm_g"))
                    for hp in range(4):
                        p_, pb_ = fm_proj(wqn, wb, 3, hp * 128, 128, cqn, cqnb)
                        s_, sb_ = nxt("vst", vst, vstb)
                        k.op("act", lambda p_=p_, s_=s_: nc.scalar.activation(out=s_[:, 0:N], in_=p_[:, 0:N], func=AF.Copy, scale=sc),
                             reads=[pb_], writes=[sb_])
                        k.dma("sp", sb_, False, [(qdtv[2 * hp, 0:64, qtok0:qtok0 + N], s_[0:64, 0:N]),
                                                 (qdtv[2 * hp + 1, 0:64, qtok0:qtok0 + N], s_[64:128, 0:N])])
                        yield
                    for rc in range(2):
                        pr, prb = fm_proj(wqr, wb, 3, rc * 128, 128, cqn, cqnb)
                        pt_, ptb_ = fm_proj(wqt, wb, 3, rc * 128, 128, cqn, cqnb)
                        o_, ob_ = rope_combine(pr, prb, pt_, ptb_, 128, sc)
                        k.dma("sp", ob_, False, [(qdtv[rc * 4 + hl, 64:96, qtok0:qtok0 + N], o_[hl * 32:(hl + 1) * 32, :])
                                                 for hl in range(4)])
                        yield
                yield

            drive([chainC(), chainD(), sincos(tok0 + N, (gi + 1) % 2) if gi + 1 < ngrp else None])
        k.barrier()
        k.release(b0tmp + [wb, gain[1]] + pre.bufs() + tpsb + pmb + stqb + vstb + [latb, lsqb, rsb, ckvnb, cqnb, posb, angb] + csbs
                  + rtb + rob + zqb)


LAM_INIT = 0.8 - 0.6 * float(np.exp(-0.3 * 1))


def phase_b1(k, C):
    nc = k.nc
    QS = NT - HALF // 128
    with ExitStack() as st:
        ovalid = k.sb(st, "ovalid", [128, NT, 128], BF16)
        lamc = k.sb(st, "lamc", [128, 4], F32)
        cb = k.buf("b1const")
        k.op("dve", lambda: nc.vector.tensor_copy(out=ovalid[:, :, :], in_=C.valid.unsqueeze(2).to_broadcast([128, NT, 128])),
             reads=[C.b_const], writes=[cb])
        lm = C.col("lam")
        pq = k.ps(st, "pq", [128, 512], F32)
        pqb = k.buf("pq")
        k.op("dve", lambda: nc.vector.tensor_tensor(out=lamc[:, 0:1], in0=lm[:, 0:1], in1=lm[:, 1:2], op=ALU.mult),
             reads=[C.b_const], writes=[cb])
        k.op("dve", lambda: nc.vector.tensor_tensor(out=lamc[:, 1:2], in0=lm[:, 2:3], in1=lm[:, 3:4], op=ALU.mult),
             reads=[C.b_const], writes=[cb])
        k.op("pe", lambda: nc.tensor.matmul(pq[:, 0:2], lhsT=C.ones_f[:, :], rhs=lamc[:, 0:2], start=True, stop=True),
             reads=[cb, C.b_const], writes=[pqb])
        k.op("act", lambda: nc.scalar.activation(out=lamc[:, 2:4], in_=pq[:, 0:2], func=AF.Exp), reads=[pqb], writes=[cb])
        k.op("dve", lambda: nc.vector.tensor_tensor(out=lamc[:, 0:1], in0=lamc[:, 3:4], in1=lamc[:, 2:3], op=ALU.subtract),
             reads=[cb], writes=[cb])
        k.op("dve", lambda: nc.vector.tensor_scalar(out=lamc[:, 0:1], in0=lamc[:, 0:1], scalar1=-LAM_INIT, scalar2=None, op0=ALU.add),
             reads=[cb], writes=[cb])
        neglam = lamc[:, 0:1]
        sgc = C.col("subln_g")
        KTu = [k.sb(st, "KTu", [128, SEQ], BF16) for _ in range(2)]
        QTu = [k.sb(st, "QTu", [128, 2, HALF], BF16) for _ in range(2)]
        Vu = [k.sb(st, "Vu", [128, NT, 128], BF16) for _ in range(2)]
        ub = k.bufs_n(2, "unit")
        pss = [k.ps(st, "pss", [128, 512], F32) for _ in range(3)]
        pssb = k.bufs_n(3, "pss")
        acc = [k.ps(st, "acc", [128, 512], F32) for _ in range(2)]
        accb = k.bufs_n(2, "acc")
        den = [k.ps(st, "den", [128, 512], F32) for _ in range(2)]
        denb = k.bufs_n(2, "den")
        pt = [k.sb(st, "pt", [128, 512], BF16) for _ in range(4)]
        ptb = k.bufs_n(4, "pt")
        rd = [k.sb(st, "rd", [128, 512], F32) for _ in range(2)]
        rdb = k.bufs_n(2, "rd")
        an = [k.sb(st, "an", [128, 512], F32) for _ in range(2)]
        anb = k.bufs_n(2, "an")
        sq = k.sb(st, "sqd", [128, 512], F32)
        sqb = k.buf("sqd")
        yo = [k.sb(st, "yo", [128, 512], BF16) for _ in range(2)]
        yob = k.bufs_n(2, "yo")
        vcv = C.VC.rearrange("(t p) c -> p t c", p=128)
        vdv = C.VD.rearrange("(t p) c -> p t c", p=128)
        units = [("c", h) for h in range(4)] + [("d", h) for h in range(8)]

        def load_unit(ui):
            kind, h = units[ui]
            kt_, qt_, v_, b_ = KTu[ui % 2], QTu[ui % 2], Vu[ui % 2], ub[ui % 2]
            pairs = []
            if kind == "c":
                pairs.append((kt_[:, :], C.KCT[h * 128:(h + 1) * 128, :]))
                pairs.append((qt_[:, :, :], C.QCT.rearrange("(c m p) n -> p c m n", m=2, p=128)[:, h, :, :]))
                for t8 in range(0, NT, 8):
                    pairs.append((v_[:, t8:t8 + 8, :], vcv[:, t8:t8 + 8, h * 128:(h + 1) * 128]))
            else:
                pairs.append((kt_[0:96, :], C.KDT[h * 96:(h + 1) * 96, :]))
                pairs.append((qt_[0:96, 0, :], C.QDT[h * 96:(h + 1) * 96, :]))
                for t8 in range(0, NT, 8):
                    pairs.append((v_[:, t8:t8 + 8, 0:64], vdv[:, t8:t8 + 8, h * 64:(h + 1) * 64]))
            k.dma("sp", b_, True, pairs)
            if kind == "d":
                k.op("pool", lambda: nc.gpsimd.tensor_copy(out=v_[:, :, 64:128], in_=ovalid[:, :, 0:64]), reads=[cb], writes=[b_])

        cnt = {"s": 0, "p": 0, "y": 0}
        load_unit(0)
        for ui, (kind, h) in enumerate(units):
            if ui + 1 < len(units):
                load_unit(ui + 1)
            kt_, qt_, v_, b_ = KTu[ui % 2], QTu[ui % 2], Vu[ui % 2], ub[ui % 2]
            nmap = 2 if kind == "c" else 1
            kd = 128 if kind == "c" else 96
            dv = 128
            for G in range(HALF // 512):
                C.bg.step(1)
                q0 = G * 512
                kdiag = QS + 4 * G
                kmax = kdiag + 3
                steps = [(kt, m) for kt in range(kmax + 1) for m in range(nmap)]

                def qk(step):
                    kt, m = step
                    j0 = max(0, kt - kdiag)
                    ncol = (4 - j0) * 128
                    s_, sb_ = pss[cnt["s"] % 3], pssb[cnt["s"] % 3]
                    cnt["s"] += 1
                    k.op("pe", lambda: nc.tensor.matmul(
                        s_[:, 0:ncol], lhsT=kt_[0:kd, kt * 128:(kt + 1) * 128],
                        rhs=qt_[0:kd, m, q0 + j0 * 128:q0 + 512], start=True, stop=True), reads=[b_], writes=[sb_])
                    return s_, sb_, j0, ncol

                def rest(step, qkres):
                    kt, m = step
                    s_, sb_, j0, ncol = qkres
                    p_, pb_ = pt[cnt["p"] % 4], ptb[cnt["p"] % 4]
                    cnt["p"] += 1
                    k.op("act", lambda: nc.scalar.activation(out=p_[:, 0:ncol], in_=s_[:, 0:ncol], func=AF.Exp),
                         reads=[sb_], writes=[pb_])
                    if kt >= kdiag:
                        k.op("dve", lambda: nc.vector.tensor_tensor(out=p_[:, 0:128], in0=p_[:, 0:128], in1=C.mask_le[:, :],
                                                                    op=ALU.mult), reads=[C.b_const], writes=[pb_])
                    k.op("pe", lambda: nc.tensor.matmul(acc[m][0:dv, j0 * 128:512], lhsT=v_[:, kt, 0:dv], rhs=p_[:, 0:ncol],
                                                        start=(kt == 0), stop=(kt == kmax)), reads=[b_, pb_], writes=[accb[m]],
                         late=pb_)
                    if kind == "c":
                        k.op("pe", lambda: nc.tensor.matmul(den[m][0:dv, j0 * 128:512], lhsT=ovalid[:, kt, 0:dv], rhs=p_[:, 0:ncol],
                                                            start=(kt == 0), stop=(kt == kmax)), reads=[cb, pb_], writes=[denb[m]])

                LA = 2
                pend = [qk(steps[i]) for i in range(min(LA, len(steps)))]
                for si, step in enumerate(steps):
                    if si + LA < len(steps):
                        pend.append(qk(steps[si + LA]))
                    rest(step, pend.pop(0))
                y_, yb_ = yo[cnt["y"] % 2], yob[cnt["y"] % 2]
                cnt["y"] += 1
                if kind == "d":
                    k.op("dve", lambda: nc.vector.tensor_copy(out=an[0][:, :], in_=acc[0][:, :]), reads=[accb[0]], writes=[anb[0]])
                    k.op("pe", lambda: nc.tensor.matmul(den[0][0:64, :], lhsT=C.ident_f[:, 64:128], rhs=an[0][:, :], start=True, stop=True),
                         reads=[anb[0], C.b_const], writes=[denb[0]])
                    k.op("dve", lambda: nc.vector.tensor_scalar(out=rd[0][0:64, :], in0=den[0][0:64, :], scalar1=1e-30, scalar2=None,
                                                                op0=ALU.max), reads=[denb[0]], writes=[rdb[0]])
                    k.op("dve", lambda: nc.vector.reciprocal(out=rd[0][0:64, :], in_=rd[0][0:64, :]), reads=[rdb[0]], writes=[rdb[0]])
                    k.op("dve", lambda: nc.vector.tensor_tensor(out=y_[0:64, :], in0=an[0][0:64, :], in1=rd[0][0:64, :], op=ALU.mult),
                         reads=[anb[0], rdb[0]], writes=[yb_])
                    k.dma("sp", yb_, False, [(C.YT1[512 + h * 64:512 + (h + 1) * 64, q0:q0 + 512], y_[0:64, :])])
                else:
                    for m in range(nmap):
                        k.op("dve", lambda m=m: nc.vector.tensor_scalar(out=rd[m][0:dv, :], in0=den[m][0:dv, :], scalar1=1e-30, scalar2=None,
                                                                        op0=ALU.max), reads=[denb[m]], writes=[rdb[m]])
                        k.op("dve", lambda m=m: nc.vector.reciprocal(out=rd[m][0:dv, :], in_=rd[m][0:dv, :]), reads=[rdb[m]], writes=[rdb[m]])
                    for m in range(2):
                        k.op("dve", lambda m=m: nc.vector.tensor_tensor(out=an[m][:, :], in0=acc[m][:, :], in1=rd[m][:, :], op=ALU.mult),
                             reads=[accb[m], rdb[m]], writes=[anb[m]])
                    k.op("dve", lambda: nc.vector.scalar_tensor_tensor(out=an[0][:, :], in0=an[1][:, :], scalar=neglam, in1=an[0][:, :],
                                                                       op0=ALU.mult, op1=ALU.add), reads=[anb[0], anb[1], cb], writes=[anb[0]])
                    k.op("dve", lambda: nc.vector.tensor_tensor(out=sq[:, :], in0=an[0][:, :], in1=an[0][:, :], op=ALU.mult),
                         reads=[anb[0]], writes=[sqb])
                    k.op("pe", lambda: nc.tensor.matmul(pq[:, :], lhsT=C.ones_f[:, :], rhs=sq[:, :], start=True, stop=True),
                         reads=[sqb, C.b_const], writes=[pqb])
                    k.op("act", lambda: nc.scalar.activation(out=sq[:, :], in_=pq[:, :], func=AF.Sqrt, bias=C.eps_col[:, 0:1],
                                                             scale=1.0 / 128), reads=[pqb, C.b_const], writes=[sqb])
                    k.op("dve", lambda: nc.vector.reciprocal(out=sq[:, :], in_=sq[:, :]), reads=[sqb], writes=[sqb])
                    k.op("dve", lambda: nc.vector.tensor_tensor(out=an[0][:, :], in0=an[0][:, :], in1=sq[:, :], op=ALU.mult),
                         reads=[anb[0], sqb], writes=[anb[0]])
                    k.op("dve", lambda: nc.vector.tensor_scalar(out=y_[:, :], in0=an[0][:, :], scalar1=sgc[:, 0:1], scalar2=1.0 - LAM_INIT,
                                                                op0=ALU.mult, op1=ALU.mult), reads=[anb[0], C.b_const], writes=[yb_])
                    k.dma("sp", yb_, False, [(C.YT1[h * 128:(h + 1) * 128, q0:q0 + 512], y_[:, :])])
        k.barrier()
        k.release([cb, pqb] + ub + pssb + accb + denb + ptb + rdb + anb + [sqb] + yob)


ALL_PHASES = ("a0", "a0b", "a1", "a2", "a3", "b0", "b1", "b2", "b3")
_NC_CACHE = {}


def kernel(**inputs):
    if "nc" not in _NC_CACHE:
        _NC_CACHE["nc"] = build_program(ALL_PHASES)
    nc = _NC_CACHE["nc"]
    in_maps = [core_inputs(inputs, c) for c in range(8)]
    res = run_bass_kernel_spmd(nc, in_maps, core_ids=list(range(8)))
    out = np.empty((4, SEQ, D), np.float32)
    for c in range(8):
        b, h = c // 2, c % 2
        out[b, h * HALF:(h + 1) * HALF] = np.asarray(res.results[c]["out"], np.float32)
    return out
```
